# Optimizing a Trainium2 kernel written in Bass

```python
import math
import jax, jax.numpy as jnp
from jax import lax
import numpy as np

D_MODEL = 1024
BATCH = 2
SEQ = 8192
DEPTH = 4
DEC_BATCH = 128
DEC_SEQ = 1
PAST_LEN = 8192
PAGE_SIZE = 128

D_MIX = D_MODEL
CONV_DIM = D_MIX // 4
POOL_DIM = D_MIX // 4
ATTN_DIM = D_MIX - CONV_DIM - POOL_DIM
HEAD_DIM = 64
CONV_WIDTH = 3
POOL_WINDOWS = (2, 4, 8, 16)
N_POOL_GROUPS = 4
POOL_GROUP_DIM = POOL_DIM // N_POOL_GROUPS
MAX_POOL = 16
N_HEADS = ATTN_DIM // HEAD_DIM
N_KV_HEADS = 2
GQA_GROUP = N_HEADS // N_KV_HEADS
WINDOW = 128
D_FF = 2816
KV_DIM = N_KV_HEADS * HEAD_DIM
IN_SPLITS = (CONV_DIM, CONV_DIM, CONV_DIM, POOL_DIM, ATTN_DIM, KV_DIM, KV_DIM)
D_IN = CONV_DIM * 3 + POOL_DIM + ATTN_DIM + 2 * KV_DIM
N_MOD = 9
ALPHA = (2.0 * DEPTH) ** 0.25
BETA = (8.0 * DEPTH) ** -0.25
LN_EPS = 1e-5
RMS_EPS = 1e-6
NEG_INF = -1e30

kernel_name = "hymba_conv_pool_swa_macaron_decode"


def layer_norm(x, g, b):
    x32 = x.astype(jnp.float32)
    mu = jnp.mean(x32, axis=-1, keepdims=True)
    var = jnp.mean(jnp.square(x32 - mu), axis=-1, keepdims=True)
    return ((x32 - mu) * lax.rsqrt(var + LN_EPS) * g.astype(jnp.float32) + b.astype(jnp.float32)).astype(x.dtype)


def rms_norm(x, g):
    x32 = x.astype(jnp.float32)
    y = x32 * lax.rsqrt(jnp.mean(x32 * x32, axis=-1, keepdims=True) + RMS_EPS)
    return (y * g.astype(jnp.float32)).astype(x.dtype)


def swiglu(h, w_gate, w_up, w_down):
    return (jax.nn.silu(h @ w_gate) * (h @ w_up)) @ w_down


def short_conv(ext, w, t):
    y = w[0] * ext[:, :t]
    for k in range(1, CONV_WIDTH):
        y = y + w[k] * ext[:, k:k + t]
    return y


def multiscale_pool(ext, start_pos, t, w_pool, pool_scale):
    p = MAX_POOL - 1
    x32 = ext.astype(jnp.float32)
    cs = jnp.concatenate([jnp.zeros_like(x32[:, :1]), jnp.cumsum(x32, axis=1)], axis=1)
    pos = start_pos + jnp.arange(t)
    outs = []
    for g, w in enumerate(POOL_WINDOWS):
        lo, hi = g * POOL_GROUP_DIM, (g + 1) * POOL_GROUP_DIM
        win_sum = cs[:, p + 1:p + 1 + t, lo:hi] - cs[:, p + 1 - w:p + 1 - w + t, lo:hi]
        count = jnp.minimum(pos + 1, w).astype(jnp.float32)[None, :, None]
        pooled = (win_sum / count - x32[:, p:, lo:hi]).astype(ext.dtype)
        outs.append(jnp.einsum('btc,cd->btd', pooled, w_pool[g]))
    return jnp.concatenate(outs, axis=-1) * pool_scale


def sink_softmax(s, mask, sinks):
    sink = sinks.astype(jnp.float32).reshape(N_KV_HEADS, GQA_GROUP, 1, 1)
    s = jnp.where(mask, s, NEG_INF)
    m = jnp.maximum(jnp.max(s, axis=-1, keepdims=True), sink)
    p = jnp.exp(s - m)
    return p / (jnp.sum(p, axis=-1, keepdims=True) + jnp.exp(sink - m))


def swa_prompt(q, k, v, sinks):
    b, s = q.shape[0], q.shape[1]
    nb = s // WINDOW
    qb = q.reshape(b, nb, WINDOW, N_KV_HEADS, GQA_GROUP, HEAD_DIM)
    kb = k.reshape(b, nb, WINDOW, N_KV_HEADS, HEAD_DIM)
    vb = v.reshape(b, nb, WINDOW, N_KV_HEADS, HEAD_DIM)
    kk = jnp.concatenate([jnp.concatenate([jnp.zeros_like(kb[:, :1]), kb[:, :-1]], axis=1), kb], axis=2)
    vv = jnp.concatenate([jnp.concatenate([jnp.zeros_like(vb[:, :1]), vb[:, :-1]], axis=1), vb], axis=2)
    scores = jnp.einsum('bnqhgd,bnkhd->bnhgqk', qb, kk).astype(jnp.float32) * (HEAD_DIM ** -0.5)
    blk = jnp.arange(nb)[:, None, None]
    qpos = blk * WINDOW + jnp.arange(WINDOW)[None, :, None]
    kpos = (blk - 1) * WINDOW + jnp.arange(2 * WINDOW)[None, None, :]
    mask = (kpos <= qpos) & (qpos - kpos < WINDOW) & (kpos >= 0)
    probs = sink_softmax(scores, mask[None, :, None, None], sinks)
    out = jnp.einsum('bnhgqk,bnkhd->bnqhgd', probs.astype(vv.dtype), vv)
    return out.reshape(b, s, ATTN_DIM)


def swa_sample(q, k_ext, v_ext, sinks):
    b, t = q.shape[0], q.shape[1]
    L = k_ext.shape[1] - t
    scores = jnp.einsum('bqhgd,bkhd->bhgqk', q, k_ext).astype(jnp.float32) * (HEAD_DIM ** -0.5)
    qpos = jnp.arange(t)[:, None]
    kpos = jnp.arange(L + t)[None, :] - L
    mask = (kpos <= qpos) & (qpos - kpos < WINDOW)
    probs = sink_softmax(scores, mask, sinks)
    out = jnp.einsum('bhgqk,bkhd->bqhgd', probs.astype(v_ext.dtype), v_ext)
    return out.reshape(b, t, ATTN_DIM)


def token_mixer(h, w_in, conv_w, pool_w, pool_scale, sinks, mix_g, w_out,
                conv_prefix, pool_prefix, k_prefix, v_prefix, start_pos):
    b, t, _ = h.shape
    proj = h @ w_in
    split_idx = np.cumsum(IN_SPLITS)[:-1].tolist()
    gb, gc, xin, u, q, k, v = jnp.split(proj, split_idx, axis=-1)
    cv = gc * xin
    conv_ext = jnp.concatenate([conv_prefix.astype(cv.dtype), cv], axis=1)
    y_conv = gb * short_conv(conv_ext, conv_w, t)
    new_conv = conv_ext[:, -(CONV_WIDTH - 1):]
    pool_ext = jnp.concatenate([pool_prefix.astype(u.dtype), u], axis=1)
    y_pool = multiscale_pool(pool_ext, start_pos, t, pool_w, pool_scale)
    new_pool = pool_ext[:, -(MAX_POOL - 1):]
    q = q.reshape(b, t, N_KV_HEADS, GQA_GROUP, HEAD_DIM)
    k = k.reshape(b, t, N_KV_HEADS, HEAD_DIM)
    v = v.reshape(b, t, N_KV_HEADS, HEAD_DIM)
    if k_prefix is None:
        y_attn = swa_prompt(q, k, v, sinks)
        new_k, new_v = k[:, -WINDOW:], v[:, -WINDOW:]
    else:
        buf = k_prefix.shape[1]
        k_ext = jnp.concatenate([k_prefix.astype(k.dtype), k], axis=1)
        v_ext = jnp.concatenate([v_prefix.astype(v.dtype), v], axis=1)
        y_attn = swa_sample(q, k_ext, v_ext, sinks)
        new_k, new_v = k_ext[:, -buf:], v_ext[:, -buf:]
    y = jnp.concatenate([
        rms_norm(y_conv, mix_g[:CONV_DIM]),
        rms_norm(y_pool, mix_g[CONV_DIM:CONV_DIM + POOL_DIM]),
        rms_norm(y_attn, mix_g[CONV_DIM + POOL_DIM:]),
    ], axis=-1)
    return y @ w_out, new_conv, new_pool, new_k, new_v


def run_trunk(x, c, conv_bufs, pool_bufs, k_bufs, v_bufs, start_pos,
              ln_g, ln_b, w_ada, b_ada, ffn1_gate, ffn1_up, ffn1_down,
              w_in, conv_w, pool_w, pool_scale, attn_sinks, mix_norm_g, w_out,
              ffn2_gate, ffn2_up, ffn2_down):
    b = x.shape[0]
    convs, pools, ks, vs = [], [], [], []
    for l in range(DEPTH):
        m = (jax.nn.silu(c) @ w_ada[l] + b_ada[l]).reshape(b, 1, N_MOD, D_MODEL)
        h = x * (1 + m[:, :, 1]) + m[:, :, 0]
        sub = 0.5 * swiglu(h, ffn1_gate[l], ffn1_up[l], ffn1_down[l])
        x = layer_norm(ALPHA * x + m[:, :, 2] * sub, ln_g[l, 0], ln_b[l, 0])
        h = x * (1 + m[:, :, 4]) + m[:, :, 3]
        if conv_bufs is None:
            conv_pre = jnp.zeros((b, CONV_WIDTH - 1, CONV_DIM), x.dtype)
            pool_pre = jnp.zeros((b, MAX_POOL - 1, POOL_DIM), x.dtype)
            k_pre, v_pre = None, None
        else:
            conv_pre, pool_pre, k_pre, v_pre = conv_bufs[l], pool_bufs[l], k_bufs[l], v_bufs[l]
        sub, nc, npool, nk, nv = token_mixer(h, w_in[l], conv_w[l], pool_w[l], pool_scale[l],
                                             attn_sinks[l], mix_norm_g[l], w_out[l],
                                             conv_pre, pool_pre, k_pre, v_pre, start_pos)
        x = layer_norm(ALPHA * x + m[:, :, 5] * sub, ln_g[l, 1], ln_b[l, 1])
        h = x * (1 + m[:, :, 7]) + m[:, :, 6]
        sub = 0.5 * swiglu(h, ffn2_gate[l], ffn2_up[l], ffn2_down[l])
        x = layer_norm(ALPHA * x + m[:, :, 8] * sub, ln_g[l, 2], ln_b[l, 2])
        convs.append(nc)
        pools.append(npool)
        ks.append(nk)
        vs.append(nv)
    return x, jnp.stack(convs), jnp.stack(pools), jnp.stack(ks), jnp.stack(vs)


def setup_inputs(seed: int = 0) -> dict:
    key = jax.random.key(seed)
    ks = jax.random.split(key, 32)
    f32 = jnp.float32
    win_buf = min(WINDOW, PAST_LEN)

    def nrm(k, shape, scale):
        return jax.random.normal(k, shape, f32) * scale

    return {
        'x_prompt': nrm(ks[0], (BATCH, SEQ, D_MODEL), 1.0),
        'x_sample': nrm(ks[1], (DEC_BATCH, DEC_SEQ, D_MODEL), 1.0),
        'state_conv': nrm(ks[2], (DEPTH, DEC_BATCH, CONV_WIDTH - 1, CONV_DIM), 1.0),
        'state_pool': nrm(ks[3], (DEPTH, DEC_BATCH, MAX_POOL - 1, POOL_DIM), 1.0),
        'cache_k_win': nrm(ks[4], (DEPTH, DEC_BATCH, win_buf, N_KV_HEADS, HEAD_DIM), 1.0),
        'cache_v_win': nrm(ks[5], (DEPTH, DEC_BATCH, win_buf, N_KV_HEADS, HEAD_DIM), 1.0),
        'c_prompt': nrm(ks[6], (BATCH, D_MODEL), 1.0),
        'c_sample': nrm(ks[7], (DEC_BATCH, D_MODEL), 1.0),
        'ln_g': 1.0 + nrm(ks[8], (DEPTH, 3, D_MODEL), 0.02),
        'ln_b': nrm(ks[9], (DEPTH, 3, D_MODEL), 0.02),
        'w_ada': nrm(ks[10], (DEPTH, D_MODEL, N_MOD * D_MODEL), 0.5 * D_MODEL ** -0.5),
        'b_ada': nrm(ks[11], (DEPTH, N_MOD * D_MODEL), 0.02),
        'ffn1_gate': nrm(ks[12], (DEPTH, D_MODEL, D_FF), D_MODEL ** -0.5),
        'ffn1_up': nrm(ks[13], (DEPTH, D_MODEL, D_FF), D_MODEL ** -0.5),
        'ffn1_down': nrm(ks[14], (DEPTH, D_FF, D_MODEL), BETA * D_FF ** -0.5),
        'w_in': nrm(ks[15], (DEPTH, D_MODEL, D_IN), D_MODEL ** -0.5),
        'conv_w': nrm(ks[16], (DEPTH, CONV_WIDTH, CONV_DIM), CONV_WIDTH ** -0.5),
        'pool_w': nrm(ks[17], (DEPTH, N_POOL_GROUPS, POOL_GROUP_DIM, POOL_GROUP_DIM), POOL_GROUP_DIM ** -0.5),
        'pool_scale': 1.0 + nrm(ks[18], (DEPTH, POOL_DIM), 0.1),
        'attn_sinks': nrm(ks[19], (DEPTH, N_HEADS), 1.0),
        'mix_norm_g': 1.0 + nrm(ks[20], (DEPTH, D_MIX), 0.1),
        'w_out': nrm(ks[21], (DEPTH, D_MIX, D_MODEL), BETA * D_MIX ** -0.5),
        'ffn2_gate': nrm(ks[22], (DEPTH, D_MODEL, D_FF), D_MODEL ** -0.5),
        'ffn2_up': nrm(ks[23], (DEPTH, D_MODEL, D_FF), D_MODEL ** -0.5),
        'ffn2_down': nrm(ks[24], (DEPTH, D_FF, D_MODEL), BETA * D_FF ** -0.5),
    }


def reference(x_prompt, x_sample, state_conv, state_pool, cache_k_win, cache_v_win,
              c_prompt, c_sample, ln_g, ln_b, w_ada, b_ada, ffn1_gate, ffn1_up, ffn1_down,
              w_in, conv_w, pool_w, pool_scale, attn_sinks, mix_norm_g, w_out,
              ffn2_gate, ffn2_up, ffn2_down):
    y_prompt, p_conv, p_pool, p_k, p_v = run_trunk(
        x_prompt, c_prompt, None, None, None, None, 0,
        ln_g, ln_b, w_ada, b_ada, ffn1_gate, ffn1_up, ffn1_down,
        w_in, conv_w, pool_w, pool_scale, attn_sinks, mix_norm_g, w_out,
        ffn2_gate, ffn2_up, ffn2_down)
    y_sample, s_conv, s_pool, s_k, s_v = run_trunk(
        x_sample, c_sample, state_conv, state_pool, cache_k_win, cache_v_win, PAST_LEN,
        ln_g, ln_b, w_ada, b_ada, ffn1_gate, ffn1_up, ffn1_down,
        w_in, conv_w, pool_w, pool_scale, attn_sinks, mix_norm_g, w_out,
        ffn2_gate, ffn2_up, ffn2_down)
    return (y_prompt, y_sample, p_conv, p_pool, p_k, p_v, s_conv, s_pool, s_k, s_v)
```

```python
import contextlib
import numpy as np
import concourse.bass as bass
import concourse.mybir as mybir
from concourse.bass_utils import run_bass_kernel_spmd

F32 = mybir.dt.float32
BF16 = mybir.dt.bfloat16
AF = mybir.ActivationFunctionType
ALU = mybir.AluOpType

QUEUES = ("pe", "act", "dve", "pool", "sp")


class Buf:
    __slots__ = ("name", "last_w", "reads")

    def __init__(self, name):
        self.name = name
        self.last_w = None
        self.reads = []


class Op:
    __slots__ = ("fn", "waits", "event", "inc")

    def __init__(self, fn, waits, event, inc):
        self.fn, self.waits, self.event, self.inc = fn, waits, event, inc


class Prog:
    def __init__(self, nc):
        self.nc = nc
        self.ops = {q: [] for q in QUEUES}
        self.count = {}
        self.known = {q: {} for q in QUEUES}
        self.dma_sems = []

    def _deps(self, queue, reads, writes):
        need = {}

        def add(ev):
            if ev is None:
                return
            k, v = ev
            if queue == "pe" and k == "q_pe":
                return
            if need.get(k, 0) < v:
                need[k] = v

        for b in reads:
            add(b.last_w)
        for b in writes:
            add(b.last_w)
            for ev in b.reads:
                add(ev)
        waits = []
        kn = self.known[queue]
        for k, v in need.items():
            if kn.get(k, 0) >= v:
                continue
            kn[k] = v
            waits.append((k, v))
        return waits

    @staticmethod
    def _commit(ev, reads, writes):
        for b in reads:
            b.reads.append(ev)
        for b in writes:
            b.last_w = ev
            b.reads = []

    def op(self, queue, fn, reads=(), writes=()):
        waits = self._deps(queue, reads, writes)
        k = "q_" + queue
        v = self.count.get(k, 0) + 1
        self.count[k] = v
        ev = (k, v)
        self.ops[queue].append(Op(fn, waits, ev, 1))
        self._commit(ev, reads, writes)
        return ev

    def dma(self, queue, fn, semkey, reads=(), writes=()):
        waits = self._deps(queue, reads, writes)
        if semkey not in self.count:
            self.dma_sems.append(semkey)
        v = self.count.get(semkey, 0) + 16
        self.count[semkey] = v
        ev = (semkey, v)
        self.ops[queue].append(Op(fn, waits, ev, 16))
        self._commit(ev, reads, writes)
        return ev

    def wait_all(self, queue, bufs):
        waits = self._deps(queue, (), bufs)
        self.ops[queue].append(Op(None, waits, None, 0))

    def emit(self):
        nc = self.nc
        keys = ["q_" + q for q in QUEUES if ("q_" + q) in self.count] + self.dma_sems
        with contextlib.ExitStack() as st:
            sems = {k: st.enter_context(nc.semaphore("s_" + k)) for k in keys}
            block = st.enter_context(nc.Block())

            def run(queue):
                def body(eng):
                    for o in self.ops[queue]:
                        for (k, v) in o.waits:
                            eng.wait_ge(sems[k], v)
                        if o.fn is None:
                            continue
                        ins = o.fn(eng)
                        if o.event is not None:
                            ins.then_inc(sems[o.event[0]], o.inc)
                return body

            if self.ops["sp"]:
                block.sync(run("sp"))
            if self.ops["pe"]:
                block.tensor(run("pe"))
            if self.ops["act"]:
                block.scalar(run("act"))
            if self.ops["dve"]:
                block.vector(run("dve"))
            if self.ops["pool"]:
                block.gpsimd(run("pool"))


D = 1024
KC = 8
DFF = 2816
MC = 22
NS = 16
TB = 256
FB = 512
WIN = 128
NMOD = 9
LN_EPS = 1e-5
RMS_EPS = 1e-6
WEXT = 2048
GROUPS = (4, 4, 4, 5, 5)
GMAX = 5


class Cfg:
    def __init__(self, L, OWN, HALO, depth_full=4):
        self.L, self.OWN, self.HALO = L, OWN, HALO
        self.NP = HALO + OWN
        self.NT = self.NP + NS
        assert self.NP % FB == 0 and HALO % TB == 0
        self.NFB = self.NP // FB
        self.NTB = self.NP // TB
        self.ALPHA = float((2.0 * depth_full) ** 0.25)


ARENA = 67584


def build_program(cfg):
    L, NP, NT, HALO, OWN = cfg.L, cfg.NP, cfg.NT, cfg.HALO, cfg.OWN
    NFB, NTB, ALPHA = cfg.NFB, cfg.NTB, cfg.ALPHA
    nc = bass.Bass("TRN2", target_bir_lowering=False)
    P = Prog(nc)

    def din(name, shape):
        return nc.dram_tensor(name, list(shape), F32, kind="ExternalInput").ap()

    def dout(name, shape):
        return nc.dram_tensor(name, list(shape), F32, kind="ExternalOutput").ap()

    d_xT = din("xT", [128, KC, NT])
    d_cT = din("cT", [128, KC, 17])
    d_wada = din("wada", [L, 72, 128, KC, 128])
    d_bada = din("bada", [L, 128, 72])
    d_wgu = [din("wgu1", [L, MC, 128, 2 * KC * 128]), din("wgu2", [L, MC, 128, 2 * KC * 128])]
    d_wd = [din("wd1", [L, MC, 128, D]), din("wd2", [L, MC, 128, D])]
    d_win = din("win", [L, 4, 128, KC, 512])
    d_wout = din("wout", [L, 2, 128, KC, 512])
    d_lng = din("lng", [128, L, 3, KC])
    d_lnb = din("lnb", [128, L, 3, KC])
    d_convw = din("convw", [128, L, 3, 2])
    d_poolw = din("poolw", [L, 2, 128, 128])
    d_pscale = din("pscale", [128, L, 2])
    d_mixg = din("mixg", [128, L, KC])
    d_sinks = din("sinks", [128, L, 4])
    d_invcnt = din("invcnt", [128, 2, 16])
    d_invw = din("invw", [128, 2])
    d_valid = din("valid", [128, 1])
    d_masks = din("masks", [128, 3, 128])
    d_ident = din("ident", [16, 16])
    d_identb = din("identb", [128, 128])
    d_sinks4 = din("sinks4", [128, L, 8])
    d_cntr = din("cntr", [128, 2, 16])
    d_kTz = din("kTz", [L, NS, 128, 512])
    d_vd = din("vd", [L, NS, 128, 256])
    d_ck = din("ck", [L, NS, 128, 128])
    d_cv = din("cv", [L, NS, 128, 128])
    d_sconvT = din("sconvT", [L, 128, 2, 2, NS])
    d_spoolT = din("spoolT", [L, 128, 2, NS, 16])
    d_sconv = din("sconv", [L, NS, 2, 256])
    d_spool = din("spool", [L, NS, 15, 256])

    o_x = dout("o_x", [128, KC, OWN + NS])
    o_pc = dout("o_pc", [L, 2, 256])
    o_pp = dout("o_pp", [L, 15, 256])
    o_pk = dout("o_pk", [L, 128, 128])
    o_pv = dout("o_pv", [L, 128, 128])
    o_sc = dout("o_sc", [L, NS, 2, 256])
    o_spl = dout("o_spl", [L, NS, 15, 256])
    o_sk = dout("o_sk", [L, NS, 128, 128])
    o_sv = dout("o_sv", [L, NS, 128, 128])
    B_out = Buf("outputs")
    n_out = [0]

    def out_dma(dst, src, reads):
        k = f"od{n_out[0] % 8}"
        n_out[0] += 1
        P.dma("sp", lambda e: e.dma_start(out=dst, in_=src), k, reads=reads + [B_out], writes=[])
        return k

    def act_id(e, out, in_, scale=1.0, bias=0.0):
        return e.activation(out=out, in_=in_, func=AF.Prelu, scale=scale, bias=bias, alpha=1.0)

    st = contextlib.ExitStack()
    with st:
        def sb(name, shape, dt=F32):
            return st.enter_context(nc.sbuf_tensor("s_" + name, list(shape), dt))

        X = sb("X", [128, KC, NT])
        HIN = sb("HIN", [128, KC, NT], BF16)
        Mt = [sb("M0", [128, 72, 17]), sb("M1", [128, 72, 17])]
        GT = sb("GT", [128, 3, KC, 17])
        SC_ = sb("siluc", [128, KC, 17], BF16)
        cTs = sb("cTs", [128, KC, 17])
        bada = sb("bada", [128, 72])
        lng = sb("lng", [128, L, 3, KC])
        lnb = sb("lnb", [128, L, 3, KC])
        convw = sb("convw", [128, L, 3, 2])
        pscale = sb("pscale", [128, L, 2])
        mixg = sb("mixg", [128, L, KC])
        sinke = sb("sinke", [128, L, 4])
        invw = sb("invw", [128, 2])
        valid = sb("valid", [128, 1])
        masks = sb("masks", [128, 3, 128], BF16)
        ident = sb("ident", [16, 16])
        ones = sb("ones", [128, 128], BF16)
        identb = sb("identb", [128, 128], BF16)
        sinke4 = sb("sinke4", [128, L, 8])
        cntr = sb("cntr", [128, 2, 16])
        poolw = sb("poolw", [128, L, 2, 128], BF16)
        lnc = sb("lnc", [128, 4, KC])
        lncs = sb("lncs", [128, 2, KC, NS])
        epsT = sb("epsT", [128, 2])
        VtokS = sb("VtokS", [16, 256])
        QS = sb("QS", [128, 4, NS], BF16)
        KZS = sb("KZS", [128, 4, NS], BF16)
        YS = sb("YS", [128, KC, NS])
        arena = sb("arena", [128, ARENA], mybir.dt.uint8)

        class Carver:
            def __init__(self, off=0):
                self.off = off

            def take(self, shape, dt):
                esz = 2 if dt == BF16 else 4
                n = int(np.prod(shape))
                self.off = (self.off + 31) // 32 * 32
                a = arena[:, self.off:self.off + n * esz].bitcast(dt)
                self.off += n * esz
                assert self.off <= ARENA, (self.off, ARENA)
                if len(shape) == 1:
                    return a
                names = " ".join(f"d{i}" for i in range(len(shape)))
                kw = {f"d{i}": int(s) for i, s in enumerate(shape)}
                return a.rearrange(f"p ({names}) -> p {names}", **kw)

        psum = st.enter_context(nc.psum_tensor("psum", [128, 8 * 512], F32))

        def bank(i, lo=0, hi=512, p0=0, p1=128):
            return psum[p0:p1, i * 512 + lo:i * 512 + hi]

        B_ps = [Buf(f"ps{i}") for i in range(8)]
        B_hb = [[B_ps[i], B_ps[i]] for i in range(2)]

        FBLK = [(i * FB, FB) for i in range(NFB)] + [(NP, NS)]
        NB = len(FBLK)

        def set_start(cs):
            for i in range(NFB):
                lo = max(i * FB, cs)
                FBLK[i] = (lo, max(0, (i + 1) * FB - lo))
        B_X = [[Buf(f"X{c}_{b}") for b in range(NB)] for c in range(KC)]
        B_H = [[Buf(f"H{c}_{t}") for t in range(NTB + 1)] for c in range(KC)]

        def hin_bufs(c0, w, chunks=range(KC)):
            if c0 >= NP:
                ts = [NTB]
            else:
                ts = list(range(c0 // TB, (c0 + w - 1) // TB + 1))
            return [B_H[c][t] for c in chunks for t in ts]

        B_M = [Buf("M0"), Buf("M1")]
        B_GT = Buf("GT")
        B_const = Buf("const")
        B_lnc = Buf("lnc")
        B_samp = Buf("samp_persist")
        arena_bufs = []

        def abuf(name):
            return Buf(name)

        def phase_switch(new_bufs, keep=()):
            mx = {}
            for b in arena_bufs:
                if b in keep:
                    continue
                evs = list(b.reads)
                if b.last_w is not None:
                    evs.append(b.last_w)
                for k, v in evs:
                    if mx.get(k, 0) < v:
                        mx[k] = v
            evl = list(mx.items())
            for b in new_bufs:
                b.last_w = None
                b.reads = list(evl)
            del arena_bufs[:]
            arena_bufs.extend(list(keep) + list(new_bufs))

        def ld(dst, src, key, queue="sp"):
            P.dma(queue, lambda e: e.dma_start(out=dst, in_=src), key, writes=[B_const])

        for (dst, src, key) in [(cTs[:], d_cT, "c0"), (lng[:], d_lng, "c2"),
                                (lnb[:], d_lnb, "c3"), (convw[:], d_convw, "c4"), (pscale[:], d_pscale, "c5"),
                                (mixg[:], d_mixg, "c6"), (sinke[:], d_sinks, "c7"),
                                (invw[:], d_invw, "c9"), (valid[:], d_valid, "c10"), (ident[:], d_ident, "c11")]:
            ld(dst, src, key)
        ld(masks[:], d_masks, "c12", "pool")
        ld(identb[:], d_identb, "c14", "pool")
        ld(sinke4[:], d_sinks4, "c15")
        ld(cntr[:], d_cntr, "c16")
        for l in range(L):
            ld(poolw[:, l, :, :], d_poolw[l].rearrange("c p n -> p c n"), f"c13_{l}", "pool")
        for c in range(KC):
            P.dma("sp", lambda e, c=c: e.dma_start(out=X[:, c, :], in_=d_xT[:, c, :]), f"xl{c}",
                  writes=[B_X[c][b] for b in range(NB)])
        P.op("dve", lambda e: e.memset(ones[:], 1.0), writes=[B_const])
        P.op("dve", lambda e: e.memset(epsT[:, 0:1], LN_EPS), writes=[B_const])
        P.op("dve", lambda e: e.memset(epsT[:, 1:2], RMS_EPS), writes=[B_const])
        P.op("dve", lambda e: e.memset(KZS[:], 0.0), writes=[B_samp])
        P.op("act", lambda e: e.activation(out=SC_[:], in_=cTs[:], func=AF.Silu), reads=[B_const], writes=[B_const])
        P.op("act", lambda e: e.activation(out=sinke[:], in_=sinke[:], func=AF.Exp), reads=[B_const], writes=[B_const])
        P.op("act", lambda e: e.activation(out=sinke4[:], in_=sinke4[:], func=AF.Exp), reads=[B_const], writes=[B_const])

        ada = {"ring": None, "bufs": None, "n": 0}
        B_bada = Buf("bada")

        def ada_load_bias(l):
            P.dma("sp", lambda e: e.dma_start(out=bada[:], in_=d_bada[l]), "c1", writes=[B_bada])

        def ada_item(l, j):
            slot = ada["n"] % 2
            ada["n"] += 1
            tile = ada["ring"][slot]
            bslot = ada["bufs"][slot]
            P.dma("pool", lambda e: e.dma_start(out=tile, in_=d_wada[l, j]), f"ada{slot}", writes=[bslot])

            def mm(e):
                ins = None
                for kc in range(KC):
                    ins = e.matmul(bank(7, 0, 17), tile[:, kc, :], SC_[:, kc, :], start=(kc == 0), stop=(kc == KC - 1))
                return ins
            P.op("pe", mm, reads=[bslot, B_const], writes=[B_ps[7]])
            P.op("dve", lambda e: e.tensor_scalar(out=Mt[l % 2][:, j, :], in0=bank(7, 0, 17), scalar1=bada[:, j:j + 1],
                                                   scalar2=None, op0=ALU.add),
                 reads=[B_ps[7], B_bada], writes=[B_M[l % 2]])

        def gates(l):
            M = Mt[l % 2]
            for i, (row, f) in enumerate([(2, 0.5), (5, 1.0), (8, 0.5)]):
                P.op("dve", lambda e, i=i, row=row, f=f: e.tensor_scalar(out=GT[:, i, :, :], in0=M[:, row * KC:(row + 1) * KC, :],
                                                                          scalar1=f, scalar2=None, op0=ALU.mult),
                     reads=[B_M[l % 2]], writes=[B_GT])

        def ln_consts(l, k, nxt, final=False):
            a = 1.0 if final else ALPHA
            rd = [B_const]
            P.op("dve", lambda e: e.tensor_scalar(out=lnc[:, 0, :], in0=lng[:, l, k, :], scalar1=a, scalar2=None, op0=ALU.mult),
                 reads=rd, writes=[B_lnc])
            P.op("dve", lambda e: e.tensor_scalar(out=lnc[:, 1, :], in0=lnb[:, l, k, :], scalar1=a, scalar2=None, op0=ALU.mult),
                 reads=rd, writes=[B_lnc])
            if nxt is None:
                return
            mi, shr, scr = nxt
            M = Mt[mi]
            rd = [B_const, B_M[mi]]
            P.op("dve", lambda e: e.scalar_tensor_tensor(out=lnc[:, 2, :], in0=M[:, scr * KC:(scr + 1) * KC, 0], scalar=1.0,
                                                          in1=lng[:, l, k, :], op0=ALU.add, op1=ALU.mult),
                 reads=rd, writes=[B_lnc])
            P.op("dve", lambda e: e.scalar_tensor_tensor(out=lnc[:, 3, :], in0=M[:, scr * KC:(scr + 1) * KC, 0], scalar=1.0,
                                                          in1=lnb[:, l, k, :], op0=ALU.add, op1=ALU.mult),
                 reads=rd, writes=[B_lnc])
            P.op("dve", lambda e: e.tensor_tensor(out=lnc[:, 3, :], in0=lnc[:, 3, :], in1=M[:, shr * KC:(shr + 1) * KC, 0], op=ALU.add),
                 reads=rd + [B_lnc], writes=[B_lnc])
            gb_ = lng[:, l, k, :].unsqueeze(2).to_broadcast([128, KC, NS])
            bb_ = lnb[:, l, k, :].unsqueeze(2).to_broadcast([128, KC, NS])
            P.op("dve", lambda e: e.scalar_tensor_tensor(out=lncs[:, 0, :, :], in0=M[:, scr * KC:(scr + 1) * KC, 1:17], scalar=1.0,
                                                          in1=gb_, op0=ALU.add, op1=ALU.mult),
                 reads=rd, writes=[B_lnc])
            P.op("dve", lambda e: e.scalar_tensor_tensor(out=lncs[:, 1, :, :], in0=M[:, scr * KC:(scr + 1) * KC, 1:17], scalar=1.0,
                                                          in1=bb_, op0=ALU.add, op1=ALU.mult),
                 reads=rd, writes=[B_lnc])
            P.op("dve", lambda e: e.tensor_tensor(out=lncs[:, 1, :, :], in0=lncs[:, 1, :, :], in1=M[:, shr * KC:(shr + 1) * KC, 1:17], op=ALU.add),
                 reads=rd + [B_lnc], writes=[B_lnc])

        lnS = {}

        NRING = 3

        def carve_ln(cv):
            lnS.clear()
            lnS["n"] = 0
            lnS["pend"] = []
            lnS["vb"] = [cv.take([FB], BF16) for _ in range(NRING)]
            lnS["vq"] = [cv.take([FB], BF16) for _ in range(NRING)]
            lnS["Bvb"] = [abuf(f"vb{i}") for i in range(NRING)]
            lnS["Bvq"] = [abuf(f"vq{i}") for i in range(NRING)]
            lnS["rstd"] = [cv.take([FB], F32) for _ in range(2)]
            lnS["nmr"] = [cv.take([FB], F32) for _ in range(2)]
            lnS["Brs"] = [abuf("rstd0"), abuf("rstd1")]
            lnS["Bnm"] = [abuf("nmr0"), abuf("nmr1")]
            lnS["sscr"] = cv.take([NS], F32)
            lnS["Bsscr"] = abuf("sscr")
            return lnS["Bvb"] + lnS["Bvq"] + lnS["Brs"] + lnS["Bnm"] + [lnS["Bsscr"]]

        def ln_stats_feed(bi, c, on_block_done):
            c0, w = FBLK[bi]
            r = lnS["n"] % NRING
            lnS["n"] += 1
            vb, vq = lnS["vb"][r], lnS["vq"][r]
            Bvb, Bvq = lnS["Bvb"][r], lnS["Bvq"][r]
            P.op("act", lambda e: act_id(e, out=vb[:, :w], in_=X[:, c, c0:c0 + w]), reads=[B_X[c][bi]], writes=[Bvb])
            P.op("act", lambda e: e.activation(out=vq[:, :w], in_=X[:, c, c0:c0 + w], func=AF.Square), reads=[B_X[c][bi]], writes=[Bvq])

            def pe_part():
                def mm(e):
                    e.matmul(bank(6, 0, w), ones[:], vb[:, :w], start=(c == 0), stop=(c == KC - 1))
                    return e.matmul(bank(7, 0, w), ones[:], vq[:, :w], start=(c == 0), stop=(c == KC - 1))
                P.op("pe", mm, reads=[Bvb, Bvq, B_const], writes=[B_ps[6], B_ps[7]])
                if c == KC - 1:
                    on_block_done(bi)
            lnS["pend"].append(pe_part)
            if len(lnS["pend"]) > 2:
                lnS["pend"].pop(0)()

        def ln_flush():
            while lnS["pend"]:
                lnS["pend"].pop(0)()

        def ln_math(bi):
            c0, w = FBLK[bi]
            q = bi % 2
            rstd, nmr = lnS["rstd"][q], lnS["nmr"][q]
            Brs, Bnm = lnS["Brs"][q], lnS["Bnm"][q]
            inv = 1.0 / D
            P.op("act", lambda e: e.activation(out=rstd[:, :w], in_=bank(6, 0, w), func=AF.Square, scale=inv), reads=[B_ps[6]], writes=[Brs])
            P.op("dve", lambda e: e.scalar_tensor_tensor(out=rstd[:, :w], in0=bank(7, 0, w), scalar=inv, in1=rstd[:, :w],
                                                          op0=ALU.mult, op1=ALU.subtract),
                 reads=[B_ps[7], Brs], writes=[Brs])
            P.op("act", lambda e: e.activation(out=rstd[:, :w], in_=rstd[:, :w], func=AF.Ln, bias=epsT[:, 0:1]), reads=[Brs, B_const], writes=[Brs])
            P.op("act", lambda e: e.activation(out=rstd[:, :w], in_=rstd[:, :w], func=AF.Exp, scale=-0.5), reads=[Brs], writes=[Brs])
            P.op("dve", lambda e: e.scalar_tensor_tensor(out=nmr[:, :w], in0=bank(6, 0, w), scalar=-inv, in1=rstd[:, :w],
                                                          op0=ALU.mult, op1=ALU.mult),
                 reads=[B_ps[6], Brs], writes=[Bnm])

        def ln_apply(bi, with_hin):
            c0, w = FBLK[bi]
            samp = (c0 >= NP)
            q = bi % 2
            rstd, nmr = lnS["rstd"][q], lnS["nmr"][q]
            Brs, Bnm = lnS["Brs"][q], lnS["Bnm"][q]
            tmp, Btmp = lnS["sscr"], lnS["Bsscr"]
            for c in range(KC):
                xs = X[:, c, c0:c0 + w]
                Bx = B_X[c][bi]
                P.op("dve", lambda e, xs=xs: e.tensor_tensor(out=xs, in0=xs, in1=rstd[:, :w], op=ALU.mult), reads=[Bx, Brs], writes=[Bx])
                P.op("dve", lambda e, xs=xs: e.tensor_tensor(out=xs, in0=xs, in1=nmr[:, :w], op=ALU.add), reads=[Bx, Bnm], writes=[Bx])
                if with_hin:
                    hs = HIN[:, c, c0:c0 + w]
                    hb = hin_bufs(c0, w, [c])
                    if not samp:
                        P.op("pool", lambda e, xs=xs, hs=hs, c=c: e.tensor_scalar(out=hs, in0=xs, scalar1=lnc[:, 2, c:c + 1],
                                                                                   scalar2=lnc[:, 3, c:c + 1], op0=ALU.mult, op1=ALU.add),
                             reads=[Bx, B_lnc], writes=hb)
                    else:
                        P.op("dve", lambda e, xs=xs, c=c: e.tensor_tensor(out=tmp[:, :w], in0=xs, in1=lncs[:, 0, c, :], op=ALU.mult),
                             reads=[Bx, B_lnc], writes=[Btmp])
                        P.op("dve", lambda e, hs=hs, c=c: e.tensor_tensor(out=hs, in0=tmp[:, :w], in1=lncs[:, 1, c, :], op=ALU.add),
                             reads=[Btmp, B_lnc], writes=hb)
                P.op("act", lambda e, xs=xs, c=c: act_id(e, xs, xs, lnc[:, 0, c:c + 1], lnc[:, 1, c:c + 1]),
                     reads=[Bx, B_lnc], writes=[Bx])

        def resid_evac(bi, o, ps_ap, ps_bufs, gi):
            c0, w = FBLK[bi]
            xs = X[:, o, c0:c0 + w]
            if c0 < NP:
                P.op("dve", lambda e: e.scalar_tensor_tensor(out=xs, in0=ps_ap, scalar=GT[:, gi, o, 0:1], in1=xs,
                                                              op0=ALU.mult, op1=ALU.add),
                     reads=ps_bufs + [B_GT, B_X[o][bi]], writes=[B_X[o][bi]])
            else:
                t = lnS["sscr"]
                Bt = lnS["Bsscr"]
                P.op("dve", lambda e: e.tensor_tensor(out=t[:, :w], in0=ps_ap, in1=GT[:, gi, o, 1:17], op=ALU.mult),
                     reads=ps_bufs + [B_GT], writes=[Bt])
                P.op("dve", lambda e: e.tensor_tensor(out=xs, in0=t[:, :w], in1=xs, op=ALU.add),
                     reads=[Bt, B_X[o][bi]], writes=[B_X[o][bi]])

        def ffn_phase(l, which, ada_next, ln_k, nxt, cs, final=False):
            set_start(cs)
            cv = Carver()
            HID = cv.take([GMAX, NT], BF16)
            GU = [cv.take([2 * KC * 128], BF16) for _ in range(2)]
            DW = cv.take([GMAX, D], BF16)
            ada["ring"] = [cv.take([KC, 128], BF16) for _ in range(2)]
            SG = cv.take([FB], F32)
            lnb_ = carve_ln(cv)
            B_hid = [[abuf(f"hid{m}_{b}") for b in range(NB)] for m in range(GMAX)]
            B_gu = [abuf(f"gu{i}") for i in range(2)]
            B_dw = [abuf(f"dw{i}") for i in range(GMAX)]
            B_sg = abuf("sg")
            ada["bufs"] = [abuf("adar0"), abuf("adar1")]
            phase_switch([b for row in B_hid for b in row] + B_gu + B_dw + [B_sg] + ada["bufs"] + lnb_)

            gi = 0 if which == 0 else 2
            ada_js = list(range(72)) if ada_next is not None else []
            if ada_next is not None:
                ada_load_bias(ada_next)
            n_gu = 0
            n_ps = 0
            n_pd = 0
            m0 = 0
            ada_slot = [0]

            prevb = [None]

            def blk_done(bi):
                ln_math(bi)
                if prevb[0] is not None:
                    ln_apply(prevb[0], nxt is not None)
                prevb[0] = bi
            for g, gsz in enumerate(GROUPS):
                last = (g == len(GROUPS) - 1)
                for mi in range(gsz):
                    m = m0 + mi
                    slot = n_gu % 2
                    n_gu += 1
                    gut = GU[slot]
                    P.dma("pool", lambda e, gut=gut, m=m: e.dma_start(out=gut, in_=d_wgu[which][l, m]), f"gu{slot}", writes=[B_gu[slot]])
                    for bi, (c0, w) in enumerate(FBLK):
                        if w == 0:
                            continue
                        pg, pu = (0, 1) if n_ps % 2 == 0 else (2, 3)
                        n_ps += 1

                        def mm(e, gut=gut, c0=c0, w=w, pg=pg, pu=pu):
                            ins = None
                            for kc in range(KC):
                                ins = e.matmul(bank(pg, 0, w), gut[:, kc * 128:(kc + 1) * 128], HIN[:, kc, c0:c0 + w],
                                               start=(kc == 0), stop=(kc == KC - 1))
                            for kc in range(KC):
                                ins = e.matmul(bank(pu, 0, w), gut[:, (KC + kc) * 128:(KC + kc + 1) * 128], HIN[:, kc, c0:c0 + w],
                                               start=(kc == 0), stop=(kc == KC - 1))
                            return ins
                        P.op("pe", mm, reads=[B_gu[slot]] + hin_bufs(c0, w), writes=[B_ps[pg], B_ps[pu]])
                        P.op("act", lambda e, pg=pg, w=w: e.activation(out=SG[:, :w], in_=bank(pg, 0, w), func=AF.Silu),
                             reads=[B_ps[pg]], writes=[B_sg])
                        P.op("dve", lambda e, pu=pu, w=w, mi=mi, c0=c0: e.tensor_tensor(out=HID[:, mi, c0:c0 + w], in0=SG[:, :w],
                                                                                        in1=bank(pu, 0, w), op=ALU.mult),
                             reads=[B_sg, B_ps[pu]], writes=[B_hid[mi][bi]])
                        ada_slot[0] += 1
                        if ada_js and (72 - len(ada_js)) < (ada_slot[0] * 72) // 120:
                            ada_item(ada_next, ada_js.pop(0))
                if last:
                    while ada_js:
                        ada_item(ada_next, ada_js.pop(0))
                    ln_consts(l, ln_k, nxt, final)
                for mi in range(gsz):
                    m = m0 + mi
                    P.dma("pool", lambda e, mi=mi, m=m: e.dma_start(out=DW[:, mi, :], in_=d_wd[which][l, m]), f"dw{mi}", writes=[B_dw[mi]])
                for bi, (c0, w) in enumerate(FBLK):
                    if w == 0:
                        continue
                    for o in range(KC):
                        pd = 4 + (n_pd % 2)
                        n_pd += 1

                        def mm(e, c0=c0, w=w, pd=pd, o=o, gsz=gsz):
                            ins = None
                            for mi in range(gsz):
                                ins = e.matmul(bank(pd, 0, w), DW[:, mi, o * 128:(o + 1) * 128], HID[:, mi, c0:c0 + w],
                                               start=(mi == 0), stop=(mi == gsz - 1))
                            return ins
                        P.op("pe", mm, reads=B_dw[:gsz] + [B_hid[mi][bi] for mi in range(gsz)], writes=[B_ps[pd]])
                        resid_evac(bi, o, bank(pd, 0, w), [B_ps[pd]], gi)
                        if last:
                            ln_stats_feed(bi, o, blk_done)
                if last:
                    ln_flush()
                    ln_apply(NB - 1, nxt is not None)
                m0 += gsz
            while ada_js:
                ada_item(ada_next, ada_js.pop(0))

        def mixer_A(l, cs):
            cv = Carver()
            WIN_ = cv.take([KC, WEXT], BF16)
            off_after_win = cv.off
            CVx = cv.take([2, TB + 2], F32)
            GBf = cv.take([2 * TB], F32)
            GBs = GBf.rearrange("p (c t) -> p c t", c=2)
            TMP = [cv.take([TB], F32) for _ in range(2)]
            Ux = cv.take([2, TB + 15], F32)
            Sa = cv.take([TB + 15], F32)
            Sb = cv.take([TB + 15], F32)
            PL = cv.take([2, TB], BF16)
            Q = cv.take([4, TB], BF16)
            KZ = cv.take([4, TB + 128], BF16)
            Vd = [cv.take([256], BF16) for _ in range(3)]
            PT = cv.take([1024], BF16)
            RD = cv.take([4, 128], F32)
            Y = cv.take([KC, TB], F32)
            SQ = [cv.take([TB], BF16) for _ in range(2)]
            RSf = cv.take([3 * TB], F32)
            RS = RSf.rearrange("p (g t) -> p g t", g=3)
            B_win = [abuf(f"win{i}") for i in range(4)]
            B_cvx, B_gbs, B_ux, B_sa, B_sb, B_pl, B_q, B_kz = (abuf(n) for n in ("cvx", "gbs", "ux", "sa", "sb", "pl", "q", "kz"))
            B_tmp = [abuf("tmp0"), abuf("tmp1")]
            B_vd = [abuf(f"vd{i}") for i in range(3)]
            B_pt, B_rd, B_rs = abuf("pt"), abuf("rd"), abuf("rs")
            B_y = [abuf(f"y{c}") for c in range(KC)]
            B_sq = [abuf("sq0"), abuf("sq1")]
            newb = B_win + [B_cvx, B_gbs, B_ux, B_sa, B_sb, B_pl, B_q, B_kz] + B_tmp + B_vd + [B_pt, B_rd, B_rs] + B_y + B_sq
            phase_switch(newb)

            for i in range(4):
                P.dma("pool", lambda e, i=i: e.dma_start(out=WIN_[:, :, i * 512:(i + 1) * 512], in_=d_win[l, i]), f"win{i}", writes=[B_win[i]])
            P.op("dve", lambda e: e.memset(KZ[:], 0.0), writes=[B_kz])
            P.op("dve", lambda e: e.memset(CVx[:, :, 0:2], 0.0), writes=[B_cvx])
            P.op("dve", lambda e: e.memset(Ux[:, :, 0:15], 0.0), writes=[B_ux])

            st_ = {"hs": 0, "sq": 0}

            def inproj_chunk(c0, w, col0, evac):
                s = st_["hs"] % 4
                st_["hs"] += 1
                bk, hf = s // 2, s % 2
                wi = col0 // 512

                def mm(e):
                    ins = None
                    for kc in range(KC):
                        ins = e.matmul(bank(bk, hf * 256, hf * 256 + w), WIN_[:, kc, col0:col0 + 128], HIN[:, kc, c0:c0 + w],
                                       start=(kc == 0), stop=(kc == KC - 1))
                    return ins
                P.op("pe", mm, reads=[B_win[wi]] + hin_bufs(c0, w), writes=[B_hb[bk][hf]])
                evac(lambda p0=0, p1=128: bank(bk, hf * 256, hf * 256 + w, p0, p1), [B_hb[bk][hf]])

            def rms_group(grp, chunks, n, w, ysrc, ybufs, rs_ap, sqbank, sqlo):
                for i, c in enumerate(chunks):
                    r = st_["sq"] % 2
                    st_["sq"] += 1
                    sq = SQ[r]
                    P.op("act", lambda e, sq=sq, c=c: e.activation(out=sq[:, :w], in_=ysrc(c), func=AF.Square),
                         reads=[ybufs[c]], writes=[B_sq[r]])
                    P.op("pe", lambda e, sq=sq, i=i: e.matmul(bank(sqbank, sqlo, sqlo + w), ones[:], sq[:, :w], start=(i == 0),
                                                              stop=(i == len(chunks) - 1)),
                         reads=[B_sq[r], B_const], writes=[B_ps[sqbank]])
                P.op("act", lambda e: e.activation(out=rs_ap, in_=bank(sqbank, sqlo, sqlo + w), func=AF.Ln, scale=1.0 / n,
                                                   bias=epsT[:, 1:2]),
                     reads=[B_ps[sqbank], B_const], writes=[B_rs])
                P.op("act", lambda e: e.activation(out=rs_ap, in_=rs_ap, func=AF.Exp, scale=-0.5), reads=[B_rs], writes=[B_rs])

            JOWN = HALO // 128
            JS = cs // 128

            def tb_body(t, c0, w, first_own):
                if first_own:
                    P.op("pool", lambda e: e.tensor_scalar(out=CVx[:, :, 0:2], in0=CVx[:, :, 0:2], scalar1=valid[:, 0:1], scalar2=0.0,
                                                            op0=ALU.mult, op1=ALU.add), reads=[B_cvx, B_const], writes=[B_cvx])
                    P.op("pool", lambda e: e.tensor_scalar(out=Ux[:, :, 0:15], in0=Ux[:, :, 0:15], scalar1=valid[:, 0:1], scalar2=0.0,
                                                            op0=ALU.mult, op1=ALU.add), reads=[B_ux, B_const], writes=[B_ux])
                for c in range(4):
                    inproj_chunk(c0, w, 1024 + c * 128,
                                 lambda ps, pb, c=c: P.op("act", lambda e: act_id(e, out=Q[:, c, :w], in_=ps()), reads=pb, writes=[B_q]))
                for h in range(2):
                    def ev(ps, pb, h=h):
                        P.op("dve", lambda e: e.tensor_copy(out=KZ[0:64, 2 * h, 128:128 + w], in_=ps(0, 64)), reads=pb, writes=[B_kz])
                        P.op("dve", lambda e: e.tensor_copy(out=KZ[64:128, 2 * h + 1, 128:128 + w], in_=ps(64, 128)), reads=pb, writes=[B_kz])
                    inproj_chunk(c0, w, 1536 + h * 128, ev)
                thunks = []

                def th_u(cc):
                    inproj_chunk(c0, w, 768 + cc * 128,
                                 lambda ps, pb: P.op("act", lambda e: act_id(e, out=Ux[:, cc, 15:15 + w], in_=ps()), reads=pb, writes=[B_ux]))

                def th_gc(cc):
                    tm = TMP[cc]
                    inproj_chunk(c0, w, 256 + cc * 128,
                                 lambda ps, pb: P.op("act", lambda e: act_id(e, out=tm[:, :w], in_=ps()), reads=pb, writes=[B_tmp[cc]]))

                def th_xin(cc):
                    tm = TMP[cc]
                    inproj_chunk(c0, w, 512 + cc * 128,
                                 lambda ps, pb: P.op("dve", lambda e: e.tensor_tensor(out=CVx[:, cc, 2:2 + w], in0=tm[:, :w], in1=ps(), op=ALU.mult),
                                                     reads=pb + [B_tmp[cc]], writes=[B_cvx]))

                def th_gb(cc):
                    inproj_chunk(c0, w, cc * 128,
                                 lambda ps, pb: P.op("act", lambda e: act_id(e, out=GBs[:, cc, :w], in_=ps()), reads=pb, writes=[B_gbs]))
                for jj in range(w // 128):
                    j = c0 // 128 + jj
                    vs = j % 3
                    ca = c0 + jj * 128

                    def mm(e, ca=ca):
                        ins = None
                        for kc in range(KC):
                            ins = e.matmul(bank(2, 0, 256), HIN[:, kc, ca:ca + 128], WIN_[:, kc, 1792:2048], start=(kc == 0), stop=(kc == KC - 1))
                        return ins
                    P.op("pe", mm, reads=[B_win[3]] + hin_bufs(ca, 128), writes=[B_ps[2]])
                    P.op("act", lambda e, vs=vs: act_id(e, out=Vd[vs][:, :], in_=bank(2, 0, 256)), reads=[B_ps[2]], writes=[B_vd[vs]])
                def conv_chain(cc):
                    acc = TMP[cc]
                    t2 = TMP[1 - cc]
                    P.op("pool", lambda e, cc=cc, acc=acc: e.tensor_scalar(out=acc[:, :w], in0=CVx[:, cc, 0:w], scalar1=convw[:, l, 0, cc:cc + 1],
                                                                            scalar2=0.0, op0=ALU.mult, op1=ALU.add),
                         reads=[B_cvx, B_const], writes=[B_tmp[cc]])
                    for k in (1, 2):
                        P.op("pool", lambda e, cc=cc, t2=t2, k=k: e.tensor_scalar(out=t2[:, :w], in0=CVx[:, cc, k:k + w],
                                                                                   scalar1=convw[:, l, k, cc:cc + 1], scalar2=0.0,
                                                                                   op0=ALU.mult, op1=ALU.add),
                             reads=[B_cvx, B_const], writes=[B_tmp[1 - cc]])
                        P.op("pool", lambda e, acc=acc, t2=t2: e.tensor_tensor(out=acc[:, :w], in0=acc[:, :w], in1=t2[:, :w], op=ALU.add),
                             reads=[B_tmp[0], B_tmp[1]], writes=[B_tmp[cc]])
                    P.op("pool", lambda e, cc=cc, acc=acc: e.tensor_tensor(out=Y[:, cc, :w], in0=acc[:, :w], in1=GBs[:, cc, :w], op=ALU.mult),
                         reads=[B_tmp[cc], B_gbs], writes=[B_y[cc]])
                def conv_tail():
                    P.op("pool", lambda e: e.tensor_copy(out=CVx[:, :, 0:2], in_=CVx[:, :, w:w + 2]), reads=[B_cvx], writes=[B_cvx])
                WX = w + 15

                def pool_chain(cc):
                    ux = Ux[:, cc, :]
                    P.op("pool", lambda e, ux=ux: e.tensor_tensor(out=Sa[:, 1:WX], in0=ux[:, 1:WX], in1=ux[:, 0:WX - 1], op=ALU.add),
                         reads=[B_ux], writes=[B_sa])
                    if cc == 0:
                        P.op("pool", lambda e: e.tensor_tensor(out=Sb[64:128, 3:WX], in0=Sa[64:128, 3:WX], in1=Sa[64:128, 1:WX - 2], op=ALU.add),
                             reads=[B_sa], writes=[B_sb])
                    else:
                        P.op("pool", lambda e: e.tensor_tensor(out=Sb[:, 3:WX], in0=Sa[:, 3:WX], in1=Sa[:, 1:WX - 2], op=ALU.add),
                             reads=[B_sa], writes=[B_sb])
                        P.op("pool", lambda e: e.tensor_tensor(out=Sa[:, 7:WX], in0=Sb[:, 7:WX], in1=Sb[:, 3:WX - 4], op=ALU.add),
                             reads=[B_sb], writes=[B_sa])
                        P.op("pool", lambda e: e.tensor_tensor(out=Sb[64:128, 15:WX], in0=Sa[64:128, 15:WX], in1=Sa[64:128, 7:WX - 8], op=ALU.add),
                             reads=[B_sa], writes=[B_sb])
                    for (p0, p1, src, Bs) in ((0, 64, Sa, B_sa), (64, 128, Sb, B_sb)):
                        P.op("pool", lambda e, p0=p0, p1=p1, src=src, cc=cc: e.tensor_scalar(
                            out=src[p0:p1, 15:WX], in0=src[p0:p1, 15:WX], scalar1=invw[p0:p1, cc:cc + 1], scalar2=0.0, op0=ALU.mult, op1=ALU.add),
                            reads=[Bs, B_const], writes=[Bs])
                        if first_own:
                            P.op("pool", lambda e, p0=p0, p1=p1, src=src, cc=cc: e.tensor_tensor(
                                out=src[p0:p1, 15:31], in0=src[p0:p1, 15:31], in1=cntr[p0:p1, cc, :], op=ALU.mult),
                                reads=[Bs, B_const], writes=[Bs])
                        P.op("pool", lambda e, p0=p0, p1=p1, src=src, cc=cc: e.tensor_tensor(
                            out=PL[p0:p1, cc, :w], in0=src[p0:p1, 15:WX], in1=Ux[p0:p1, cc, 15:WX], op=ALU.subtract),
                            reads=[Bs, B_ux], writes=[B_pl])

                def pool_mm(cc):
                    P.op("pe", lambda e, cc=cc: e.matmul(bank(3, 0, w), poolw[:, l, cc, :], PL[:, cc, :w], start=True, stop=True),
                         reads=[B_pl, B_const], writes=[B_ps[3]])
                    P.op("act", lambda e, cc=cc: act_id(e, Y[:, 2 + cc, :w], bank(3, 0, w), pscale[:, l, cc:cc + 1]),
                         reads=[B_ps[3], B_const], writes=[B_y[2 + cc]])
                def pool_tail():
                    P.op("pool", lambda e: e.tensor_copy(out=Ux[:, :, 0:15], in_=Ux[:, :, w:w + 15]), reads=[B_ux], writes=[B_ux])
                thunks += [lambda: th_u(0), lambda: (th_u(1), pool_chain(0), pool_chain(1), pool_tail()),
                           lambda: th_gc(0), lambda: th_xin(0), lambda: (th_gb(0), conv_chain(0)),
                           lambda: th_gc(1), lambda: th_xin(1), lambda: (th_gb(1), conv_chain(1), conv_tail()),
                           lambda: pool_mm(0), lambda: pool_mm(1)]
                nper = 2 if w == TB else 4

                def pop_thunks(n):
                    for _ in range(n):
                        if thunks:
                            thunks.pop(0)()
                for jj in range(w // 128):
                    j = c0 // 128 + jj
                    qlo = jj * 128
                    kprev = qlo
                    kdiag = 128 + qlo
                    has_prev = j > JS
                    mprev = 2 if j == JOWN else 0
                    for h in range(2):
                        def mm(e, h=h, qlo=qlo, kprev=kprev, kdiag=kdiag, has_prev=has_prev, mprev=mprev):
                            ins = None
                            for g in range(4):
                                c = 2 * h + g // 2
                                half = g % 2
                                if has_prev:
                                    e.matmul(bank(4, g * 128, (g + 1) * 128), identb[:], masks[:, mprev, :], start=True, stop=False)
                                    ins = e.matmul(bank(4, g * 128, (g + 1) * 128), KZ[:, 2 * h + half, kprev:kprev + 128],
                                                   Q[:, c, qlo:qlo + 128], start=False, stop=True)
                                e.matmul(bank(5, g * 128, (g + 1) * 128), identb[:], masks[:, 1, :], start=True, stop=False)
                                ins = e.matmul(bank(5, g * 128, (g + 1) * 128), KZ[:, 2 * h + half, kdiag:kdiag + 128],
                                               Q[:, c, qlo:qlo + 128], start=False, stop=True)
                            return ins
                        P.op("pe", mm, reads=[B_kz, B_q, B_const], writes=[B_ps[4], B_ps[5]])
                        if has_prev:
                            P.op("act", lambda e: e.activation(out=PT[:, 0:512], in_=bank(4), func=AF.Exp, scale=0.125),
                                 reads=[B_ps[4]], writes=[B_pt])
                        P.op("act", lambda e: e.activation(out=PT[:, 512:1024], in_=bank(5), func=AF.Exp, scale=0.125),
                             reads=[B_ps[5]], writes=[B_pt])
                        pop_thunks(nper)
                        vprev, vcur = (j - 1) % 3, j % 3

                        def pv(e, h=h, vprev=vprev, vcur=vcur, has_prev=has_prev):
                            if has_prev:
                                e.matmul(bank(6), Vd[vprev][:, h * 128:(h + 1) * 128], PT[:, 0:512], start=True, stop=False)
                            e.matmul(bank(6), Vd[vcur][:, h * 128:(h + 1) * 128], PT[:, 512:1024], start=not has_prev, stop=True)
                            if has_prev:
                                e.matmul(bank(7), ones[:], PT[:, 0:512], start=True, stop=False)
                            return e.matmul(bank(7), ones[:], PT[:, 512:1024], start=not has_prev, stop=True)
                        P.op("pe", pv, reads=[B_pt, B_vd[vprev], B_vd[vcur], B_const], writes=[B_ps[6], B_ps[7]])
                        sk4 = sinke4[:, l, 4 * h:4 * h + 4].unsqueeze(2).to_broadcast([128, 4, 128])
                        P.op("dve", lambda e, sk4=sk4: e.tensor_tensor(out=RD[:, :, :], in0=bank(7).rearrange("p (g q) -> p g q", g=4), in1=sk4,
                                                                        op=ALU.add), reads=[B_ps[7], B_const], writes=[B_rd])
                        P.op("act", lambda e: e.activation(out=RD[:, :, :], in_=RD[:, :, :], func=AF.Ln), reads=[B_rd], writes=[B_rd])
                        P.op("act", lambda e: e.activation(out=RD[:, :, :], in_=RD[:, :, :], func=AF.Exp, scale=-1.0), reads=[B_rd], writes=[B_rd])
                        for half in range(2):
                            p0, p1 = half * 64, half * 64 + 64
                            oo = bank(6, 0, 512, p0, p1).rearrange("p (gg hf q) -> p gg hf q", gg=2, hf=2)[:, :, half, :]
                            rr = RD[p0:p1, :, :].rearrange("p (gg hf) q -> p gg hf q", hf=2)[:, :, half, :]
                            P.op("dve", lambda e, oo=oo, rr=rr, p0=p0, p1=p1, h=h, qlo=qlo: e.tensor_tensor(
                                out=Y[p0:p1, 4 + 2 * h:6 + 2 * h, qlo:qlo + 128], in0=oo, in1=rr, op=ALU.mult),
                                reads=[B_ps[6], B_rd], writes=[B_y[4 + 2 * h], B_y[5 + 2 * h]])
                pop_thunks(100)
                P.op("act", lambda e: act_id(e, out=KZ[:, :, 0:128], in_=KZ[:, :, w:w + 128]), reads=[B_kz], writes=[B_kz])
                if t == NTB - 1:
                    token_major_tail(l, NP - 128, 128, WIN_, B_win, GBf, B_gbs, TMP[0], B_tmp[0], False, RSf, B_rs)
                ysrc = lambda c, w=w: Y[:, c, :w]
                slots = [(2, 256), (3, 0), (3, 256)]
                for grp, (chunks, n) in enumerate([([0, 1], 256.0), ([2, 3], 256.0), ([4, 5, 6, 7], 512.0)]):
                    bk_, lo_ = slots[grp]
                    for i, c in enumerate(chunks):
                        r = st_["sq"] % 2
                        st_["sq"] += 1
                        sq = SQ[r]
                        P.op("act", lambda e, sq=sq, c=c, n=n: e.activation(out=sq[:, :w], in_=Y[:, c, :w], func=AF.Square, scale=float(n) ** -0.5),
                             reads=[B_y[c]], writes=[B_sq[r]])
                        P.op("pe", lambda e, sq=sq, i=i, bk_=bk_, lo_=lo_, chunks=chunks: e.matmul(bank(bk_, lo_, lo_ + w), ones[:], sq[:, :w],
                                                                                                 start=(i == 0), stop=(i == len(chunks) - 1)),
                             reads=[B_sq[r], B_const], writes=[B_ps[bk_]])
                P.op("act", lambda e: e.activation(out=RSf[:, 0:3 * TB], in_=psum[:, 2 * 512 + 256:4 * 512], func=AF.Ln, bias=epsT[:, 1:2]),
                     reads=[B_ps[2], B_ps[3], B_const], writes=[B_rs])
                P.op("act", lambda e: e.activation(out=RSf[:, 0:3 * TB], in_=RSf[:, 0:3 * TB], func=AF.Exp, scale=-0.5), reads=[B_rs], writes=[B_rs])
                for c in range(KC):
                    grp = 0 if c < 2 else (1 if c < 4 else 2)
                    P.op("dve", lambda e, c=c, grp=grp, c0=c0, w=w: e.scalar_tensor_tensor(out=HIN[:, c, c0:c0 + w], in0=Y[:, c, :w],
                                                                                scalar=mixg[:, l, c:c + 1], in1=RS[:, grp, :w],
                                                                                op0=ALU.mult, op1=ALU.mult),
                         reads=[B_y[c], B_rs, B_const], writes=hin_bufs(c0, w, [c]))


            for t in range(NTB):
                c0_ = max(t * TB, cs)
                w_ = (t + 1) * TB - c0_
                if w_ > 0:
                    tb_body(t, c0_, w_, c0_ == HALO)

            cs = Carver(off_after_win)
            cvS = cs.take([2, NS], F32)
            gbS = cs.take([2, NS], F32)
            tmS = cs.take([NS], F32)
            accS = cs.take([NS], F32)
            S1 = cs.take([NS], F32)
            SCV = cs.take([2, 2, NS], F32)
            U16 = cs.take([2, NS, 16], F32)
            PLS = cs.take([2, NS], BF16)
            STa = cs.take([512], F32)
            STb = cs.take([256], F32)
            STc = cs.take([512], F32)
            B_s = abuf("sampA")
            B_scv, B_u16 = abuf("scv"), abuf("u16")
            B_sta, B_stb, B_stc = abuf("sta"), abuf("stb"), abuf("stc")
            phase_switch([B_s, B_scv, B_u16, B_sta, B_stb, B_stc], keep=B_win)
            P.dma("sp", lambda e: e.dma_start(out=SCV[:], in_=d_sconvT[l]), "scv", writes=[B_scv])
            P.dma("sp", lambda e: e.dma_start(out=U16[:], in_=d_spoolT[l]), "u16", writes=[B_u16])
            c0, w = NP, NS
            for cc in range(2):
                inproj_chunk(c0, w, 256 + cc * 128,
                             lambda ps, pb: P.op("act", lambda e: act_id(e, out=tmS[:, :], in_=ps()), reads=pb, writes=[B_s]))
                inproj_chunk(c0, w, 512 + cc * 128,
                             lambda ps, pb, cc=cc: P.op("dve", lambda e: e.tensor_tensor(out=cvS[:, cc, :], in0=tmS[:, :], in1=ps(), op=ALU.mult),
                                                        reads=pb + [B_s], writes=[B_s]))
                inproj_chunk(c0, w, cc * 128,
                             lambda ps, pb, cc=cc: P.op("act", lambda e: act_id(e, out=gbS[:, cc, :], in_=ps()), reads=pb, writes=[B_s]))
                inproj_chunk(c0, w, 768 + cc * 128,
                             lambda ps, pb, cc=cc: P.op("act", lambda e: act_id(e, out=U16[:, cc, :, 15], in_=ps()), reads=pb, writes=[B_u16]))
            for c in range(4):
                inproj_chunk(c0, w, 1024 + c * 128,
                             lambda ps, pb, c=c: P.op("act", lambda e: act_id(e, out=QS[:, c, :], in_=ps()), reads=pb, writes=[B_samp]))
            for h in range(2):
                def ev(ps, pb, h=h):
                    P.op("dve", lambda e: e.tensor_copy(out=KZS[0:64, 2 * h, :], in_=ps(0, 64)), reads=pb, writes=[B_samp])
                    P.op("dve", lambda e: e.tensor_copy(out=KZS[64:128, 2 * h + 1, :], in_=ps(64, 128)), reads=pb, writes=[B_samp])
                inproj_chunk(c0, w, 1536 + h * 128, ev)
            for cc in range(2):
                P.op("dve", lambda e, cc=cc: e.tensor_scalar(out=accS[:, :], in0=SCV[:, cc, 0, :], scalar1=convw[:, l, 0, cc:cc + 1], scalar2=None,
                                                             op0=ALU.mult), reads=[B_scv, B_const], writes=[B_s])
                P.op("dve", lambda e, cc=cc: e.scalar_tensor_tensor(out=accS[:, :], in0=SCV[:, cc, 1, :], scalar=convw[:, l, 1, cc:cc + 1],
                                                                    in1=accS[:, :], op0=ALU.mult, op1=ALU.add),
                     reads=[B_scv, B_const, B_s], writes=[B_s])
                P.op("dve", lambda e, cc=cc: e.scalar_tensor_tensor(out=accS[:, :], in0=cvS[:, cc, :], scalar=convw[:, l, 2, cc:cc + 1],
                                                                    in1=accS[:, :], op0=ALU.mult, op1=ALU.add),
                     reads=[B_const, B_s], writes=[B_s])
                P.op("dve", lambda e, cc=cc: e.tensor_tensor(out=YS[:, cc, :], in0=accS[:, :], in1=gbS[:, cc, :], op=ALU.mult),
                     reads=[B_s], writes=[B_samp])
            for cc in range(2):
                for half in range(2):
                    p0, p1 = half * 64, half * 64 + 64
                    wd_ = 2 ** (2 * cc + half + 1)
                    P.op("dve", lambda e, p0=p0, p1=p1, cc=cc, wd_=wd_: e.tensor_reduce(out=S1[p0:p1, :], in_=U16[p0:p1, cc, :, 16 - wd_:16],
                                                                                         axis=mybir.AxisListType.X, op=ALU.add),
                         reads=[B_u16], writes=[B_s])
                    P.op("dve", lambda e, p0=p0, p1=p1, cc=cc: e.scalar_tensor_tensor(out=PLS[p0:p1, cc, :], in0=S1[p0:p1, :],
                                                                                      scalar=invw[p0:p1, cc:cc + 1], in1=U16[p0:p1, cc, :, 15],
                                                                                      op0=ALU.mult, op1=ALU.subtract),
                         reads=[B_s, B_u16, B_const], writes=[B_s])
                P.op("pe", lambda e, cc=cc: e.matmul(bank(3, 0, NS), poolw[:, l, cc, :], PLS[:, cc, :], start=True, stop=True),
                     reads=[B_s, B_const], writes=[B_ps[3]])
                P.op("act", lambda e, cc=cc: act_id(e, YS[:, 2 + cc, :], bank(3, 0, NS), pscale[:, l, cc:cc + 1]),
                     reads=[B_ps[3], B_const], writes=[B_samp])
            token_major_tail(l, NP, NS, WIN_, B_win, STa, B_sta, STb, B_stb, True, STc, B_stc)

        def token_major_tail(l, ca, M, WIN_, B_win, ST, B_st, ST2, B_st2, is_sample, ST3=None, B_st3=None):
            hb = hin_bufs(ca, M)
            if ST3 is None:
                raise ValueError

            def mm_pass(bk, col0, n):
                def mm(e):
                    ins = None
                    for kc in range(KC):
                        ins = e.matmul(bank(bk, 0, n, 0, M), HIN[:, kc, ca:ca + M], WIN_[:, kc, col0:col0 + n], start=(kc == 0), stop=(kc == KC - 1))
                    return ins
                wis = sorted(set([col0 // 512, (col0 + n - 1) // 512]))
                P.op("pe", mm, reads=[B_win[i] for i in wis] + hb, writes=[B_ps[bk]] + B_hb[bk])
            mm_pass(0, 256, 512)
            P.op("act", lambda e: act_id(e, out=ST2[0:M, 0:256], in_=bank(0, 0, 256, 0, M)), reads=[B_ps[0], B_hb[0][0], B_hb[0][1]], writes=[B_st2])
            P.op("dve", lambda e: e.tensor_tensor(out=ST[0:M, 0:256], in0=ST2[0:M, 0:256], in1=bank(0, 256, 512, 0, M), op=ALU.mult),
                 reads=[B_ps[0], B_hb[0][0], B_hb[0][1], B_st2], writes=[B_st])
            mm_pass(1, 768, 256)
            P.op("act", lambda e: act_id(e, out=ST[0:M, 256:512], in_=bank(1, 0, 256, 0, M)), reads=[B_ps[1], B_hb[1][0], B_hb[1][1]], writes=[B_st])
            mm_pass(0, 1536, 512)
            P.op("act", lambda e: act_id(e, out=ST3[0:M, 0:512], in_=bank(0, 0, 512, 0, M)), reads=[B_ps[0], B_hb[0][0], B_hb[0][1]], writes=[B_st3])
            kv = ST3[0:M, 0:512].rearrange("p (a h r) -> p a h r", a=2, h=2)[:, :, :, 0:64]
            tag = "s" if is_sample else "p"
            if not is_sample:
                P.dma("sp", lambda e: e.dma_start(out=o_pc[l], in_=ST[M - 2:M, 0:256]), "o_st" + tag, reads=[B_st, B_out])
                P.dma("sp", lambda e: e.dma_start(out=o_pp[l], in_=ST[M - 15:M, 256:512]), "o_st" + tag, reads=[B_st, B_out])
                P.dma("sp", lambda e: e.dma_start(out=o_pk[l].rearrange("t (h d) -> t h d", h=2), in_=kv[:, 0, :, :]), "o_st3" + tag,
                      reads=[B_st3, B_out])
                P.dma("sp", lambda e: e.dma_start(out=o_pv[l].rearrange("t (h d) -> t h d", h=2), in_=kv[:, 1, :, :]), "o_st3" + tag,
                      reads=[B_st3, B_out])
            else:
                P.dma("sp", lambda e: e.dma_start(out=o_sc[l, :, 1, :], in_=ST[0:M, 0:256]), "o_st" + tag, reads=[B_st, B_out])
                P.dma("sp", lambda e: e.dma_start(out=o_spl[l, :, 14, :], in_=ST[0:M, 256:512]), "o_st" + tag, reads=[B_st, B_out])
                P.dma("sp", lambda e: e.dma_start(out=o_sk[l, :, 127, :].rearrange("t (h d) -> t h d", h=2), in_=kv[:, 0, :, :]), "o_st3" + tag,
                      reads=[B_st3, B_out])
                P.dma("sp", lambda e: e.dma_start(out=o_sv[l, :, 127, :].rearrange("t (h d) -> t h d", h=2), in_=kv[:, 1, :, :]), "o_st3" + tag,
                      reads=[B_st3, B_out])
                P.op("act", lambda e: act_id(e, out=VtokS[0:M, :], in_=ST3[0:M, 256:512]), reads=[B_st3], writes=[B_samp])

        def mixer_B(l, cs):
            set_start(cs)
            cv = Carver()
            WOUT = cv.take([KC, D], BF16)
            lnb_ = carve_ln(cv)
            KS = [cv.take([4, 512], BF16) for _ in range(2)]
            VS = [cv.take([4, 256], BF16) for _ in range(2)]
            PTs = cv.take([128], BF16)
            T1 = cv.take([NS, 4], F32)
            SQs = [cv.take([NS], BF16) for _ in range(2)]
            RSs = cv.take([3, NS], F32)
            B_wout = [abuf("wout0"), abuf("wout1")]
            B_ks = [abuf("ks0"), abuf("ks1")]
            B_vs = [abuf("vs0"), abuf("vs1")]
            B_pts, B_t1, B_rss = abuf("pts"), abuf("t1"), abuf("rss")
            B_sqs = [abuf("sqs0"), abuf("sqs1")]
            phase_switch(B_wout + lnb_ + B_ks + B_vs + [B_pts, B_t1, B_rss] + B_sqs)
            for i in range(2):
                P.dma("pool", lambda e, i=i: e.dma_start(out=WOUT[:, :, i * 512:(i + 1) * 512], in_=d_wout[l, i]), f"wout{i}", writes=[B_wout[i]])
            ln_consts(l, 1, (l % 2, 6, 7))
            def samp_dma(gq):
                slot = gq % 2
                ks, vs = KS[slot], VS[slot]
                P.dma("pool", lambda e, ks=ks, gq=gq: e.dma_start(out=ks[:, :, :], in_=d_kTz[l, 4 * gq:4 * gq + 4].rearrange("b p n -> p b n")),
                      f"ks{slot}", writes=[B_ks[slot]])
                P.dma("pool", lambda e, vs=vs, gq=gq: e.dma_start(out=vs[:, :, :], in_=d_vd[l, 4 * gq:4 * gq + 4].rearrange("b p n -> p b n")),
                      f"vs{slot}", writes=[B_vs[slot]])
            samp_dma(0)
            samp_dma(1)

            def samp_group(gq):
                slot = gq % 2
                ks, vs = KS[slot], VS[slot]
                for bl in range(4):
                    b = 4 * gq + bl
                    P.op("act", lambda e, ks=ks, bl=bl, b=b: act_id(e, out=ks[:, bl, :].rearrange("p (x k) -> p x k", x=4)[:, :, 0], in_=KZS[:, :, b]),
                         reads=[B_samp, B_ks[slot]], writes=[B_ks[slot]])
                    P.op("pe", lambda e, b=b: e.matmul(bank(3, 0, 256, 0, 1), ident[0:16, b:b + 1], VtokS[0:16, :], start=True, stop=True),
                         reads=[B_samp, B_const], writes=[B_ps[3]])
                    P.op("act", lambda e, vs=vs, bl=bl: act_id(e, out=vs[0:1, bl, :], in_=bank(3, 0, 256, 0, 1)), reads=[B_ps[3], B_vs[slot]],
                         writes=[B_vs[slot]])

                    def sc(e, ks=ks, bl=bl, b=b):
                        ins = None
                        for h in range(2):
                            for half in range(2):
                                for cl in range(2):
                                    col = b * 8 + h * 4 + half * 2 + cl
                                    x = h * 2 + half
                                    ins = e.matmul(bank(0, col, col + 1), ks[:, bl, x * 128:(x + 1) * 128], QS[:, 2 * h + cl, b:b + 1],
                                                   start=True, stop=True)
                        return ins
                    P.op("pe", sc, reads=[B_ks[slot], B_samp], writes=[B_ps[0], B_hb[0][0], B_hb[0][1]])
                g0, g1 = gq * 32, gq * 32 + 32
                P.op("act", lambda e, g0=g0, g1=g1: e.activation(out=PTs[:, g0:g1], in_=bank(0, g0, g1), func=AF.Exp, scale=0.125),
                     reads=[B_ps[0], B_hb[0][0], B_hb[0][1]], writes=[B_pts])

                def pv(e, vs=vs, gq=gq, g0=g0, g1=g1):
                    ins = e.matmul(bank(1, g0, g1), ones[:], PTs[:, g0:g1], start=True, stop=True)
                    for bl in range(4):
                        b = 4 * gq + bl
                        for h in range(2):
                            col0 = b * 8 + h * 4
                            ins = e.matmul(bank(2, col0, col0 + 4), vs[:, bl, h * 128:(h + 1) * 128], PTs[:, col0:col0 + 4], start=True, stop=True)
                    return ins
                P.op("pe", pv, reads=[B_pts, B_vs[slot], B_const], writes=[B_ps[1], B_hb[1][0], B_hb[1][1], B_ps[2]])
                if gq + 2 < NS // 4:
                    samp_dma(gq + 2)
            def samp_finish():
                for half in range(2):
                    p0, p1 = half * 64, half * 64 + 64
                    den = bank(1, 0, 128, p0, p1).rearrange("p (b h f c) -> p b h f c", b=NS, h=2, f=2)[:, :, :, half, :]
                    oo = bank(2, 0, 128, p0, p1).rearrange("p (b h f c) -> p b h f c", b=NS, h=2, f=2)[:, :, :, half, :]
                    sk = sinke[p0:p1, l, :].rearrange("p (h c) -> p h c", h=2).unsqueeze(1).to_broadcast([64, NS, 2, 2])
                    t1 = T1[p0:p1, :, :].rearrange("p b (h c) -> p b h c", h=2)
                    P.op("dve", lambda e, den=den, sk=sk, t1=t1: e.tensor_tensor(out=t1, in0=den, in1=sk, op=ALU.add),
                         reads=[B_ps[1], B_hb[1][0], B_hb[1][1], B_const], writes=[B_t1])
                    P.op("dve", lambda e, t1=t1: e.reciprocal(out=t1, in_=t1), reads=[B_t1], writes=[B_t1])
                    ys = YS[p0:p1, 4:8, :].rearrange("p (h c) b -> p b h c", h=2)
                    P.op("dve", lambda e, oo=oo, t1=t1, ys=ys: e.tensor_tensor(out=ys, in0=oo, in1=t1, op=ALU.mult),
                         reads=[B_ps[2], B_t1], writes=[B_samp])
                nsq = [0]
                for grp, (chunks, n) in enumerate([([0, 1], 256.0), ([2, 3], 256.0), ([4, 5, 6, 7], 512.0)]):
                    for i, c in enumerate(chunks):
                        r = nsq[0] % 2
                        nsq[0] += 1
                        sq = SQs[r]
                        P.op("act", lambda e, sq=sq, c=c: e.activation(out=sq[:, :], in_=YS[:, c, :], func=AF.Square), reads=[B_samp], writes=[B_sqs[r]])
                        P.op("pe", lambda e, sq=sq, i=i, grp=grp, chunks=chunks: e.matmul(bank(1, 256 + grp * 16, 256 + grp * 16 + NS), ones[:], sq[:, :],
                                                                                          start=(i == 0), stop=(i == len(chunks) - 1)),
                             reads=[B_sqs[r], B_const], writes=[B_ps[1], B_hb[1][0], B_hb[1][1]])
                    P.op("act", lambda e, grp=grp, n=n: e.activation(out=RSs[:, grp, :], in_=bank(1, 256 + grp * 16, 256 + grp * 16 + NS), func=AF.Ln,
                                                                     scale=1.0 / n, bias=epsT[:, 1:2]),
                         reads=[B_ps[1], B_hb[1][0], B_hb[1][1], B_const], writes=[B_rss])
                    P.op("act", lambda e, grp=grp: e.activation(out=RSs[:, grp, :], in_=RSs[:, grp, :], func=AF.Exp, scale=-0.5), reads=[B_rss], writes=[B_rss])
                for c in range(KC):
                    grp = 0 if c < 2 else (1 if c < 4 else 2)
                    P.op("dve", lambda e, c=c, grp=grp: e.scalar_tensor_tensor(out=HIN[:, c, NP:NP + NS], in0=YS[:, c, :], scalar=mixg[:, l, c:c + 1],
                                                                                in1=RSs[:, grp, :], op0=ALU.mult, op1=ALU.mult),
                         reads=[B_samp, B_rss, B_const], writes=hin_bufs(NP, NS, [c]))

            prevb = [None]

            def blk_done(bi):
                ln_math(bi)
                if prevb[0] is not None:
                    ln_apply(prevb[0], True)
                prevb[0] = bi
            n_pd = 0
            sg_next = [0]

            def samp_some(n):
                for _ in range(n):
                    if sg_next[0] < NS // 4:
                        samp_group(sg_next[0])
                        sg_next[0] += 1
                        if sg_next[0] == NS // 4:
                            samp_finish()
            for bi, (c0, w) in enumerate(FBLK):
                if w == 0:
                    continue
                if c0 >= NP:
                    samp_some(NS // 4)
                for o in range(KC):
                    if o == 4 and c0 < NP:
                        samp_some(1)
                    pd = 4 + (n_pd % 2)
                    n_pd += 1

                    def mm(e, c0=c0, w=w, pd=pd, o=o):
                        ins = None
                        for kc in range(KC):
                            ins = e.matmul(bank(pd, 0, w), WOUT[:, kc, o * 128:(o + 1) * 128], HIN[:, kc, c0:c0 + w], start=(kc == 0),
                                           stop=(kc == KC - 1))
                        return ins
                    P.op("pe", mm, reads=[B_wout[o // 4]] + hin_bufs(c0, w), writes=[B_ps[pd]])
                    resid_evac(bi, o, bank(pd, 0, w), [B_ps[pd]], 1)
                    ln_stats_feed(bi, o, blk_done)
            ln_flush()
            ln_apply(NB - 1, True)

        for l in range(L):
            for (dst, src) in [(o_sc[l, :, 0, :], d_sconv[l, :, 1, :]), (o_spl[l, :, 0:14, :], d_spool[l, :, 1:15, :]),
                               (o_sk[l, :, 0:127, :], d_ck[l, :, 1:128, :]), (o_sv[l, :, 0:127, :], d_cv[l, :, 1:128, :])]:
                P.dma("sp", lambda e, dst=dst, src=src: e.dma_start(out=dst, in_=src), "dd", reads=[B_out])

        cv0 = Carver()
        ada["ring"] = [cv0.take([KC, 128], BF16) for _ in range(2)]
        ada["bufs"] = [abuf("adar0"), abuf("adar1")]
        phase_switch(ada["bufs"])
        ada_load_bias(0)
        for j in range(72):
            ada_item(0, j)
        gates(0)
        M = Mt[0]
        P.op("dve", lambda e: e.tensor_scalar(out=lnc[:, 2, :], in0=M[:, KC:2 * KC, 0], scalar1=1.0, scalar2=None, op0=ALU.add),
             reads=[B_M[0]], writes=[B_lnc])
        P.op("dve", lambda e: e.tensor_scalar(out=lncs[:, 0, :, :], in0=M[:, KC:2 * KC, 1:17], scalar1=1.0, scalar2=None, op0=ALU.add),
             reads=[B_M[0]], writes=[B_lnc])
        for bi, (c0, w) in enumerate(FBLK):
            for c in range(KC):
                xs = X[:, c, c0:c0 + w]
                hs = HIN[:, c, c0:c0 + w]
                hb = hin_bufs(c0, w, [c])
                if c0 < NP:
                    P.op("dve", lambda e, xs=xs, hs=hs, c=c: e.tensor_scalar(out=hs, in0=xs, scalar1=lnc[:, 2, c:c + 1], scalar2=M[:, c, 0:1],
                                                                              op0=ALU.mult, op1=ALU.add),
                         reads=[B_X[c][bi], B_lnc, B_M[0]], writes=hb)
                else:
                    P.op("dve", lambda e, xs=xs, c=c: e.tensor_tensor(out=YS[:, c, :], in0=xs, in1=lncs[:, 0, c, :], op=ALU.mult),
                         reads=[B_X[c][bi], B_lnc], writes=[B_samp])
                    P.op("dve", lambda e, hs=hs, c=c: e.tensor_tensor(out=hs, in0=YS[:, c, :], in1=M[:, c, 1:17], op=ALU.add),
                         reads=[B_samp, B_M[0]], writes=hb)
                P.op("act", lambda e, xs=xs: act_id(e, xs, xs, ALPHA), reads=[B_X[c][bi]] + hb, writes=[B_X[c][bi]])

        for l in range(L):
            last = (l == L - 1)
            if l > 0:
                gates(l)
            cs0 = min(128 * l, HALO)
            cs1 = min(128 * (l + 1), HALO)
            ffn_phase(l, 0, None, 0, (l % 2, 3, 4), cs0)
            mixer_A(l, cs0)
            mixer_B(l, cs1)
            ffn_phase(l, 1, None if last else l + 1, 2, None if last else ((l + 1) % 2, 0, 1), cs1, final=last)

        for c in range(KC):
            P.dma("sp", lambda e, c=c: e.dma_start(out=o_x[:, c, :], in_=X[:, c, HALO:NT]), f"ox{c}",
                  reads=[B_X[c][b] for b in range(NB)] + [B_out])
        P.wait_all("sp", [B_out])
        P.emit()
    return nc


def _shared_weights(inp, L):
    f = lambda a: np.ascontiguousarray(a, dtype=np.float32)
    w = {}
    w["wada"] = f(inp["w_ada"].reshape(L, KC, 128, 72, 128).transpose(0, 3, 2, 1, 4))
    w["bada"] = f(inp["b_ada"].reshape(L, 72, 128).transpose(0, 2, 1))
    for i, (g, u, d) in enumerate([("ffn1_gate", "ffn1_up", "ffn1_down"), ("ffn2_gate", "ffn2_up", "ffn2_down")]):
        gg = inp[g].reshape(L, KC, 128, MC, 128).transpose(0, 3, 2, 1, 4)
        uu = inp[u].reshape(L, KC, 128, MC, 128).transpose(0, 3, 2, 1, 4)
        w[f"wgu{i + 1}"] = f(np.stack([gg, uu], axis=3).reshape(L, MC, 128, 2 * KC * 128))
        w[f"wd{i + 1}"] = f(inp[d].reshape(L, MC, 128, D))
    W = inp["w_in"]
    K0, K1, V0, V1 = W[:, :, 1536:1600], W[:, :, 1600:1664], W[:, :, 1664:1728], W[:, :, 1728:1792]
    ext = np.concatenate([W[:, :, 0:1536], K0, K0, K1, K1, V0, V0, V1, V1], axis=2)
    w["win"] = f(ext.reshape(L, KC, 128, 4, 512).transpose(0, 3, 2, 1, 4))
    w["wout"] = f(inp["w_out"].reshape(L, KC, 128, 2, 512).transpose(0, 3, 2, 1, 4))
    w["lng"] = f(inp["ln_g"].reshape(L, 3, KC, 128).transpose(3, 0, 1, 2))
    w["lnb"] = f(inp["ln_b"].reshape(L, 3, KC, 128).transpose(3, 0, 1, 2))
    w["convw"] = f(inp["conv_w"].reshape(L, 3, 2, 128).transpose(3, 0, 1, 2))
    pw = np.zeros((L, 2, 128, 128), np.float32)
    for cc in range(2):
        pw[:, cc, 0:64, 0:64] = inp["pool_w"][:, 2 * cc]
        pw[:, cc, 64:128, 64:128] = inp["pool_w"][:, 2 * cc + 1]
    w["poolw"] = pw
    w["pscale"] = f(inp["pool_scale"].reshape(L, 2, 128).transpose(2, 0, 1))
    w["mixg"] = f(inp["mix_norm_g"].reshape(L, KC, 128).transpose(2, 0, 1))
    sk = np.zeros((128, L, 4), np.float32)
    for c in range(4):
        sk[0:64, :, c] = inp["attn_sinks"][:, 2 * c][None, :]
        sk[64:128, :, c] = inp["attn_sinks"][:, 2 * c + 1][None, :]
    w["sinks"] = sk
    iw = np.zeros((128, 2), np.float32)
    for cc in range(2):
        iw[0:64, cc] = 1.0 / (2 ** (2 * cc + 1))
        iw[64:128, cc] = 1.0 / (2 ** (2 * cc + 2))
    w["invw"] = iw
    w["ident"] = np.eye(16, dtype=np.float32)
    w["identb"] = np.eye(128, dtype=np.float32)
    w["sinks4"] = f(np.broadcast_to(inp["attn_sinks"][None, :, :], (128, L, 8)))
    return w


def _core_inputs(cfg, inp, core, shared):
    L, OWN, HALO, NP, NT = cfg.L, cfg.OWN, cfg.HALO, cfg.NP, cfg.NT
    f = lambda a: np.ascontiguousarray(a, dtype=np.float32)
    nseg = inp["x_prompt"].shape[1] // OWN
    b, seg = core // nseg, core % nseg
    s0 = seg * OWN
    idx = np.arange(s0 - HALO, s0 + OWN)
    ok = idx >= 0
    xall = np.zeros((NT, D), np.float32)
    xall[:NP][ok] = inp["x_prompt"][b, idx[ok]]
    sl = slice(NS * core, NS * core + NS)
    xall[NP:] = inp["x_sample"][sl, 0, :]
    m = dict(shared)
    m["xT"] = f(xall.T.reshape(KC, 128, NT).transpose(1, 0, 2))
    call = np.concatenate([inp["c_prompt"][b:b + 1], inp["c_sample"][sl]], axis=0)
    m["cT"] = f(call.T.reshape(KC, 128, 17).transpose(1, 0, 2))
    ic = np.zeros((128, 2, 16), np.float32)
    for cc in range(2):
        for half in range(2):
            wd_ = 2 ** (2 * cc + half + 1)
            pos = np.arange(16)
            cnt = np.minimum(pos + 1, wd_) if seg == 0 else np.full(16, wd_)
            ic[half * 64:half * 64 + 64, cc, :] = (1.0 / cnt)[None, :]
    m["invcnt"] = ic
    m["valid"] = np.full((128, 1), 0.0 if seg == 0 else 1.0, np.float32)
    i = np.arange(128)[:, None]
    t = np.arange(128)[None, :]
    mk = np.zeros((128, 3, 128), np.float32)
    NEG = -30000.0
    mk[:, 0, :] = np.where(i > t, 0.0, NEG)
    mk[:, 1, :] = np.where(i <= t, 0.0, NEG)
    mk[:, 2, :] = np.where(i > t, 0.0, NEG) if seg != 0 else NEG
    m["masks"] = mk
    cr = np.zeros((128, 2, 16), np.float32)
    for cc in range(2):
        for half in range(2):
            wd_ = 2 ** (2 * cc + half + 1)
            pos = np.arange(16)
            cnt = np.minimum(pos + 1, wd_) if seg == 0 else np.full(16, wd_)
            cr[half * 64:half * 64 + 64, cc, :] = (wd_ / cnt)[None, :]
    m["cntr"] = cr
    ck = inp["cache_k_win"][:, sl]
    cvv = inp["cache_v_win"][:, sl]
    kt = ck.transpose(0, 1, 3, 4, 2)
    kz = np.zeros((L, NS, 128, 2, 2, 128), np.float32)
    for h in range(2):
        kz[:, :, 0:64, h, 0, :] = kt[:, :, h]
        kz[:, :, 64:128, h, 1, :] = kt[:, :, h]
    m["kTz"] = kz.reshape(L, NS, 128, 512)
    m["vd"] = f(np.concatenate([cvv, cvv], axis=-1).reshape(L, NS, 128, 256))
    m["ck"] = f(ck.reshape(L, NS, 128, 128))
    m["cv"] = f(cvv.reshape(L, NS, 128, 128))
    sc = inp["state_conv"][:, sl]
    m["sconvT"] = f(sc.reshape(L, NS, 2, 2, 128).transpose(0, 4, 3, 2, 1))
    sp = inp["state_pool"][:, sl]
    spt = np.zeros((L, 128, 2, NS, 16), np.float32)
    spt[..., 0:15] = sp.reshape(L, NS, 15, 2, 128).transpose(0, 4, 3, 1, 2)
    m["spoolT"] = spt
    m["sconv"] = f(sc)
    m["spool"] = f(sp)
    return m


_PROG_CACHE = {}


def kernel(**inputs):
    inp = {k: np.asarray(v) for k, v in inputs.items()}
    L = inp["w_ada"].shape[0]
    B, SEQ = inp["x_prompt"].shape[:2]
    ncores = 8
    nseg = ncores // B
    OWN = SEQ // nseg
    HALO = max(TB, ((128 * L + TB - 1) // TB) * TB)
    cfg = Cfg(L, OWN, HALO)
    key = (L, OWN, HALO)
    if key not in _PROG_CACHE:
        _PROG_CACHE[key] = build_program(cfg)
    nc = _PROG_CACHE[key]
    shared = _shared_weights(inp, L)
    in_maps = [_core_inputs(cfg, inp, c, shared) for c in range(ncores)]
    res = run_bass_kernel_spmd(nc, in_maps, core_ids=list(range(ncores)))
    R = res.results
    NSB = inp["x_sample"].shape[0]
    y_prompt = np.zeros((B, SEQ, D), np.float32)
    y_sample = np.zeros((NSB, 1, D), np.float32)
    p_conv = np.zeros((L, B, 2, 256), np.float32)
    p_pool = np.zeros((L, B, 15, 256), np.float32)
    p_k = np.zeros((L, B, 128, 2, 64), np.float32)
    p_v = np.zeros((L, B, 128, 2, 64), np.float32)
    s_conv = np.zeros((L, NSB, 2, 256), np.float32)
    s_pool = np.zeros((L, NSB, 15, 256), np.float32)
    s_k = np.zeros((L, NSB, 128, 2, 64), np.float32)
    s_v = np.zeros((L, NSB, 128, 2, 64), np.float32)
    for c in range(ncores):
        b, seg = c // nseg, c % nseg
        r = R[c]
        yt = np.asarray(r["o_x"]).transpose(2, 1, 0).reshape(OWN + NS, D)
        y_prompt[b, seg * OWN:(seg + 1) * OWN] = yt[:OWN]
        sl = slice(NS * c, NS * c + NS)
        y_sample[sl, 0] = yt[OWN:]
        if seg == nseg - 1:
            p_conv[:, b] = r["o_pc"]
            p_pool[:, b] = r["o_pp"]
            p_k[:, b] = np.asarray(r["o_pk"]).reshape(L, 128, 2, 64)
            p_v[:, b] = np.asarray(r["o_pv"]).reshape(L, 128, 2, 64)
        s_conv[:, sl] = r["o_sc"]
        s_pool[:, sl] = r["o_spl"]
        s_k[:, sl] = np.asarray(r["o_sk"]).reshape(L, NS, 128, 2, 64)
        s_v[:, sl] = np.asarray(r["o_sv"]).reshape(L, NS, 128, 2, 64)
    return (y_prompt, y_sample, p_conv, p_pool, p_k, p_v, s_conv, s_pool, s_k, s_v)
```

```python
import contextlib
import numpy as np
import concourse.bass as bass
import concourse.mybir as mybir
from concourse.bass_utils import run_bass_kernel_spmd

F32 = mybir.dt.float32
BF16 = mybir.dt.bfloat16
AF = mybir.ActivationFunctionType
ALU = mybir.AluOpType

QUEUES = ("pe", "act", "dve", "pool", "sp")


class Buf:
    __slots__ = ("name", "last_w", "reads")

    def __init__(self, name):
        self.name = name
        self.last_w = None
        self.reads = []


class Op:
    __slots__ = ("fn", "waits", "event", "inc")

    def __init__(self, fn, waits, event, inc):
        self.fn, self.waits, self.event, self.inc = fn, waits, event, inc


class Prog:
    def __init__(self, nc):
        self.nc = nc
        self.ops = {q: [] for q in QUEUES}
        self.count = {}
        self.known = {q: {} for q in QUEUES}
        self.dma_sems = []

    def _deps(self, queue, reads, writes):
        need = {}

        def add(ev):
            if ev is None:
                return
            k, v = ev
            if queue == "pe" and k == "q_pe":
                return
            if need.get(k, 0) < v:
                need[k] = v

        for b in reads:
            add(b.last_w)
        for b in writes:
            add(b.last_w)
            for ev in b.reads:
                add(ev)
        waits = []
        kn = self.known[queue]
        for k, v in need.items():
            if kn.get(k, 0) >= v:
                continue
            kn[k] = v
            waits.append((k, v))
        return waits

    @staticmethod
    def _commit(ev, reads, writes):
        for b in reads:
            b.reads.append(ev)
        for b in writes:
            b.last_w = ev
            b.reads = []

    def op(self, queue, fn, reads=(), writes=()):
        waits = self._deps(queue, reads, writes)
        k = "q_" + queue
        v = self.count.get(k, 0) + 1
        self.count[k] = v
        ev = (k, v)
        self.ops[queue].append(Op(fn, waits, ev, 1))
        self._commit(ev, reads, writes)
        return ev

    def dma(self, queue, fn, semkey, reads=(), writes=()):
        waits = self._deps(queue, reads, writes)
        if semkey not in self.count:
            self.dma_sems.append(semkey)
        v = self.count.get(semkey, 0) + 16
        self.count[semkey] = v
        ev = (semkey, v)
        self.ops[queue].append(Op(fn, waits, ev, 16))
        self._commit(ev, reads, writes)
        return ev

    def wait_all(self, queue, bufs):
        waits = self._deps(queue, (), bufs)
        self.ops[queue].append(Op(None, waits, None, 0))

    def emit(self):
        nc = self.nc
        keys = ["q_" + q for q in QUEUES if ("q_" + q) in self.count] + self.dma_sems
        with contextlib.ExitStack() as st:
            sems = {k: st.enter_context(nc.semaphore("s_" + k)) for k in keys}
            block = st.enter_context(nc.Block())

            def run(queue):
                def body(eng):
                    for o in self.ops[queue]:
                        for (k, v) in o.waits:
                            eng.wait_ge(sems[k], v)
                        if o.fn is None:
                            continue
                        ins = o.fn(eng)
                        if o.event is not None:
                            ins.then_inc(sems[o.event[0]], o.inc)
                return body

            if self.ops["sp"]:
                block.sync(run("sp"))
            if self.ops["pe"]:
                block.tensor(run("pe"))
            if self.ops["act"]:
                block.scalar(run("act"))
            if self.ops["dve"]:
                block.vector(run("dve"))
            if self.ops["pool"]:
                block.gpsimd(run("pool"))


D = 1024
KC = 8
DFF = 2816
MC = 22
NS = 16
TB = 256
FB = 512
WIN = 128
NMOD = 9
LN_EPS = 1e-5
RMS_EPS = 1e-6
WEXT = 2048
GROUPS = (4, 4, 4, 5, 5)
GMAX = 5


class Cfg:
    def __init__(self, L, OWN, HALO, depth_full=4):
        self.L, self.OWN, self.HALO = L, OWN, HALO
        self.NP = HALO + OWN
        self.NT = self.NP + NS
        assert self.NP % FB == 0 and HALO % TB == 0
        self.NFB = self.NP // FB
        self.NTB = self.NP // TB
        self.ALPHA = float((2.0 * depth_full) ** 0.25)


ARENA = 67584


def build_program(cfg):
    L, NP, NT, HALO, OWN = cfg.L, cfg.NP, cfg.NT, cfg.HALO, cfg.OWN
    NFB, NTB, ALPHA = cfg.NFB, cfg.NTB, cfg.ALPHA
    nc = bass.Bass("TRN2", target_bir_lowering=False)
    P = Prog(nc)

    def din(name, shape):
        return nc.dram_tensor(name, list(shape), F32, kind="ExternalInput").ap()

    def dout(name, shape):
        return nc.dram_tensor(name, list(shape), F32, kind="ExternalOutput").ap()

    d_xT = din("xT", [128, KC, NT])
    d_cT = din("cT", [128, KC, 17])
    d_wada = din("wada", [L, 72, 128, KC, 128])
    d_bada = din("bada", [L, 128, 72])
    d_wgu = [din("wgu1", [L, MC, 128, 2 * KC * 128]), din("wgu2", [L, MC, 128, 2 * KC * 128])]
    d_wd = [din("wd1", [L, MC, 128, D]), din("wd2", [L, MC, 128, D])]
    d_win = din("win", [L, 4, 128, KC, 512])
    d_wout = din("wout", [L, 2, 128, KC, 512])
    d_lng = din("lng", [128, L, 3, KC])
    d_lnb = din("lnb", [128, L, 3, KC])
    d_convw = din("convw", [128, L, 3, 2])
    d_poolw = din("poolw", [L, 2, 128, 128])
    d_pscale = din("pscale", [128, L, 2])
    d_mixg = din("mixg", [128, L, KC])
    d_sinks = din("sinks", [128, L, 4])
    d_invcnt = din("invcnt", [128, 2, 16])
    d_invw = din("invw", [128, 2])
    d_valid = din("valid", [128, 1])
    d_masks = din("masks", [128, 3, 128])
    d_ident = din("ident", [16, 16])
    d_identb = din("identb", [128, 128])
    d_sinks4 = din("sinks4", [128, L, 8])
    d_cntr = din("cntr", [128, 2, 16])
    d_kTz = din("kTz", [L, NS, 128, 512])
    d_vd = din("vd", [L, NS, 128, 256])
    d_ck = din("ck", [L, NS, 128, 128])
    d_cv = din("cv", [L, NS, 128, 128])
    d_sconvT = din("sconvT", [L, 128, 2, 2, NS])
    d_spoolT = din("spoolT", [L, 128, 2, NS, 16])
    d_sconv = din("sconv", [L, NS, 2, 256])
    d_spool = din("spool", [L, NS, 15, 256])

    o_x = dout("o_x", [128, KC, OWN + NS])
    o_pc = dout("o_pc", [L, 2, 256])
    o_pp = dout("o_pp", [L, 15, 256])
    o_pk = dout("o_pk", [L, 128, 128])
    o_pv = dout("o_pv", [L, 128, 128])
    o_sc = dout("o_sc", [L, NS, 2, 256])
    o_spl = dout("o_spl", [L, NS, 15, 256])
    o_sk = dout("o_sk", [L, NS, 128, 128])
    o_sv = dout("o_sv", [L, NS, 128, 128])
    B_out = Buf("outputs")
    n_out = [0]

    def out_dma(dst, src, reads):
        k = f"od{n_out[0] % 8}"
        n_out[0] += 1
        P.dma("sp", lambda e: e.dma_start(out=dst, in_=src), k, reads=reads + [B_out], writes=[])
        return k

    def act_id(e, out, in_, scale=1.0, bias=0.0):
        return e.activation(out=out, in_=in_, func=AF.Prelu, scale=scale, bias=bias, alpha=1.0)

    st = contextlib.ExitStack()
    with st:
        def sb(name, shape, dt=F32):
            return st.enter_context(nc.sbuf_tensor("s_" + name, list(shape), dt))

        X = sb("X", [128, KC, NT])
        HIN = sb("HIN", [128, KC, NT], BF16)
        Mt = [sb("M0", [128, 72, 17]), sb("M1", [128, 72, 17])]
        GT = sb("GT", [128, 3, KC, 17])
        SC_ = sb("siluc", [128, KC, 17], BF16)
        cTs = sb("cTs", [128, KC, 17])
        bada = sb("bada", [128, 72])
        lng = sb("lng", [128, L, 3, KC])
        lnb = sb("lnb", [128, L, 3, KC])
        convw = sb("convw", [128, L, 3, 2])
        pscale = sb("pscale", [128, L, 2])
        mixg = sb("mixg", [128, L, KC])
        sinke = sb("sinke", [128, L, 4])
        invw = sb("invw", [128, 2])
        valid = sb("valid", [128, 1])
        masks = sb("masks", [128, 3, 128], BF16)
        ident = sb("ident", [16, 16])
        ones = sb("ones", [128, 128], BF16)
        identb = sb("identb", [128, 128], BF16)
        sinke4 = sb("sinke4", [128, L, 8])
        cntr = sb("cntr", [128, 2, 16])
        poolw = sb("poolw", [128, L, 2, 128], BF16)
        lnc = sb("lnc", [128, 4, KC])
        lncs = sb("lncs", [128, 2, KC, NS])
        epsT = sb("epsT", [128, 2])
        VtokS = sb("VtokS", [16, 256])
        QS = sb("QS", [128, 4, NS], BF16)
        KZS = sb("KZS", [128, 4, NS], BF16)
        YS = sb("YS", [128, KC, NS])
        arena = sb("arena", [128, ARENA], mybir.dt.uint8)

        class Carver:
            def __init__(self, off=0):
                self.off = off

            def take(self, shape, dt):
                esz = 2 if dt == BF16 else 4
                n = int(np.prod(shape))
                self.off = (self.off + 31) // 32 * 32
                a = arena[:, self.off:self.off + n * esz].bitcast(dt)
                self.off += n * esz
                assert self.off <= ARENA, (self.off, ARENA)
                if len(shape) == 1:
                    return a
                names = " ".join(f"d{i}" for i in range(len(shape)))
                kw = {f"d{i}": int(s) for i, s in enumerate(shape)}
                return a.rearrange(f"p ({names}) -> p {names}", **kw)

        psum = st.enter_context(nc.psum_tensor("psum", [128, 8 * 512], F32))

        def bank(i, lo=0, hi=512, p0=0, p1=128):
            return psum[p0:p1, i * 512 + lo:i * 512 + hi]

        B_ps = [Buf(f"ps{i}") for i in range(8)]
        B_hb = [[B_ps[i], B_ps[i]] for i in range(2)]

        FBLK = [(i * FB, FB) for i in range(NFB)] + [(NP, NS)]
        NB = len(FBLK)

        def set_start(cs):
            for i in range(NFB):
                lo = max(i * FB, cs)
                FBLK[i] = (lo, max(0, (i + 1) * FB - lo))
        B_X = [[Buf(f"X{c}_{b}") for b in range(NB)] for c in range(KC)]
        B_H = [[Buf(f"H{c}_{t}") for t in range(NTB + 1)] for c in range(KC)]

        def hin_bufs(c0, w, chunks=range(KC)):
            if c0 >= NP:
                ts = [NTB]
            else:
                ts = list(range(c0 // TB, (c0 + w - 1) // TB + 1))
            return [B_H[c][t] for c in chunks for t in ts]

        B_M = [Buf("M0"), Buf("M1")]
        B_GT = Buf("GT")
        B_const = Buf("const")
        B_lnc = Buf("lnc")
        B_samp = Buf("samp_persist")
        arena_bufs = []

        def abuf(name):
            return Buf(name)

        def phase_switch(new_bufs, keep=()):
            mx = {}
            for b in arena_bufs:
                if b in keep:
                    continue
                evs = list(b.reads)
                if b.last_w is not None:
                    evs.append(b.last_w)
                for k, v in evs:
                    if mx.get(k, 0) < v:
                        mx[k] = v
            evl = list(mx.items())
            for b in new_bufs:
                b.last_w = None
                b.reads = list(evl)
            del arena_bufs[:]
            arena_bufs.extend(list(keep) + list(new_bufs))

        def ld(dst, src, key, queue="sp"):
            P.dma(queue, lambda e: e.dma_start(out=dst, in_=src), key, writes=[B_const])

        for (dst, src, key) in [(cTs[:], d_cT, "c0"), (lng[:], d_lng, "c2"),
                                (lnb[:], d_lnb, "c3"), (convw[:], d_convw, "c4"), (pscale[:], d_pscale, "c5"),
                                (mixg[:], d_mixg, "c6"), (sinke[:], d_sinks, "c7"),
                                (invw[:], d_invw, "c9"), (valid[:], d_valid, "c10"), (ident[:], d_ident, "c11")]:
            ld(dst, src, key)
        ld(masks[:], d_masks, "c12", "pool")
        ld(identb[:], d_identb, "c14", "pool")
        ld(sinke4[:], d_sinks4, "c15")
        ld(cntr[:], d_cntr, "c16")
        for l in range(L):
            ld(poolw[:, l, :, :], d_poolw[l].rearrange("c p n -> p c n"), f"c13_{l}", "pool")
        for c in range(KC):
            P.dma("sp", lambda e, c=c: e.dma_start(out=X[:, c, :], in_=d_xT[:, c, :]), f"xl{c}",
                  writes=[B_X[c][b] for b in range(NB)])
        P.op("dve", lambda e: e.memset(ones[:], 1.0), writes=[B_const])
        P.op("dve", lambda e: e.memset(epsT[:, 0:1], LN_EPS), writes=[B_const])
        P.op("dve", lambda e: e.memset(epsT[:, 1:2], RMS_EPS), writes=[B_const])
        P.op("dve", lambda e: e.memset(KZS[:], 0.0), writes=[B_samp])
        P.op("act", lambda e: e.activation(out=SC_[:], in_=cTs[:], func=AF.Silu), reads=[B_const], writes=[B_const])
        P.op("act", lambda e: e.activation(out=sinke[:], in_=sinke[:], func=AF.Exp), reads=[B_const], writes=[B_const])
        P.op("act", lambda e: e.activation(out=sinke4[:], in_=sinke4[:], func=AF.Exp), reads=[B_const], writes=[B_const])

        ada = {"ring": None, "bufs": None, "n": 0}
        B_bada = Buf("bada")

        def ada_load_bias(l):
            P.dma("sp", lambda e: e.dma_start(out=bada[:], in_=d_bada[l]), "c1", writes=[B_bada])

        def ada_item(l, j):
            slot = ada["n"] % ada.get("nslot", 2)
            ada["n"] += 1
            tile = ada["ring"][slot]
            bslot = ada["bufs"][slot]
            P.dma("pool", lambda e: e.dma_start(out=tile, in_=d_wada[l, j]), f"ada{slot}", writes=[bslot])

            def mm(e):
                ins = None
                for kc in range(KC):
                    ins = e.matmul(bank(7, 0, 17), tile[:, kc, :], SC_[:, kc, :], start=(kc == 0), stop=(kc == KC - 1))
                return ins
            P.op("pe", mm, reads=[bslot, B_const], writes=[B_ps[7]])
            P.op("dve", lambda e: e.tensor_scalar(out=Mt[l % 2][:, j, :], in0=bank(7, 0, 17), scalar1=bada[:, j:j + 1],
                                                   scalar2=None, op0=ALU.add),
                 reads=[B_ps[7], B_bada], writes=[B_M[l % 2]])

        def gates(l, i):
            M = Mt[l % 2]
            row, f = [(2, 0.5), (5, 1.0), (8, 0.5)][i]
            P.op("dve", lambda e: e.tensor_scalar(out=GT[:, i, :, :], in0=M[:, row * KC:(row + 1) * KC, :],
                                                   scalar1=f, scalar2=None, op0=ALU.mult),
                 reads=[B_M[l % 2]], writes=[B_GT])

        def ln_consts(l, k, nxt, final=False):
            a = 1.0 if final else ALPHA
            rd = [B_const]
            P.op("dve", lambda e: e.tensor_scalar(out=lnc[:, 0, :], in0=lng[:, l, k, :], scalar1=a, scalar2=None, op0=ALU.mult),
                 reads=rd, writes=[B_lnc])
            P.op("dve", lambda e: e.tensor_scalar(out=lnc[:, 1, :], in0=lnb[:, l, k, :], scalar1=a, scalar2=None, op0=ALU.mult),
                 reads=rd, writes=[B_lnc])
            if nxt is None:
                return
            mi, shr, scr = nxt
            M = Mt[mi]
            rd = [B_const, B_M[mi]]
            P.op("dve", lambda e: e.scalar_tensor_tensor(out=lnc[:, 2, :], in0=M[:, scr * KC:(scr + 1) * KC, 0], scalar=1.0,
                                                          in1=lng[:, l, k, :], op0=ALU.add, op1=ALU.mult),
                 reads=rd, writes=[B_lnc])
            P.op("dve", lambda e: e.scalar_tensor_tensor(out=lnc[:, 3, :], in0=M[:, scr * KC:(scr + 1) * KC, 0], scalar=1.0,
                                                          in1=lnb[:, l, k, :], op0=ALU.add, op1=ALU.mult),
                 reads=rd, writes=[B_lnc])
            P.op("dve", lambda e: e.tensor_tensor(out=lnc[:, 3, :], in0=lnc[:, 3, :], in1=M[:, shr * KC:(shr + 1) * KC, 0], op=ALU.add),
                 reads=rd + [B_lnc], writes=[B_lnc])
            gb_ = lng[:, l, k, :].unsqueeze(2).to_broadcast([128, KC, NS])
            bb_ = lnb[:, l, k, :].unsqueeze(2).to_broadcast([128, KC, NS])
            P.op("dve", lambda e: e.scalar_tensor_tensor(out=lncs[:, 0, :, :], in0=M[:, scr * KC:(scr + 1) * KC, 1:17], scalar=1.0,
                                                          in1=gb_, op0=ALU.add, op1=ALU.mult),
                 reads=rd, writes=[B_lnc])
            P.op("dve", lambda e: e.scalar_tensor_tensor(out=lncs[:, 1, :, :], in0=M[:, scr * KC:(scr + 1) * KC, 1:17], scalar=1.0,
                                                          in1=bb_, op0=ALU.add, op1=ALU.mult),
                 reads=rd, writes=[B_lnc])
            P.op("dve", lambda e: e.tensor_tensor(out=lncs[:, 1, :, :], in0=lncs[:, 1, :, :], in1=M[:, shr * KC:(shr + 1) * KC, 1:17], op=ALU.add),
                 reads=rd + [B_lnc], writes=[B_lnc])

        lnS = {}

        NRING = 3

        def carve_ln(cv):
            lnS.clear()
            lnS["n"] = 0
            lnS["pend"] = []
            lnS["vb"] = [cv.take([FB], BF16) for _ in range(NRING)]
            lnS["vq"] = [cv.take([FB], BF16) for _ in range(NRING)]
            lnS["Bvb"] = [abuf(f"vb{i}") for i in range(NRING)]
            lnS["Bvq"] = [abuf(f"vq{i}") for i in range(NRING)]
            lnS["rstd"] = [cv.take([FB], F32) for _ in range(2)]
            lnS["nmr"] = [cv.take([FB], F32) for _ in range(2)]
            lnS["Brs"] = [abuf("rstd0"), abuf("rstd1")]
            lnS["Bnm"] = [abuf("nmr0"), abuf("nmr1")]
            lnS["sscr"] = cv.take([NS], F32)
            lnS["Bsscr"] = abuf("sscr")
            return lnS["Bvb"] + lnS["Bvq"] + lnS["Brs"] + lnS["Bnm"] + [lnS["Bsscr"]]

        def ln_stats_feed(bi, c, on_block_done):
            c0, w = FBLK[bi]
            r = lnS["n"] % NRING
            lnS["n"] += 1
            vb, vq = lnS["vb"][r], lnS["vq"][r]
            Bvb, Bvq = lnS["Bvb"][r], lnS["Bvq"][r]
            P.op("act", lambda e: act_id(e, out=vb[:, :w], in_=X[:, c, c0:c0 + w]), reads=[B_X[c][bi]], writes=[Bvb])
            P.op("act", lambda e: e.activation(out=vq[:, :w], in_=X[:, c, c0:c0 + w], func=AF.Square), reads=[B_X[c][bi]], writes=[Bvq])

            def pe_part():
                def mm(e):
                    e.matmul(bank(6, 0, w), ones[:], vb[:, :w], start=(c == 0), stop=(c == KC - 1))
                    return e.matmul(bank(7, 0, w), ones[:], vq[:, :w], start=(c == 0), stop=(c == KC - 1))
                P.op("pe", mm, reads=[Bvb, Bvq, B_const], writes=[B_ps[6], B_ps[7]])
                if c == KC - 1:
                    on_block_done(bi)
            lnS["pend"].append(pe_part)
            if len(lnS["pend"]) > 2:
                lnS["pend"].pop(0)()

        def ln_flush():
            while lnS["pend"]:
                lnS["pend"].pop(0)()

        def ln_math(bi):
            c0, w = FBLK[bi]
            q = bi % 2
            rstd, nmr = lnS["rstd"][q], lnS["nmr"][q]
            Brs, Bnm = lnS["Brs"][q], lnS["Bnm"][q]
            inv = 1.0 / D
            P.op("act", lambda e: e.activation(out=rstd[:, :w], in_=bank(6, 0, w), func=AF.Square, scale=inv), reads=[B_ps[6]], writes=[Brs])
            P.op("dve", lambda e: e.scalar_tensor_tensor(out=rstd[:, :w], in0=bank(7, 0, w), scalar=inv, in1=rstd[:, :w],
                                                          op0=ALU.mult, op1=ALU.subtract),
                 reads=[B_ps[7], Brs], writes=[Brs])
            P.op("act", lambda e: e.activation(out=rstd[:, :w], in_=rstd[:, :w], func=AF.Ln, bias=epsT[:, 0:1]), reads=[Brs, B_const], writes=[Brs])
            P.op("act", lambda e: e.activation(out=rstd[:, :w], in_=rstd[:, :w], func=AF.Exp, scale=-0.5), reads=[Brs], writes=[Brs])
            P.op("dve", lambda e: e.scalar_tensor_tensor(out=nmr[:, :w], in0=bank(6, 0, w), scalar=-inv, in1=rstd[:, :w],
                                                          op0=ALU.mult, op1=ALU.mult),
                 reads=[B_ps[6], Brs], writes=[Bnm])

        def ln_apply(bi, with_hin):
            c0, w = FBLK[bi]
            samp = (c0 >= NP)
            q = bi % 2
            rstd, nmr = lnS["rstd"][q], lnS["nmr"][q]
            Brs, Bnm = lnS["Brs"][q], lnS["Bnm"][q]
            tmp, Btmp = lnS["sscr"], lnS["Bsscr"]
            for c in range(KC):
                xs = X[:, c, c0:c0 + w]
                Bx = B_X[c][bi]
                P.op("dve", lambda e, xs=xs: e.tensor_tensor(out=xs, in0=xs, in1=rstd[:, :w], op=ALU.mult), reads=[Bx, Brs], writes=[Bx])
                P.op("dve", lambda e, xs=xs: e.tensor_tensor(out=xs, in0=xs, in1=nmr[:, :w], op=ALU.add), reads=[Bx, Bnm], writes=[Bx])
                if with_hin:
                    hs = HIN[:, c, c0:c0 + w]
                    hb = hin_bufs(c0, w, [c])
                    if not samp:
                        P.op("pool", lambda e, xs=xs, hs=hs, c=c: e.tensor_scalar(out=hs, in0=xs, scalar1=lnc[:, 2, c:c + 1],
                                                                                   scalar2=lnc[:, 3, c:c + 1], op0=ALU.mult, op1=ALU.add),
                             reads=[Bx, B_lnc], writes=hb)
                    else:
                        P.op("dve", lambda e, xs=xs, c=c: e.tensor_tensor(out=tmp[:, :w], in0=xs, in1=lncs[:, 0, c, :], op=ALU.mult),
                             reads=[Bx, B_lnc], writes=[Btmp])
                        P.op("dve", lambda e, hs=hs, c=c: e.tensor_tensor(out=hs, in0=tmp[:, :w], in1=lncs[:, 1, c, :], op=ALU.add),
                             reads=[Btmp, B_lnc], writes=hb)
                P.op("act", lambda e, xs=xs, c=c: act_id(e, xs, xs, lnc[:, 0, c:c + 1], lnc[:, 1, c:c + 1]),
                     reads=[Bx, B_lnc], writes=[Bx])

        def resid_evac(bi, o, ps_ap, ps_bufs, gi):
            c0, w = FBLK[bi]
            xs = X[:, o, c0:c0 + w]
            if c0 < NP:
                P.op("dve", lambda e: e.scalar_tensor_tensor(out=xs, in0=ps_ap, scalar=GT[:, gi, o, 0:1], in1=xs,
                                                              op0=ALU.mult, op1=ALU.add),
                     reads=ps_bufs + [B_GT, B_X[o][bi]], writes=[B_X[o][bi]])
            else:
                t = lnS["sscr"]
                Bt = lnS["Bsscr"]
                P.op("dve", lambda e: e.tensor_tensor(out=t[:, :w], in0=ps_ap, in1=GT[:, gi, o, 1:17], op=ALU.mult),
                     reads=ps_bufs + [B_GT], writes=[Bt])
                P.op("dve", lambda e: e.tensor_tensor(out=xs, in0=t[:, :w], in1=xs, op=ALU.add),
                     reads=[Bt, B_X[o][bi]], writes=[B_X[o][bi]])

        def ffn_phase(l, which, ada_next, ln_k, nxt, cs, final=False, ada_list=None):
            set_start(cs)
            cv = Carver()
            HID = cv.take([GMAX, NT], BF16)
            GU = [cv.take([2 * KC * 128], BF16) for _ in range(2)]
            DW = cv.take([GMAX, D], BF16)
            ada["ring"] = [cv.take([KC, 128], BF16) for _ in range(2)]
            SG = cv.take([FB], F32)
            lnb_ = carve_ln(cv)
            B_hid = [[abuf(f"hid{m}_{b}") for b in range(NB)] for m in range(GMAX)]
            B_gu = [abuf(f"gu{i}") for i in range(2)]
            B_dw = [abuf(f"dw{i}") for i in range(GMAX)]
            B_sg = abuf("sg")
            ada["bufs"] = [abuf("adar0"), abuf("adar1")]
            phase_switch([b for row in B_hid for b in row] + B_gu + B_dw + [B_sg] + ada["bufs"] + lnb_)

            gi = 0 if which == 0 else 2
            ada_js = (list(range(72)) if ada_list is None else list(ada_list)) if ada_next is not None else []
            ada_total = len(ada_js)
            if ada_next is not None and ada_list is None:
                ada_load_bias(ada_next)
            gates(l, gi)
            n_gu = 0
            n_ps = 0
            n_pd = 0
            m0 = 0
            ada_slot = [0]

            prevb = [None]

            def blk_done(bi):
                ln_math(bi)
                if prevb[0] is not None:
                    ln_apply(prevb[0], nxt is not None)
                prevb[0] = bi
            for g, gsz in enumerate(GROUPS):
                last = (g == len(GROUPS) - 1)
                for mi in range(gsz):
                    m = m0 + mi
                    slot = n_gu % 2
                    n_gu += 1
                    gut = GU[slot]
                    P.dma("pool", lambda e, gut=gut, m=m: e.dma_start(out=gut, in_=d_wgu[which][l, m]), f"gu{slot}", writes=[B_gu[slot]])
                    for bi, (c0, w) in enumerate(FBLK):
                        if w == 0:
                            continue
                        pg, pu = (0, 1) if n_ps % 2 == 0 else (2, 3)
                        n_ps += 1

                        def mm(e, gut=gut, c0=c0, w=w, pg=pg, pu=pu):
                            ins = None
                            for kc in range(KC):
                                ins = e.matmul(bank(pg, 0, w), gut[:, kc * 128:(kc + 1) * 128], HIN[:, kc, c0:c0 + w],
                                               start=(kc == 0), stop=(kc == KC - 1))
                            for kc in range(KC):
                                ins = e.matmul(bank(pu, 0, w), gut[:, (KC + kc) * 128:(KC + kc + 1) * 128], HIN[:, kc, c0:c0 + w],
                                               start=(kc == 0), stop=(kc == KC - 1))
                            return ins
                        P.op("pe", mm, reads=[B_gu[slot]] + hin_bufs(c0, w), writes=[B_ps[pg], B_ps[pu]])
                        P.op("act", lambda e, pg=pg, w=w: e.activation(out=SG[:, :w], in_=bank(pg, 0, w), func=AF.Silu),
                             reads=[B_ps[pg]], writes=[B_sg])
                        P.op("dve", lambda e, pu=pu, w=w, mi=mi, c0=c0: e.tensor_tensor(out=HID[:, mi, c0:c0 + w], in0=SG[:, :w],
                                                                                        in1=bank(pu, 0, w), op=ALU.mult),
                             reads=[B_sg, B_ps[pu]], writes=[B_hid[mi][bi]])
                        ada_slot[0] += 1
                        if ada_js and (ada_total - len(ada_js)) < (ada_slot[0] * ada_total) // 100:
                            ada_item(ada_next, ada_js.pop(0))
                if last:
                    while ada_js:
                        ada_item(ada_next, ada_js.pop(0))
                    ln_consts(l, ln_k, nxt, final)
                for mi in range(gsz):
                    m = m0 + mi
                    P.dma("pool", lambda e, mi=mi, m=m: e.dma_start(out=DW[:, mi, :], in_=d_wd[which][l, m]), f"dw{mi}", writes=[B_dw[mi]])
                for bi, (c0, w) in enumerate(FBLK):
                    if w == 0:
                        continue
                    for o in range(KC):
                        pd = 4 + (n_pd % 2)
                        n_pd += 1

                        def mm(e, c0=c0, w=w, pd=pd, o=o, gsz=gsz):
                            ins = None
                            for mi in range(gsz):
                                ins = e.matmul(bank(pd, 0, w), DW[:, mi, o * 128:(o + 1) * 128], HID[:, mi, c0:c0 + w],
                                               start=(mi == 0), stop=(mi == gsz - 1))
                            return ins
                        P.op("pe", mm, reads=B_dw[:gsz] + [B_hid[mi][bi] for mi in range(gsz)], writes=[B_ps[pd]])
                        resid_evac(bi, o, bank(pd, 0, w), [B_ps[pd]], gi)
                        if last:
                            ln_stats_feed(bi, o, blk_done)
                if last:
                    ln_flush()
                    ln_apply(NB - 1, nxt is not None)
                m0 += gsz
            while ada_js:
                ada_item(ada_next, ada_js.pop(0))

        def mixer_A(l, cs):
            cv = Carver()
            WIN_ = cv.take([KC, WEXT], BF16)
            off_after_win = cv.off
            CVx = cv.take([2, TB + 2], F32)
            GBf = cv.take([2 * TB], F32)
            GBs = GBf.rearrange("p (c t) -> p c t", c=2)
            TMP = [cv.take([TB], F32) for _ in range(2)]
            Ux = cv.take([2, TB + 15], F32)
            Sa = cv.take([TB + 15], F32)
            Sb = cv.take([TB + 15], F32)
            PL = cv.take([2, TB], BF16)
            Q = cv.take([4, TB], BF16)
            KZ = cv.take([4, TB + 128], BF16)
            Vd = [cv.take([256], BF16) for _ in range(3)]
            PT = cv.take([1024], BF16)
            RD = cv.take([4, 128], F32)
            Y = cv.take([KC, TB], F32)
            SQ = [cv.take([TB], BF16) for _ in range(2)]
            RSf = cv.take([3 * TB], F32)
            RS = RSf.rearrange("p (g t) -> p g t", g=3)
            B_win = [abuf(f"win{i}") for i in range(4)]
            B_cvx, B_gbs, B_ux, B_sa, B_sb, B_pl, B_q, B_kz = (abuf(n) for n in ("cvx", "gbs", "ux", "sa", "sb", "pl", "q", "kz"))
            B_tmp = [abuf("tmp0"), abuf("tmp1")]
            B_vd = [abuf(f"vd{i}") for i in range(3)]
            B_pt, B_rd, B_rs = abuf("pt"), abuf("rd"), abuf("rs")
            B_y = [abuf(f"y{c}") for c in range(KC)]
            B_sq = [abuf("sq0"), abuf("sq1")]
            newb = B_win + [B_cvx, B_gbs, B_ux, B_sa, B_sb, B_pl, B_q, B_kz] + B_tmp + B_vd + [B_pt, B_rd, B_rs] + B_y + B_sq
            phase_switch(newb)

            for i in range(4):
                P.dma("pool", lambda e, i=i: e.dma_start(out=WIN_[:, :, i * 512:(i + 1) * 512], in_=d_win[l, i]), f"win{i}", writes=[B_win[i]])
            P.op("dve", lambda e: e.memset(KZ[:], 0.0), writes=[B_kz])
            P.op("dve", lambda e: e.memset(CVx[:, :, 0:2], 0.0), writes=[B_cvx])
            P.op("dve", lambda e: e.memset(Ux[:, :, 0:15], 0.0), writes=[B_ux])

            st_ = {"hs": 0, "sq": 0}

            def inproj_chunk(c0, w, col0, evac):
                s = st_["hs"] % 4
                st_["hs"] += 1
                bk, hf = s // 2, s % 2
                wi = col0 // 512

                def mm(e):
                    ins = None
                    for kc in range(KC):
                        ins = e.matmul(bank(bk, hf * 256, hf * 256 + w), WIN_[:, kc, col0:col0 + 128], HIN[:, kc, c0:c0 + w],
                                       start=(kc == 0), stop=(kc == KC - 1))
                    return ins
                P.op("pe", mm, reads=[B_win[wi]] + hin_bufs(c0, w), writes=[B_hb[bk][hf]])
                evac(lambda p0=0, p1=128: bank(bk, hf * 256, hf * 256 + w, p0, p1), [B_hb[bk][hf]])

            def rms_group(grp, chunks, n, w, ysrc, ybufs, rs_ap, sqbank, sqlo):
                for i, c in enumerate(chunks):
                    r = st_["sq"] % 2
                    st_["sq"] += 1
                    sq = SQ[r]
                    P.op("act", lambda e, sq=sq, c=c: e.activation(out=sq[:, :w], in_=ysrc(c), func=AF.Square),
                         reads=[ybufs[c]], writes=[B_sq[r]])
                    P.op("pe", lambda e, sq=sq, i=i: e.matmul(bank(sqbank, sqlo, sqlo + w), ones[:], sq[:, :w], start=(i == 0),
                                                              stop=(i == len(chunks) - 1)),
                         reads=[B_sq[r], B_const], writes=[B_ps[sqbank]])
                P.op("act", lambda e: e.activation(out=rs_ap, in_=bank(sqbank, sqlo, sqlo + w), func=AF.Ln, scale=1.0 / n,
                                                   bias=epsT[:, 1:2]),
                     reads=[B_ps[sqbank], B_const], writes=[B_rs])
                P.op("act", lambda e: e.activation(out=rs_ap, in_=rs_ap, func=AF.Exp, scale=-0.5), reads=[B_rs], writes=[B_rs])

            JOWN = HALO // 128
            JS = cs // 128

            def tb_body(t, c0, w, first_own):
                if first_own:
                    P.op("pool", lambda e: e.tensor_scalar(out=CVx[:, :, 0:2], in0=CVx[:, :, 0:2], scalar1=valid[:, 0:1], scalar2=0.0,
                                                            op0=ALU.mult, op1=ALU.add), reads=[B_cvx, B_const], writes=[B_cvx])
                    P.op("pool", lambda e: e.tensor_scalar(out=Ux[:, :, 0:15], in0=Ux[:, :, 0:15], scalar1=valid[:, 0:1], scalar2=0.0,
                                                            op0=ALU.mult, op1=ALU.add), reads=[B_ux, B_const], writes=[B_ux])
                for c in range(4):
                    inproj_chunk(c0, w, 1024 + c * 128,
                                 lambda ps, pb, c=c: P.op("act", lambda e: act_id(e, out=Q[:, c, :w], in_=ps()), reads=pb, writes=[B_q]))
                for h in range(2):
                    def ev(ps, pb, h=h):
                        P.op("dve", lambda e: e.tensor_copy(out=KZ[0:64, 2 * h, 128:128 + w], in_=ps(0, 64)), reads=pb, writes=[B_kz])
                        P.op("dve", lambda e: e.tensor_copy(out=KZ[64:128, 2 * h + 1, 128:128 + w], in_=ps(64, 128)), reads=pb, writes=[B_kz])
                    inproj_chunk(c0, w, 1536 + h * 128, ev)
                thunks = []

                def th_u(cc):
                    inproj_chunk(c0, w, 768 + cc * 128,
                                 lambda ps, pb: P.op("act", lambda e: act_id(e, out=Ux[:, cc, 15:15 + w], in_=ps()), reads=pb, writes=[B_ux]))

                def th_gc(cc):
                    tm = TMP[cc]
                    inproj_chunk(c0, w, 256 + cc * 128,
                                 lambda ps, pb: P.op("act", lambda e: act_id(e, out=tm[:, :w], in_=ps()), reads=pb, writes=[B_tmp[cc]]))

                def th_xin(cc):
                    tm = TMP[cc]
                    inproj_chunk(c0, w, 512 + cc * 128,
                                 lambda ps, pb: P.op("dve", lambda e: e.tensor_tensor(out=CVx[:, cc, 2:2 + w], in0=tm[:, :w], in1=ps(), op=ALU.mult),
                                                     reads=pb + [B_tmp[cc]], writes=[B_cvx]))

                def th_gb(cc):
                    inproj_chunk(c0, w, cc * 128,
                                 lambda ps, pb: P.op("act", lambda e: act_id(e, out=GBs[:, cc, :w], in_=ps()), reads=pb, writes=[B_gbs]))
                for jj in range(w // 128):
                    j = c0 // 128 + jj
                    vs = j % 3
                    ca = c0 + jj * 128

                    def mm(e, ca=ca):
                        ins = None
                        for kc in range(KC):
                            ins = e.matmul(bank(2, 0, 256), HIN[:, kc, ca:ca + 128], WIN_[:, kc, 1792:2048], start=(kc == 0), stop=(kc == KC - 1))
                        return ins
                    P.op("pe", mm, reads=[B_win[3]] + hin_bufs(ca, 128), writes=[B_ps[2]])
                    P.op("act", lambda e, vs=vs: act_id(e, out=Vd[vs][:, :], in_=bank(2, 0, 256)), reads=[B_ps[2]], writes=[B_vd[vs]])
                def conv_chain(cc):
                    acc = TMP[cc]
                    t2 = TMP[1 - cc]
                    P.op("pool", lambda e, cc=cc, acc=acc: e.tensor_scalar(out=acc[:, :w], in0=CVx[:, cc, 0:w], scalar1=convw[:, l, 0, cc:cc + 1],
                                                                            scalar2=0.0, op0=ALU.mult, op1=ALU.add),
                         reads=[B_cvx, B_const], writes=[B_tmp[cc]])
                    for k in (1, 2):
                        P.op("pool", lambda e, cc=cc, t2=t2, k=k: e.tensor_scalar(out=t2[:, :w], in0=CVx[:, cc, k:k + w],
                                                                                   scalar1=convw[:, l, k, cc:cc + 1], scalar2=0.0,
                                                                                   op0=ALU.mult, op1=ALU.add),
                             reads=[B_cvx, B_const], writes=[B_tmp[1 - cc]])
                        P.op("pool", lambda e, acc=acc, t2=t2: e.tensor_tensor(out=acc[:, :w], in0=acc[:, :w], in1=t2[:, :w], op=ALU.add),
                             reads=[B_tmp[0], B_tmp[1]], writes=[B_tmp[cc]])
                    P.op("pool", lambda e, cc=cc, acc=acc: e.tensor_tensor(out=Y[:, cc, :w], in0=acc[:, :w], in1=GBs[:, cc, :w], op=ALU.mult),
                         reads=[B_tmp[cc], B_gbs], writes=[B_y[cc]])
                def conv_tail():
                    P.op("pool", lambda e: e.tensor_copy(out=CVx[:, :, 0:2], in_=CVx[:, :, w:w + 2]), reads=[B_cvx], writes=[B_cvx])
                WX = w + 15

                def pool_chain(cc):
                    ux = Ux[:, cc, :]
                    P.op("pool", lambda e, ux=ux: e.tensor_tensor(out=Sa[:, 1:WX], in0=ux[:, 1:WX], in1=ux[:, 0:WX - 1], op=ALU.add),
                         reads=[B_ux], writes=[B_sa])
                    if cc == 0:
                        P.op("pool", lambda e: e.tensor_tensor(out=Sb[64:128, 3:WX], in0=Sa[64:128, 3:WX], in1=Sa[64:128, 1:WX - 2], op=ALU.add),
                             reads=[B_sa], writes=[B_sb])
                    else:
                        P.op("pool", lambda e: e.tensor_tensor(out=Sb[:, 3:WX], in0=Sa[:, 3:WX], in1=Sa[:, 1:WX - 2], op=ALU.add),
                             reads=[B_sa], writes=[B_sb])
                        P.op("pool", lambda e: e.tensor_tensor(out=Sa[:, 7:WX], in0=Sb[:, 7:WX], in1=Sb[:, 3:WX - 4], op=ALU.add),
                             reads=[B_sb], writes=[B_sa])
                        P.op("pool", lambda e: e.tensor_tensor(out=Sb[64:128, 15:WX], in0=Sa[64:128, 15:WX], in1=Sa[64:128, 7:WX - 8], op=ALU.add),
                             reads=[B_sa], writes=[B_sb])
                    for (p0, p1, src, Bs) in ((0, 64, Sa, B_sa), (64, 128, Sb, B_sb)):
                        P.op("pool", lambda e, p0=p0, p1=p1, src=src, cc=cc: e.tensor_scalar(
                            out=src[p0:p1, 15:WX], in0=src[p0:p1, 15:WX], scalar1=invw[p0:p1, cc:cc + 1], scalar2=0.0, op0=ALU.mult, op1=ALU.add),
                            reads=[Bs, B_const], writes=[Bs])
                        if first_own:
                            P.op("pool", lambda e, p0=p0, p1=p1, src=src, cc=cc: e.tensor_tensor(
                                out=src[p0:p1, 15:31], in0=src[p0:p1, 15:31], in1=cntr[p0:p1, cc, :], op=ALU.mult),
                                reads=[Bs, B_const], writes=[Bs])
                        P.op("pool", lambda e, p0=p0, p1=p1, src=src, cc=cc: e.tensor_tensor(
                            out=PL[p0:p1, cc, :w], in0=src[p0:p1, 15:WX], in1=Ux[p0:p1, cc, 15:WX], op=ALU.subtract),
                            reads=[Bs, B_ux], writes=[B_pl])

                def pool_mm(cc):
                    P.op("pe", lambda e, cc=cc: e.matmul(bank(3, 0, w), poolw[:, l, cc, :], PL[:, cc, :w], start=True, stop=True),
                         reads=[B_pl, B_const], writes=[B_ps[3]])
                    P.op("act", lambda e, cc=cc: act_id(e, Y[:, 2 + cc, :w], bank(3, 0, w), pscale[:, l, cc:cc + 1]),
                         reads=[B_ps[3], B_const], writes=[B_y[2 + cc]])
                def pool_tail():
                    P.op("pool", lambda e: e.tensor_copy(out=Ux[:, :, 0:15], in_=Ux[:, :, w:w + 15]), reads=[B_ux], writes=[B_ux])
                thunks += [lambda: th_u(0), lambda: (th_u(1), pool_chain(0), pool_chain(1), pool_tail()),
                           lambda: th_gc(0), lambda: th_xin(0), lambda: (th_gb(0), conv_chain(0)),
                           lambda: th_gc(1), lambda: th_xin(1), lambda: (th_gb(1), conv_chain(1), conv_tail()),
                           lambda: pool_mm(0), lambda: pool_mm(1)]
                nper = 2 if w == TB else 4

                def pop_thunks(n):
                    for _ in range(n):
                        if thunks:
                            thunks.pop(0)()
                for jj in range(w // 128):
                    j = c0 // 128 + jj
                    qlo = jj * 128
                    kprev = qlo
                    kdiag = 128 + qlo
                    has_prev = j > JS
                    mprev = 2 if j == JOWN else 0
                    for h in range(2):
                        def mm(e, h=h, qlo=qlo, kprev=kprev, kdiag=kdiag, has_prev=has_prev, mprev=mprev):
                            ins = None
                            for g in range(4):
                                c = 2 * h + g // 2
                                half = g % 2
                                if has_prev:
                                    e.matmul(bank(4, g * 128, (g + 1) * 128), identb[:], masks[:, mprev, :], start=True, stop=False)
                                    ins = e.matmul(bank(4, g * 128, (g + 1) * 128), KZ[:, 2 * h + half, kprev:kprev + 128],
                                                   Q[:, c, qlo:qlo + 128], start=False, stop=True)
                                e.matmul(bank(5, g * 128, (g + 1) * 128), identb[:], masks[:, 1, :], start=True, stop=False)
                                ins = e.matmul(bank(5, g * 128, (g + 1) * 128), KZ[:, 2 * h + half, kdiag:kdiag + 128],
                                               Q[:, c, qlo:qlo + 128], start=False, stop=True)
                            return ins
                        P.op("pe", mm, reads=[B_kz, B_q, B_const], writes=[B_ps[4], B_ps[5]])
                        if has_prev:
                            P.op("act", lambda e: e.activation(out=PT[:, 0:512], in_=bank(4), func=AF.Exp, scale=0.125),
                                 reads=[B_ps[4]], writes=[B_pt])
                        P.op("act", lambda e: e.activation(out=PT[:, 512:1024], in_=bank(5), func=AF.Exp, scale=0.125),
                             reads=[B_ps[5]], writes=[B_pt])
                        pop_thunks(nper)
                        vprev, vcur = (j - 1) % 3, j % 3

                        def pv(e, h=h, vprev=vprev, vcur=vcur, has_prev=has_prev):
                            if has_prev:
                                e.matmul(bank(6), Vd[vprev][:, h * 128:(h + 1) * 128], PT[:, 0:512], start=True, stop=False)
                            e.matmul(bank(6), Vd[vcur][:, h * 128:(h + 1) * 128], PT[:, 512:1024], start=not has_prev, stop=True)
                            if has_prev:
                                e.matmul(bank(7), ones[:], PT[:, 0:512], start=True, stop=False)
                            return e.matmul(bank(7), ones[:], PT[:, 512:1024], start=not has_prev, stop=True)
                        P.op("pe", pv, reads=[B_pt, B_vd[vprev], B_vd[vcur], B_const], writes=[B_ps[6], B_ps[7]])
                        sk4 = sinke4[:, l, 4 * h:4 * h + 4].unsqueeze(2).to_broadcast([128, 4, 128])
                        P.op("dve", lambda e, sk4=sk4: e.tensor_tensor(out=RD[:, :, :], in0=bank(7).rearrange("p (g q) -> p g q", g=4), in1=sk4,
                                                                        op=ALU.add), reads=[B_ps[7], B_const], writes=[B_rd])
                        P.op("act", lambda e: e.activation(out=RD[:, :, :], in_=RD[:, :, :], func=AF.Ln), reads=[B_rd], writes=[B_rd])
                        P.op("act", lambda e: e.activation(out=RD[:, :, :], in_=RD[:, :, :], func=AF.Exp, scale=-1.0), reads=[B_rd], writes=[B_rd])
                        for half in range(2):
                            p0, p1 = half * 64, half * 64 + 64
                            oo = bank(6, 0, 512, p0, p1).rearrange("p (gg hf q) -> p gg hf q", gg=2, hf=2)[:, :, half, :]
                            rr = RD[p0:p1, :, :].rearrange("p (gg hf) q -> p gg hf q", hf=2)[:, :, half, :]
                            P.op("dve", lambda e, oo=oo, rr=rr, p0=p0, p1=p1, h=h, qlo=qlo: e.tensor_tensor(
                                out=Y[p0:p1, 4 + 2 * h:6 + 2 * h, qlo:qlo + 128], in0=oo, in1=rr, op=ALU.mult),
                                reads=[B_ps[6], B_rd], writes=[B_y[4 + 2 * h], B_y[5 + 2 * h]])
                pop_thunks(100)
                P.op("act", lambda e: act_id(e, out=KZ[:, :, 0:128], in_=KZ[:, :, w:w + 128]), reads=[B_kz], writes=[B_kz])
                if t == NTB - 1:
                    token_major_tail(l, NP - 128, 128, WIN_, B_win, GBf, B_gbs, TMP[0], B_tmp[0], False, RSf, B_rs)
                ysrc = lambda c, w=w: Y[:, c, :w]
                slots = [(2, 256), (3, 0), (3, 256)]
                for grp, (chunks, n) in enumerate([([0, 1], 256.0), ([2, 3], 256.0), ([4, 5, 6, 7], 512.0)]):
                    bk_, lo_ = slots[grp]
                    for i, c in enumerate(chunks):
                        r = st_["sq"] % 2
                        st_["sq"] += 1
                        sq = SQ[r]
                        P.op("act", lambda e, sq=sq, c=c, n=n: e.activation(out=sq[:, :w], in_=Y[:, c, :w], func=AF.Square, scale=float(n) ** -0.5),
                             reads=[B_y[c]], writes=[B_sq[r]])
                        P.op("pe", lambda e, sq=sq, i=i, bk_=bk_, lo_=lo_, chunks=chunks: e.matmul(bank(bk_, lo_, lo_ + w), ones[:], sq[:, :w],
                                                                                                 start=(i == 0), stop=(i == len(chunks) - 1)),
                             reads=[B_sq[r], B_const], writes=[B_ps[bk_]])
                P.op("act", lambda e: e.activation(out=RSf[:, 0:3 * TB], in_=psum[:, 2 * 512 + 256:4 * 512], func=AF.Ln, bias=epsT[:, 1:2]),
                     reads=[B_ps[2], B_ps[3], B_const], writes=[B_rs])
                P.op("act", lambda e: e.activation(out=RSf[:, 0:3 * TB], in_=RSf[:, 0:3 * TB], func=AF.Exp, scale=-0.5), reads=[B_rs], writes=[B_rs])
                for c in range(KC):
                    grp = 0 if c < 2 else (1 if c < 4 else 2)
                    P.op("dve", lambda e, c=c, grp=grp, c0=c0, w=w: e.scalar_tensor_tensor(out=HIN[:, c, c0:c0 + w], in0=Y[:, c, :w],
                                                                                scalar=mixg[:, l, c:c + 1], in1=RS[:, grp, :w],
                                                                                op0=ALU.mult, op1=ALU.mult),
                         reads=[B_y[c], B_rs, B_const], writes=hin_bufs(c0, w, [c]))


            for t in range(NTB):
                c0_ = max(t * TB, cs)
                w_ = (t + 1) * TB - c0_
                if w_ > 0:
                    tb_body(t, c0_, w_, c0_ == HALO)

            cs = Carver(off_after_win)
            cvS = cs.take([2, NS], F32)
            gbS = cs.take([2, NS], F32)
            tmS = cs.take([NS], F32)
            accS = cs.take([NS], F32)
            S1 = cs.take([NS], F32)
            SCV = cs.take([2, 2, NS], F32)
            U16 = cs.take([2, NS, 16], F32)
            PLS = cs.take([2, NS], BF16)
            STa = cs.take([512], F32)
            STb = cs.take([256], F32)
            STc = cs.take([512], F32)
            B_s = abuf("sampA")
            B_scv, B_u16 = abuf("scv"), abuf("u16")
            B_sta, B_stb, B_stc = abuf("sta"), abuf("stb"), abuf("stc")
            phase_switch([B_s, B_scv, B_u16, B_sta, B_stb, B_stc], keep=B_win)
            P.dma("sp", lambda e: e.dma_start(out=SCV[:], in_=d_sconvT[l]), "scv", writes=[B_scv])
            P.dma("sp", lambda e: e.dma_start(out=U16[:], in_=d_spoolT[l]), "u16", writes=[B_u16])
            c0, w = NP, NS
            for cc in range(2):
                inproj_chunk(c0, w, 256 + cc * 128,
                             lambda ps, pb: P.op("act", lambda e: act_id(e, out=tmS[:, :], in_=ps()), reads=pb, writes=[B_s]))
                inproj_chunk(c0, w, 512 + cc * 128,
                             lambda ps, pb, cc=cc: P.op("dve", lambda e: e.tensor_tensor(out=cvS[:, cc, :], in0=tmS[:, :], in1=ps(), op=ALU.mult),
                                                        reads=pb + [B_s], writes=[B_s]))
                inproj_chunk(c0, w, cc * 128,
                             lambda ps, pb, cc=cc: P.op("act", lambda e: act_id(e, out=gbS[:, cc, :], in_=ps()), reads=pb, writes=[B_s]))
                inproj_chunk(c0, w, 768 + cc * 128,
                             lambda ps, pb, cc=cc: P.op("act", lambda e: act_id(e, out=U16[:, cc, :, 15], in_=ps()), reads=pb, writes=[B_u16]))
            for c in range(4):
                inproj_chunk(c0, w, 1024 + c * 128,
                             lambda ps, pb, c=c: P.op("act", lambda e: act_id(e, out=QS[:, c, :], in_=ps()), reads=pb, writes=[B_samp]))
            for h in range(2):
                def ev(ps, pb, h=h):
                    P.op("dve", lambda e: e.tensor_copy(out=KZS[0:64, 2 * h, :], in_=ps(0, 64)), reads=pb, writes=[B_samp])
                    P.op("dve", lambda e: e.tensor_copy(out=KZS[64:128, 2 * h + 1, :], in_=ps(64, 128)), reads=pb, writes=[B_samp])
                inproj_chunk(c0, w, 1536 + h * 128, ev)
            for cc in range(2):
                P.op("dve", lambda e, cc=cc: e.tensor_scalar(out=accS[:, :], in0=SCV[:, cc, 0, :], scalar1=convw[:, l, 0, cc:cc + 1], scalar2=None,
                                                             op0=ALU.mult), reads=[B_scv, B_const], writes=[B_s])
                P.op("dve", lambda e, cc=cc: e.scalar_tensor_tensor(out=accS[:, :], in0=SCV[:, cc, 1, :], scalar=convw[:, l, 1, cc:cc + 1],
                                                                    in1=accS[:, :], op0=ALU.mult, op1=ALU.add),
                     reads=[B_scv, B_const, B_s], writes=[B_s])
                P.op("dve", lambda e, cc=cc: e.scalar_tensor_tensor(out=accS[:, :], in0=cvS[:, cc, :], scalar=convw[:, l, 2, cc:cc + 1],
                                                                    in1=accS[:, :], op0=ALU.mult, op1=ALU.add),
                     reads=[B_const, B_s], writes=[B_s])
                P.op("dve", lambda e, cc=cc: e.tensor_tensor(out=YS[:, cc, :], in0=accS[:, :], in1=gbS[:, cc, :], op=ALU.mult),
                     reads=[B_s], writes=[B_samp])
            for cc in range(2):
                for half in range(2):
                    p0, p1 = half * 64, half * 64 + 64
                    wd_ = 2 ** (2 * cc + half + 1)
                    P.op("dve", lambda e, p0=p0, p1=p1, cc=cc, wd_=wd_: e.tensor_reduce(out=S1[p0:p1, :], in_=U16[p0:p1, cc, :, 16 - wd_:16],
                                                                                         axis=mybir.AxisListType.X, op=ALU.add),
                         reads=[B_u16], writes=[B_s])
                    P.op("dve", lambda e, p0=p0, p1=p1, cc=cc: e.scalar_tensor_tensor(out=PLS[p0:p1, cc, :], in0=S1[p0:p1, :],
                                                                                      scalar=invw[p0:p1, cc:cc + 1], in1=U16[p0:p1, cc, :, 15],
                                                                                      op0=ALU.mult, op1=ALU.subtract),
                         reads=[B_s, B_u16, B_const], writes=[B_s])
                P.op("pe", lambda e, cc=cc: e.matmul(bank(3, 0, NS), poolw[:, l, cc, :], PLS[:, cc, :], start=True, stop=True),
                     reads=[B_s, B_const], writes=[B_ps[3]])
                P.op("act", lambda e, cc=cc: act_id(e, YS[:, 2 + cc, :], bank(3, 0, NS), pscale[:, l, cc:cc + 1]),
                     reads=[B_ps[3], B_const], writes=[B_samp])
            token_major_tail(l, NP, NS, WIN_, B_win, STa, B_sta, STb, B_stb, True, STc, B_stc)

        def token_major_tail(l, ca, M, WIN_, B_win, ST, B_st, ST2, B_st2, is_sample, ST3=None, B_st3=None):
            hb = hin_bufs(ca, M)
            if ST3 is None:
                raise ValueError

            def mm_pass(bk, col0, n):
                def mm(e):
                    ins = None
                    for kc in range(KC):
                        ins = e.matmul(bank(bk, 0, n, 0, M), HIN[:, kc, ca:ca + M], WIN_[:, kc, col0:col0 + n], start=(kc == 0), stop=(kc == KC - 1))
                    return ins
                wis = sorted(set([col0 // 512, (col0 + n - 1) // 512]))
                P.op("pe", mm, reads=[B_win[i] for i in wis] + hb, writes=[B_ps[bk]] + B_hb[bk])
            mm_pass(0, 256, 512)
            P.op("act", lambda e: act_id(e, out=ST2[0:M, 0:256], in_=bank(0, 0, 256, 0, M)), reads=[B_ps[0], B_hb[0][0], B_hb[0][1]], writes=[B_st2])
            P.op("dve", lambda e: e.tensor_tensor(out=ST[0:M, 0:256], in0=ST2[0:M, 0:256], in1=bank(0, 256, 512, 0, M), op=ALU.mult),
                 reads=[B_ps[0], B_hb[0][0], B_hb[0][1], B_st2], writes=[B_st])
            mm_pass(1, 768, 256)
            P.op("act", lambda e: act_id(e, out=ST[0:M, 256:512], in_=bank(1, 0, 256, 0, M)), reads=[B_ps[1], B_hb[1][0], B_hb[1][1]], writes=[B_st])
            mm_pass(0, 1536, 512)
            P.op("act", lambda e: act_id(e, out=ST3[0:M, 0:512], in_=bank(0, 0, 512, 0, M)), reads=[B_ps[0], B_hb[0][0], B_hb[0][1]], writes=[B_st3])
            kv = ST3[0:M, 0:512].rearrange("p (a h r) -> p a h r", a=2, h=2)[:, :, :, 0:64]
            tag = "s" if is_sample else "p"
            if not is_sample:
                P.dma("sp", lambda e: e.dma_start(out=o_pc[l], in_=ST[M - 2:M, 0:256]), "o_st" + tag, reads=[B_st, B_out])
                P.dma("sp", lambda e: e.dma_start(out=o_pp[l], in_=ST[M - 15:M, 256:512]), "o_st" + tag, reads=[B_st, B_out])
                P.dma("sp", lambda e: e.dma_start(out=o_pk[l].rearrange("t (h d) -> t h d", h=2), in_=kv[:, 0, :, :]), "o_st3" + tag,
                      reads=[B_st3, B_out])
                P.dma("sp", lambda e: e.dma_start(out=o_pv[l].rearrange("t (h d) -> t h d", h=2), in_=kv[:, 1, :, :]), "o_st3" + tag,
                      reads=[B_st3, B_out])
            else:
                P.dma("sp", lambda e: e.dma_start(out=o_sc[l, :, 1, :], in_=ST[0:M, 0:256]), "o_st" + tag, reads=[B_st, B_out])
                P.dma("sp", lambda e: e.dma_start(out=o_spl[l, :, 14, :], in_=ST[0:M, 256:512]), "o_st" + tag, reads=[B_st, B_out])
                P.dma("sp", lambda e: e.dma_start(out=o_sk[l, :, 127, :].rearrange("t (h d) -> t h d", h=2), in_=kv[:, 0, :, :]), "o_st3" + tag,
                      reads=[B_st3, B_out])
                P.dma("sp", lambda e: e.dma_start(out=o_sv[l, :, 127, :].rearrange("t (h d) -> t h d", h=2), in_=kv[:, 1, :, :]), "o_st3" + tag,
                      reads=[B_st3, B_out])
                P.op("act", lambda e: act_id(e, out=VtokS[0:M, :], in_=ST3[0:M, 256:512]), reads=[B_st3], writes=[B_samp])

        def mixer_B(l, cs):
            set_start(cs)
            cv = Carver()
            WOUT = cv.take([KC, D], BF16)
            lnb_ = carve_ln(cv)
            KS = [cv.take([4, 512], BF16) for _ in range(2)]
            VS = [cv.take([4, 256], BF16) for _ in range(2)]
            PTs = cv.take([128], BF16)
            T1 = cv.take([NS, 4], F32)
            SQs = [cv.take([NS], BF16) for _ in range(2)]
            RSs = cv.take([3, NS], F32)
            B_wout = [abuf("wout0"), abuf("wout1")]
            B_ks = [abuf("ks0"), abuf("ks1")]
            B_vs = [abuf("vs0"), abuf("vs1")]
            B_pts, B_t1, B_rss = abuf("pts"), abuf("t1"), abuf("rss")
            B_sqs = [abuf("sqs0"), abuf("sqs1")]
            phase_switch(B_wout + lnb_ + B_ks + B_vs + [B_pts, B_t1, B_rss] + B_sqs)
            for i in range(2):
                P.dma("pool", lambda e, i=i: e.dma_start(out=WOUT[:, :, i * 512:(i + 1) * 512], in_=d_wout[l, i]), f"wout{i}", writes=[B_wout[i]])
            ln_consts(l, 1, (l % 2, 6, 7))
            gates(l, 1)
            def samp_dma(gq):
                slot = gq % 2
                ks, vs = KS[slot], VS[slot]
                P.dma("pool", lambda e, ks=ks, gq=gq: e.dma_start(out=ks[:, :, :], in_=d_kTz[l, 4 * gq:4 * gq + 4].rearrange("b p n -> p b n")),
                      f"ks{slot}", writes=[B_ks[slot]])
                P.dma("pool", lambda e, vs=vs, gq=gq: e.dma_start(out=vs[:, :, :], in_=d_vd[l, 4 * gq:4 * gq + 4].rearrange("b p n -> p b n")),
                      f"vs{slot}", writes=[B_vs[slot]])
            samp_dma(0)
            samp_dma(1)

            def samp_group(gq):
                slot = gq % 2
                ks, vs = KS[slot], VS[slot]
                for bl in range(4):
                    b = 4 * gq + bl
                    P.op("act", lambda e, ks=ks, bl=bl, b=b: act_id(e, out=ks[:, bl, :].rearrange("p (x k) -> p x k", x=4)[:, :, 0], in_=KZS[:, :, b]),
                         reads=[B_samp, B_ks[slot]], writes=[B_ks[slot]])
                    P.op("pe", lambda e, b=b: e.matmul(bank(3, 0, 256, 0, 1), ident[0:16, b:b + 1], VtokS[0:16, :], start=True, stop=True),
                         reads=[B_samp, B_const], writes=[B_ps[3]])
                    P.op("act", lambda e, vs=vs, bl=bl: act_id(e, out=vs[0:1, bl, :], in_=bank(3, 0, 256, 0, 1)), reads=[B_ps[3], B_vs[slot]],
                         writes=[B_vs[slot]])

                    def sc(e, ks=ks, bl=bl, b=b):
                        ins = None
                        for h in range(2):
                            for half in range(2):
                                for cl in range(2):
                                    col = b * 8 + h * 4 + half * 2 + cl
                                    x = h * 2 + half
                                    ins = e.matmul(bank(0, col, col + 1), ks[:, bl, x * 128:(x + 1) * 128], QS[:, 2 * h + cl, b:b + 1],
                                                   start=True, stop=True)
                        return ins
                    P.op("pe", sc, reads=[B_ks[slot], B_samp], writes=[B_ps[0], B_hb[0][0], B_hb[0][1]])
                g0, g1 = gq * 32, gq * 32 + 32
                P.op("act", lambda e, g0=g0, g1=g1: e.activation(out=PTs[:, g0:g1], in_=bank(0, g0, g1), func=AF.Exp, scale=0.125),
                     reads=[B_ps[0], B_hb[0][0], B_hb[0][1]], writes=[B_pts])

                def pv(e, vs=vs, gq=gq, g0=g0, g1=g1):
                    ins = e.matmul(bank(1, g0, g1), ones[:], PTs[:, g0:g1], start=True, stop=True)
                    for bl in range(4):
                        b = 4 * gq + bl
                        for h in range(2):
                            col0 = b * 8 + h * 4
                            ins = e.matmul(bank(2, col0, col0 + 4), vs[:, bl, h * 128:(h + 1) * 128], PTs[:, col0:col0 + 4], start=True, stop=True)
                    return ins
                P.op("pe", pv, reads=[B_pts, B_vs[slot], B_const], writes=[B_ps[1], B_hb[1][0], B_hb[1][1], B_ps[2]])
                if gq + 2 < NS // 4:
                    samp_dma(gq + 2)
            def samp_finish():
                for half in range(2):
                    p0, p1 = half * 64, half * 64 + 64
                    den = bank(1, 0, 128, p0, p1).rearrange("p (b h f c) -> p b h f c", b=NS, h=2, f=2)[:, :, :, half, :]
                    oo = bank(2, 0, 128, p0, p1).rearrange("p (b h f c) -> p b h f c", b=NS, h=2, f=2)[:, :, :, half, :]
                    sk = sinke[p0:p1, l, :].rearrange("p (h c) -> p h c", h=2).unsqueeze(1).to_broadcast([64, NS, 2, 2])
                    t1 = T1[p0:p1, :, :].rearrange("p b (h c) -> p b h c", h=2)
                    P.op("dve", lambda e, den=den, sk=sk, t1=t1: e.tensor_tensor(out=t1, in0=den, in1=sk, op=ALU.add),
                         reads=[B_ps[1], B_hb[1][0], B_hb[1][1], B_const], writes=[B_t1])
                    P.op("dve", lambda e, t1=t1: e.reciprocal(out=t1, in_=t1), reads=[B_t1], writes=[B_t1])
                    ys = YS[p0:p1, 4:8, :].rearrange("p (h c) b -> p b h c", h=2)
                    P.op("dve", lambda e, oo=oo, t1=t1, ys=ys: e.tensor_tensor(out=ys, in0=oo, in1=t1, op=ALU.mult),
                         reads=[B_ps[2], B_t1], writes=[B_samp])
                nsq = [0]
                for grp, (chunks, n) in enumerate([([0, 1], 256.0), ([2, 3], 256.0), ([4, 5, 6, 7], 512.0)]):
                    for i, c in enumerate(chunks):
                        r = nsq[0] % 2
                        nsq[0] += 1
                        sq = SQs[r]
                        P.op("act", lambda e, sq=sq, c=c: e.activation(out=sq[:, :], in_=YS[:, c, :], func=AF.Square), reads=[B_samp], writes=[B_sqs[r]])
                        P.op("pe", lambda e, sq=sq, i=i, grp=grp, chunks=chunks: e.matmul(bank(1, 256 + grp * 16, 256 + grp * 16 + NS), ones[:], sq[:, :],
                                                                                          start=(i == 0), stop=(i == len(chunks) - 1)),
                             reads=[B_sqs[r], B_const], writes=[B_ps[1], B_hb[1][0], B_hb[1][1]])
                    P.op("act", lambda e, grp=grp, n=n: e.activation(out=RSs[:, grp, :], in_=bank(1, 256 + grp * 16, 256 + grp * 16 + NS), func=AF.Ln,
                                                                     scale=1.0 / n, bias=epsT[:, 1:2]),
                         reads=[B_ps[1], B_hb[1][0], B_hb[1][1], B_const], writes=[B_rss])
                    P.op("act", lambda e, grp=grp: e.activation(out=RSs[:, grp, :], in_=RSs[:, grp, :], func=AF.Exp, scale=-0.5), reads=[B_rss], writes=[B_rss])
                for c in range(KC):
                    grp = 0 if c < 2 else (1 if c < 4 else 2)
                    P.op("dve", lambda e, c=c, grp=grp: e.scalar_tensor_tensor(out=HIN[:, c, NP:NP + NS], in0=YS[:, c, :], scalar=mixg[:, l, c:c + 1],
                                                                                in1=RSs[:, grp, :], op0=ALU.mult, op1=ALU.mult),
                         reads=[B_samp, B_rss, B_const], writes=hin_bufs(NP, NS, [c]))

            prevb = [None]

            def blk_done(bi):
                ln_math(bi)
                if prevb[0] is not None:
                    ln_apply(prevb[0], True)
                prevb[0] = bi
            n_pd = 0
            sg_next = [0]

            def samp_some(n):
                for _ in range(n):
                    if sg_next[0] < NS // 4:
                        samp_group(sg_next[0])
                        sg_next[0] += 1
                        if sg_next[0] == NS // 4:
                            samp_finish()
            for bi, (c0, w) in enumerate(FBLK):
                if w == 0:
                    continue
                if c0 >= NP:
                    samp_some(NS // 4)
                for o in range(KC):
                    if o == 4 and c0 < NP:
                        samp_some(1)
                    pd = 4 + (n_pd % 2)
                    n_pd += 1

                    def mm(e, c0=c0, w=w, pd=pd, o=o):
                        ins = None
                        for kc in range(KC):
                            ins = e.matmul(bank(pd, 0, w), WOUT[:, kc, o * 128:(o + 1) * 128], HIN[:, kc, c0:c0 + w], start=(kc == 0),
                                           stop=(kc == KC - 1))
                        return ins
                    P.op("pe", mm, reads=[B_wout[o // 4]] + hin_bufs(c0, w), writes=[B_ps[pd]])
                    resid_evac(bi, o, bank(pd, 0, w), [B_ps[pd]], 1)
                    ln_stats_feed(bi, o, blk_done)
            ln_flush()
            ln_apply(NB - 1, True)

        for l in range(L):
            for (dst, src) in [(o_sc[l, :, 0, :], d_sconv[l, :, 1, :]), (o_spl[l, :, 0:14, :], d_spool[l, :, 1:15, :]),
                               (o_sk[l, :, 0:127, :], d_ck[l, :, 1:128, :]), (o_sv[l, :, 0:127, :], d_cv[l, :, 1:128, :])]:
                P.dma("sp", lambda e, dst=dst, src=src: e.dma_start(out=dst, in_=src), "dd", reads=[B_out])

        cv0 = Carver()
        NR0 = 8
        ada["ring"] = [cv0.take([KC, 128], BF16) for _ in range(NR0)]
        ada["bufs"] = [abuf(f"adar{i}") for i in range(NR0)]
        ada["nslot"] = NR0
        phase_switch(ada["bufs"])
        ada_load_bias(0)
        for j in range(24):
            ada_item(0, j)
        ada["nslot"] = 2
        ada["n"] = 0
        M = Mt[0]
        P.op("dve", lambda e: e.tensor_scalar(out=lnc[:, 2, :], in0=M[:, KC:2 * KC, 0], scalar1=1.0, scalar2=None, op0=ALU.add),
             reads=[B_M[0]], writes=[B_lnc])
        P.op("dve", lambda e: e.tensor_scalar(out=lncs[:, 0, :, :], in0=M[:, KC:2 * KC, 1:17], scalar1=1.0, scalar2=None, op0=ALU.add),
             reads=[B_M[0]], writes=[B_lnc])
        for bi, (c0, w) in enumerate(FBLK):
            for c in range(KC):
                xs = X[:, c, c0:c0 + w]
                hs = HIN[:, c, c0:c0 + w]
                hb = hin_bufs(c0, w, [c])
                if c0 < NP:
                    P.op("dve", lambda e, xs=xs, hs=hs, c=c: e.tensor_scalar(out=hs, in0=xs, scalar1=lnc[:, 2, c:c + 1], scalar2=M[:, c, 0:1],
                                                                              op0=ALU.mult, op1=ALU.add),
                         reads=[B_X[c][bi], B_lnc, B_M[0]], writes=hb)
                else:
                    P.op("dve", lambda e, xs=xs, c=c: e.tensor_tensor(out=YS[:, c, :], in0=xs, in1=lncs[:, 0, c, :], op=ALU.mult),
                         reads=[B_X[c][bi], B_lnc], writes=[B_samp])
                    P.op("dve", lambda e, hs=hs, c=c: e.tensor_tensor(out=hs, in0=YS[:, c, :], in1=M[:, c, 1:17], op=ALU.add),
                         reads=[B_samp, B_M[0]], writes=hb)
                P.op("act", lambda e, xs=xs: act_id(e, xs, xs, ALPHA), reads=[B_X[c][bi]] + hb, writes=[B_X[c][bi]])

        for l in range(L):
            last = (l == L - 1)
            cs0 = min(128 * l, HALO)
            cs1 = min(128 * (l + 1), HALO)
            ffn_phase(l, 0, 0 if l == 0 else None, 0, (l % 2, 3, 4), cs0, ada_list=list(range(24, 72)) if l == 0 else None)
            mixer_A(l, cs0)
            mixer_B(l, cs1)
            ffn_phase(l, 1, None if last else l + 1, 2, None if last else ((l + 1) % 2, 0, 1), cs1, final=last)

        for c in range(KC):
            P.dma("sp", lambda e, c=c: e.dma_start(out=o_x[:, c, :], in_=X[:, c, HALO:NT]), f"ox{c}",
                  reads=[B_X[c][b] for b in range(NB)] + [B_out])
        P.wait_all("sp", [B_out])
        P.emit()
    return nc


def _shared_weights(inp, L):
    f = lambda a: np.ascontiguousarray(a, dtype=np.float32)
    w = {}
    w["wada"] = f(inp["w_ada"].reshape(L, KC, 128, 72, 128).transpose(0, 3, 2, 1, 4))
    w["bada"] = f(inp["b_ada"].reshape(L, 72, 128).transpose(0, 2, 1))
    for i, (g, u, d) in enumerate([("ffn1_gate", "ffn1_up", "ffn1_down"), ("ffn2_gate", "ffn2_up", "ffn2_down")]):
        gg = inp[g].reshape(L, KC, 128, MC, 128).transpose(0, 3, 2, 1, 4)
        uu = inp[u].reshape(L, KC, 128, MC, 128).transpose(0, 3, 2, 1, 4)
        w[f"wgu{i + 1}"] = f(np.stack([gg, uu], axis=3).reshape(L, MC, 128, 2 * KC * 128))
        w[f"wd{i + 1}"] = f(inp[d].reshape(L, MC, 128, D))
    W = inp["w_in"]
    K0, K1, V0, V1 = W[:, :, 1536:1600], W[:, :, 1600:1664], W[:, :, 1664:1728], W[:, :, 1728:1792]
    ext = np.concatenate([W[:, :, 0:1536], K0, K0, K1, K1, V0, V0, V1, V1], axis=2)
    w["win"] = f(ext.reshape(L, KC, 128, 4, 512).transpose(0, 3, 2, 1, 4))
    w["wout"] = f(inp["w_out"].reshape(L, KC, 128, 2, 512).transpose(0, 3, 2, 1, 4))
    w["lng"] = f(inp["ln_g"].reshape(L, 3, KC, 128).transpose(3, 0, 1, 2))
    w["lnb"] = f(inp["ln_b"].reshape(L, 3, KC, 128).transpose(3, 0, 1, 2))
    w["convw"] = f(inp["conv_w"].reshape(L, 3, 2, 128).transpose(3, 0, 1, 2))
    pw = np.zeros((L, 2, 128, 128), np.float32)
    for cc in range(2):
        pw[:, cc, 0:64, 0:64] = inp["pool_w"][:, 2 * cc]
        pw[:, cc, 64:128, 64:128] = inp["pool_w"][:, 2 * cc + 1]
    w["poolw"] = pw
    w["pscale"] = f(inp["pool_scale"].reshape(L, 2, 128).transpose(2, 0, 1))
    w["mixg"] = f(inp["mix_norm_g"].reshape(L, KC, 128).transpose(2, 0, 1))
    sk = np.zeros((128, L, 4), np.float32)
    for c in range(4):
        sk[0:64, :, c] = inp["attn_sinks"][:, 2 * c][None, :]
        sk[64:128, :, c] = inp["attn_sinks"][:, 2 * c + 1][None, :]
    w["sinks"] = sk
    iw = np.zeros((128, 2), np.float32)
    for cc in range(2):
        iw[0:64, cc] = 1.0 / (2 ** (2 * cc + 1))
        iw[64:128, cc] = 1.0 / (2 ** (2 * cc + 2))
    w["invw"] = iw
    w["ident"] = np.eye(16, dtype=np.float32)
    w["identb"] = np.eye(128, dtype=np.float32)
    w["sinks4"] = f(np.broadcast_to(inp["attn_sinks"][None, :, :], (128, L, 8)))
    return w


def _core_inputs(cfg, inp, core, shared):
    L, OWN, HALO, NP, NT = cfg.L, cfg.OWN, cfg.HALO, cfg.NP, cfg.NT
    f = lambda a: np.ascontiguousarray(a, dtype=np.float32)
    nseg = inp["x_prompt"].shape[1] // OWN
    b, seg = core // nseg, core % nseg
    s0 = seg * OWN
    idx = np.arange(s0 - HALO, s0 + OWN)
    ok = idx >= 0
    xall = np.zeros((NT, D), np.float32)
    xall[:NP][ok] = inp["x_prompt"][b, idx[ok]]
    sl = slice(NS * core, NS * core + NS)
    xall[NP:] = inp["x_sample"][sl, 0, :]
    m = dict(shared)
    m["xT"] = f(xall.T.reshape(KC, 128, NT).transpose(1, 0, 2))
    call = np.concatenate([inp["c_prompt"][b:b + 1], inp["c_sample"][sl]], axis=0)
    m["cT"] = f(call.T.reshape(KC, 128, 17).transpose(1, 0, 2))
    ic = np.zeros((128, 2, 16), np.float32)
    for cc in range(2):
        for half in range(2):
            wd_ = 2 ** (2 * cc + half + 1)
            pos = np.arange(16)
            cnt = np.minimum(pos + 1, wd_) if seg == 0 else np.full(16, wd_)
            ic[half * 64:half * 64 + 64, cc, :] = (1.0 / cnt)[None, :]
    m["invcnt"] = ic
    m["valid"] = np.full((128, 1), 0.0 if seg == 0 else 1.0, np.float32)
    i = np.arange(128)[:, None]
    t = np.arange(128)[None, :]
    mk = np.zeros((128, 3, 128), np.float32)
    NEG = -30000.0
    mk[:, 0, :] = np.where(i > t, 0.0, NEG)
    mk[:, 1, :] = np.where(i <= t, 0.0, NEG)
    mk[:, 2, :] = np.where(i > t, 0.0, NEG) if seg != 0 else NEG
    m["masks"] = mk
    cr = np.zeros((128, 2, 16), np.float32)
    for cc in range(2):
        for half in range(2):
            wd_ = 2 ** (2 * cc + half + 1)
            pos = np.arange(16)
            cnt = np.minimum(pos + 1, wd_) if seg == 0 else np.full(16, wd_)
            cr[half * 64:half * 64 + 64, cc, :] = (wd_ / cnt)[None, :]
    m["cntr"] = cr
    ck = inp["cache_k_win"][:, sl]
    cvv = inp["cache_v_win"][:, sl]
    kt = ck.transpose(0, 1, 3, 4, 2)
    kz = np.zeros((L, NS, 128, 2, 2, 128), np.float32)
    for h in range(2):
        kz[:, :, 0:64, h, 0, :] = kt[:, :, h]
        kz[:, :, 64:128, h, 1, :] = kt[:, :, h]
    m["kTz"] = kz.reshape(L, NS, 128, 512)
    m["vd"] = f(np.concatenate([cvv, cvv], axis=-1).reshape(L, NS, 128, 256))
    m["ck"] = f(ck.reshape(L, NS, 128, 128))
    m["cv"] = f(cvv.reshape(L, NS, 128, 128))
    sc = inp["state_conv"][:, sl]
    m["sconvT"] = f(sc.reshape(L, NS, 2, 2, 128).transpose(0, 4, 3, 2, 1))
    sp = inp["state_pool"][:, sl]
    spt = np.zeros((L, 128, 2, NS, 16), np.float32)
    spt[..., 0:15] = sp.reshape(L, NS, 15, 2, 128).transpose(0, 4, 3, 1, 2)
    m["spoolT"] = spt
    m["sconv"] = f(sc)
    m["spool"] = f(sp)
    return m


_PROG_CACHE = {}


def kernel(**inputs):
    inp = {k: np.asarray(v) for k, v in inputs.items()}
    L = inp["w_ada"].shape[0]
    B, SEQ = inp["x_prompt"].shape[:2]
    ncores = 8
    nseg = ncores // B
    OWN = SEQ // nseg
    HALO = max(TB, ((128 * L + TB - 1) // TB) * TB)
    cfg = Cfg(L, OWN, HALO)
    key = (L, OWN, HALO)
    if key not in _PROG_CACHE:
        _PROG_CACHE[key] = build_program(cfg)
    nc = _PROG_CACHE[key]
    shared = _shared_weights(inp, L)
    in_maps = [_core_inputs(cfg, inp, c, shared) for c in range(ncores)]
    res = run_bass_kernel_spmd(nc, in_maps, core_ids=list(range(ncores)))
    R = res.results
    NSB = inp["x_sample"].shape[0]
    y_prompt = np.zeros((B, SEQ, D), np.float32)
    y_sample = np.zeros((NSB, 1, D), np.float32)
    p_conv = np.zeros((L, B, 2, 256), np.float32)
    p_pool = np.zeros((L, B, 15, 256), np.float32)
    p_k = np.zeros((L, B, 128, 2, 64), np.float32)
    p_v = np.zeros((L, B, 128, 2, 64), np.float32)
    s_conv = np.zeros((L, NSB, 2, 256), np.float32)
    s_pool = np.zeros((L, NSB, 15, 256), np.float32)
    s_k = np.zeros((L, NSB, 128, 2, 64), np.float32)
    s_v = np.zeros((L, NSB, 128, 2, 64), np.float32)
    for c in range(ncores):
        b, seg = c // nseg, c % nseg
        r = R[c]
        yt = np.asarray(r["o_x"]).transpose(2, 1, 0).reshape(OWN + NS, D)
        y_prompt[b, seg * OWN:(seg + 1) * OWN] = yt[:OWN]
        sl = slice(NS * c, NS * c + NS)
        y_sample[sl, 0] = yt[OWN:]
        if seg == nseg - 1:
            p_conv[:, b] = r["o_pc"]
            p_pool[:, b] = r["o_pp"]
            p_k[:, b] = np.asarray(r["o_pk"]).reshape(L, 128, 2, 64)
            p_v[:, b] = np.asarray(r["o_pv"]).reshape(L, 128, 2, 64)
        s_conv[:, sl] = r["o_sc"]
        s_pool[:, sl] = r["o_spl"]
        s_k[:, sl] = np.asarray(r["o_sk"]).reshape(L, NS, 128, 2, 64)
        s_v[:, sl] = np.asarray(r["o_sv"]).reshape(L, NS, 128, 2, 64)
    return (y_prompt, y_sample, p_conv, p_pool, p_k, p_v, s_conv, s_pool, s_k, s_v)
```

```python
import contextlib
import numpy as np
import concourse.bass as bass
import concourse.mybir as mybir
from concourse.bass_utils import run_bass_kernel_spmd

F32 = mybir.dt.float32
BF16 = mybir.dt.bfloat16
AF = mybir.ActivationFunctionType
ALU = mybir.AluOpType

QUEUES = ("pe", "act", "dve", "pool", "sp")


class Buf:
    __slots__ = ("name", "last_w", "reads")

    def __init__(self, name):
        self.name = name
        self.last_w = None
        self.reads = []


class Op:
    __slots__ = ("fn", "waits", "event", "inc")

    def __init__(self, fn, waits, event, inc):
        self.fn, self.waits, self.event, self.inc = fn, waits, event, inc


class Prog:
    def __init__(self, nc):
        self.nc = nc
        self.ops = {q: [] for q in QUEUES}
        self.count = {}
        self.known = {q: {} for q in QUEUES}
        self.dma_sems = []

    def _deps(self, queue, reads, writes):
        need = {}

        def add(ev):
            if ev is None:
                return
            k, v = ev
            if queue == "pe" and k == "q_pe":
                return
            if need.get(k, 0) < v:
                need[k] = v

        for b in reads:
            add(b.last_w)
        for b in writes:
            add(b.last_w)
            for ev in b.reads:
                add(ev)
        waits = []
        kn = self.known[queue]
        for k, v in need.items():
            if kn.get(k, 0) >= v:
                continue
            kn[k] = v
            waits.append((k, v))
        return waits

    @staticmethod
    def _commit(ev, reads, writes):
        for b in reads:
            b.reads.append(ev)
        for b in writes:
            b.last_w = ev
            b.reads = []

    def op(self, queue, fn, reads=(), writes=()):
        waits = self._deps(queue, reads, writes)
        k = "q_" + queue
        v = self.count.get(k, 0) + 1
        self.count[k] = v
        ev = (k, v)
        self.ops[queue].append(Op(fn, waits, ev, 1))
        self._commit(ev, reads, writes)
        return ev

    def dma(self, queue, fn, semkey, reads=(), writes=()):
        waits = self._deps(queue, reads, writes)
        if semkey not in self.count:
            self.dma_sems.append(semkey)
        v = self.count.get(semkey, 0) + 16
        self.count[semkey] = v
        ev = (semkey, v)
        self.ops[queue].append(Op(fn, waits, ev, 16))
        self._commit(ev, reads, writes)
        return ev

    def wait_all(self, queue, bufs):
        waits = self._deps(queue, (), bufs)
        self.ops[queue].append(Op(None, waits, None, 0))

    def emit(self):
        nc = self.nc
        keys = ["q_" + q for q in QUEUES if ("q_" + q) in self.count] + self.dma_sems
        with contextlib.ExitStack() as st:
            sems = {k: st.enter_context(nc.semaphore("s_" + k)) for k in keys}
            block = st.enter_context(nc.Block())

            def run(queue):
                def body(eng):
                    for o in self.ops[queue]:
                        for (k, v) in o.waits:
                            eng.wait_ge(sems[k], v)
                        if o.fn is None:
                            continue
                        ins = o.fn(eng)
                        if o.event is not None:
                            ins.then_inc(sems[o.event[0]], o.inc)
                return body

            if self.ops["sp"]:
                block.sync(run("sp"))
            if self.ops["pe"]:
                block.tensor(run("pe"))
            if self.ops["act"]:
                block.scalar(run("act"))
            if self.ops["dve"]:
                block.vector(run("dve"))
            if self.ops["pool"]:
                block.gpsimd(run("pool"))


D = 1024
KC = 8
DFF = 2816
MC = 22
NS = 16
TB = 256
FB = 512
WIN = 128
NMOD = 9
LN_EPS = 1e-5
RMS_EPS = 1e-6
WEXT = 2048
GROUPS = (4, 4, 4, 5, 5)
GMAX = 5


class Cfg:
    def __init__(self, L, OWN, HALO, depth_full=4):
        self.L, self.OWN, self.HALO = L, OWN, HALO
        self.NP = HALO + OWN
        self.NT = self.NP + NS
        assert self.NP % FB == 0 and HALO % TB == 0
        self.NFB = self.NP // FB
        self.NTB = self.NP // TB
        self.ALPHA = float((2.0 * depth_full) ** 0.25)


ARENA = 67584


def build_program(cfg):
    L, NP, NT, HALO, OWN = cfg.L, cfg.NP, cfg.NT, cfg.HALO, cfg.OWN
    NFB, NTB, ALPHA = cfg.NFB, cfg.NTB, cfg.ALPHA
    nc = bass.Bass("TRN2", target_bir_lowering=False)
    P = Prog(nc)

    def din(name, shape):
        return nc.dram_tensor(name, list(shape), F32, kind="ExternalInput").ap()

    def dout(name, shape):
        return nc.dram_tensor(name, list(shape), F32, kind="ExternalOutput").ap()

    d_xT = din("xT", [128, KC, NT])
    d_cT = din("cT", [128, KC, 17])
    d_wada = din("wada", [L, 72, 128, KC, 128])
    d_bada = din("bada", [L, 128, 72])
    d_wgu = [din("wgu1", [L, MC, 128, 2 * KC * 128]), din("wgu2", [L, MC, 128, 2 * KC * 128])]
    d_wd = [din("wd1", [L, MC, 128, D]), din("wd2", [L, MC, 128, D])]
    d_win = din("win", [L, 4, 128, KC, 512])
    d_wout = din("wout", [L, 2, 128, KC, 512])
    d_lng = din("lng", [128, L, 3, KC])
    d_lnb = din("lnb", [128, L, 3, KC])
    d_convw = din("convw", [128, L, 3, 2])
    d_poolw = din("poolw", [L, 2, 128, 128])
    d_pscale = din("pscale", [128, L, 2])
    d_mixg = din("mixg", [128, L, KC])
    d_sinks = din("sinks", [128, L, 4])
    d_invcnt = din("invcnt", [128, 2, 16])
    d_invw = din("invw", [128, 2])
    d_valid = din("valid", [128, 1])
    d_masks = din("masks", [128, 3, 128])
    d_ident = din("ident", [16, 16])
    d_identb = din("identb", [128, 128])
    d_sinks4 = din("sinks4", [128, L, 8])
    d_cntr = din("cntr", [128, 2, 16])
    d_kTz = din("kTz", [L, NS, 128, 512])
    d_vd = din("vd", [L, NS, 128, 256])
    d_ck = din("ck", [L, NS, 128, 128])
    d_cv = din("cv", [L, NS, 128, 128])
    d_sconvT = din("sconvT", [L, 128, 2, 2, NS])
    d_spoolT = din("spoolT", [L, 128, 2, NS, 16])
    d_sconv = din("sconv", [L, NS, 2, 256])
    d_spool = din("spool", [L, NS, 15, 256])

    o_x = dout("o_x", [128, KC, OWN + NS])
    o_pc = dout("o_pc", [L, 2, 256])
    o_pp = dout("o_pp", [L, 15, 256])
    o_pk = dout("o_pk", [L, 128, 128])
    o_pv = dout("o_pv", [L, 128, 128])
    o_sc = dout("o_sc", [L, NS, 2, 256])
    o_spl = dout("o_spl", [L, NS, 15, 256])
    o_sk = dout("o_sk", [L, NS, 128, 128])
    o_sv = dout("o_sv", [L, NS, 128, 128])
    B_out = Buf("outputs")
    n_out = [0]

    def out_dma(dst, src, reads):
        k = f"od{n_out[0] % 8}"
        n_out[0] += 1
        P.dma("sp", lambda e: e.dma_start(out=dst, in_=src), k, reads=reads + [B_out], writes=[])
        return k

    def act_id(e, out, in_, scale=1.0, bias=0.0):
        return e.activation(out=out, in_=in_, func=AF.Prelu, scale=scale, bias=bias, alpha=1.0)

    st = contextlib.ExitStack()
    with st:
        def sb(name, shape, dt=F32):
            return st.enter_context(nc.sbuf_tensor("s_" + name, list(shape), dt))

        X = sb("X", [128, KC, NT])
        HIN = sb("HIN", [128, KC, NT], BF16)
        Mt = [sb("M0", [128, 72, 17]), sb("M1", [128, 72, 17])]
        GT = sb("GT", [128, 3, KC, 17])
        SC_ = sb("siluc", [128, KC, 17], BF16)
        cTs = sb("cTs", [128, KC, 17])
        bada = sb("bada", [128, 72])
        lng = sb("lng", [128, L, 3, KC])
        lnb = sb("lnb", [128, L, 3, KC])
        convw = sb("convw", [128, L, 3, 2])
        pscale = sb("pscale", [128, L, 2])
        mixg = sb("mixg", [128, L, KC])
        sinke = sb("sinke", [128, L, 4])
        invw = sb("invw", [128, 2])
        valid = sb("valid", [128, 1])
        masks = sb("masks", [128, 3, 128], BF16)
        ident = sb("ident", [16, 16])
        ones = sb("ones", [128, 128], BF16)
        identb = sb("identb", [128, 128], BF16)
        sinke4 = sb("sinke4", [128, L, 8])
        cntr = sb("cntr", [128, 2, 16])
        poolw = sb("poolw", [128, L, 2, 128], BF16)
        lnc = sb("lnc", [128, 4, KC])
        lncs = sb("lncs", [128, 2, KC, NS])
        epsT = sb("epsT", [128, 2])
        VtokS = sb("VtokS", [16, 256])
        QS = sb("QS", [128, 4, NS], BF16)
        KZS = sb("KZS", [128, 4, NS], BF16)
        YS = sb("YS", [128, KC, NS])
        arena = sb("arena", [128, ARENA], mybir.dt.uint8)

        class Carver:
            def __init__(self, off=0):
                self.off = off

            def take(self, shape, dt):
                esz = 2 if dt == BF16 else 4
                n = int(np.prod(shape))
                self.off = (self.off + 31) // 32 * 32
                a = arena[:, self.off:self.off + n * esz].bitcast(dt)
                self.off += n * esz
                assert self.off <= ARENA, (self.off, ARENA)
                if len(shape) == 1:
                    return a
                names = " ".join(f"d{i}" for i in range(len(shape)))
                kw = {f"d{i}": int(s) for i, s in enumerate(shape)}
                return a.rearrange(f"p ({names}) -> p {names}", **kw)

        psum = st.enter_context(nc.psum_tensor("psum", [128, 8 * 512], F32))

        def bank(i, lo=0, hi=512, p0=0, p1=128):
            return psum[p0:p1, i * 512 + lo:i * 512 + hi]

        B_ps = [Buf(f"ps{i}") for i in range(8)]
        B_hb = [[B_ps[i], B_ps[i]] for i in range(2)]

        FBLK = [(i * FB, FB) for i in range(NFB)] + [(NP, NS)]
        NB = len(FBLK)

        def set_start(cs):
            for i in range(NFB):
                lo = max(i * FB, cs)
                FBLK[i] = (lo, max(0, (i + 1) * FB - lo))
        B_X = [[Buf(f"X{c}_{b}") for b in range(NB)] for c in range(KC)]
        B_H = [[Buf(f"H{c}_{t}") for t in range(NTB + 1)] for c in range(KC)]

        def hin_bufs(c0, w, chunks=range(KC)):
            if c0 >= NP:
                ts = [NTB]
            else:
                ts = list(range(c0 // TB, (c0 + w - 1) // TB + 1))
            return [B_H[c][t] for c in chunks for t in ts]

        B_M = [Buf("M0"), Buf("M1")]
        B_GT = Buf("GT")
        B_const = Buf("const")
        B_lnc = Buf("lnc")
        B_samp = Buf("samp_persist")
        arena_bufs = []

        def abuf(name):
            return Buf(name)

        def phase_switch(new_bufs, keep=()):
            mx = {}
            for b in arena_bufs:
                if b in keep:
                    continue
                evs = list(b.reads)
                if b.last_w is not None:
                    evs.append(b.last_w)
                for k, v in evs:
                    if mx.get(k, 0) < v:
                        mx[k] = v
            evl = list(mx.items())
            for b in new_bufs:
                b.last_w = None
                b.reads = list(evl)
            del arena_bufs[:]
            arena_bufs.extend(list(keep) + list(new_bufs))

        def ld(dst, src, key, queue="sp"):
            P.dma(queue, lambda e: e.dma_start(out=dst, in_=src), key, writes=[B_const])

        for (dst, src, key) in [(cTs[:], d_cT, "c0"), (lng[:], d_lng, "c2"),
                                (lnb[:], d_lnb, "c3"), (convw[:], d_convw, "c4"), (pscale[:], d_pscale, "c5"),
                                (mixg[:], d_mixg, "c6"), (sinke[:], d_sinks, "c7"),
                                (invw[:], d_invw, "c9"), (valid[:], d_valid, "c10"), (ident[:], d_ident, "c11")]:
            ld(dst, src, key)
        ld(masks[:], d_masks, "c12", "pool")
        ld(identb[:], d_identb, "c14", "pool")
        ld(sinke4[:], d_sinks4, "c15")
        ld(cntr[:], d_cntr, "c16")
        for l in range(L):
            ld(poolw[:, l, :, :], d_poolw[l].rearrange("c p n -> p c n"), f"c13_{l}", "pool")
        for c in range(KC):
            P.dma("sp", lambda e, c=c: e.dma_start(out=X[:, c, :], in_=d_xT[:, c, :]), f"xl{c}",
                  writes=[B_X[c][b] for b in range(NB)])
        P.op("dve", lambda e: e.memset(ones[:], 1.0), writes=[B_const])
        P.op("dve", lambda e: e.memset(epsT[:, 0:1], LN_EPS), writes=[B_const])
        P.op("dve", lambda e: e.memset(epsT[:, 1:2], RMS_EPS), writes=[B_const])
        P.op("dve", lambda e: e.memset(KZS[:], 0.0), writes=[B_samp])
        P.op("act", lambda e: e.activation(out=SC_[:], in_=cTs[:], func=AF.Silu), reads=[B_const], writes=[B_const])
        P.op("act", lambda e: e.activation(out=sinke[:], in_=sinke[:], func=AF.Exp), reads=[B_const], writes=[B_const])
        P.op("act", lambda e: e.activation(out=sinke4[:], in_=sinke4[:], func=AF.Exp), reads=[B_const], writes=[B_const])

        ada = {"ring": None, "bufs": None, "n": 0}
        B_bada = Buf("bada")

        def ada_load_bias(l):
            P.dma("sp", lambda e: e.dma_start(out=bada[:], in_=d_bada[l]), "c1", writes=[B_bada])

        def ada_item(l, j):
            slot = ada["n"] % ada.get("nslot", 2)
            ada["n"] += 1
            tile = ada["ring"][slot]
            bslot = ada["bufs"][slot]
            P.dma("pool", lambda e: e.dma_start(out=tile, in_=d_wada[l, j]), f"ada{slot}", writes=[bslot])

            def mm(e):
                ins = None
                for kc in range(KC):
                    ins = e.matmul(bank(7, 0, 17), tile[:, kc, :], SC_[:, kc, :], start=(kc == 0), stop=(kc == KC - 1))
                return ins
            P.op("pe", mm, reads=[bslot, B_const], writes=[B_ps[7]])
            P.op("dve", lambda e: e.tensor_scalar(out=Mt[l % 2][:, j, :], in0=bank(7, 0, 17), scalar1=bada[:, j:j + 1],
                                                   scalar2=None, op0=ALU.add),
                 reads=[B_ps[7], B_bada], writes=[B_M[l % 2]])

        def gates(l, i):
            M = Mt[l % 2]
            row, f = [(2, 0.5), (5, 1.0), (8, 0.5)][i]
            P.op("dve", lambda e: e.tensor_scalar(out=GT[:, i, :, :], in0=M[:, row * KC:(row + 1) * KC, :],
                                                   scalar1=f, scalar2=None, op0=ALU.mult),
                 reads=[B_M[l % 2]], writes=[B_GT])

        def ln_consts(l, k, nxt, final=False):
            a = 1.0 if final else ALPHA
            rd = [B_const]
            P.op("dve", lambda e: e.tensor_scalar(out=lnc[:, 0, :], in0=lng[:, l, k, :], scalar1=a, scalar2=None, op0=ALU.mult),
                 reads=rd, writes=[B_lnc])
            P.op("dve", lambda e: e.tensor_scalar(out=lnc[:, 1, :], in0=lnb[:, l, k, :], scalar1=a, scalar2=None, op0=ALU.mult),
                 reads=rd, writes=[B_lnc])
            if nxt is None:
                return
            mi, shr, scr = nxt
            M = Mt[mi]
            rd = [B_const, B_M[mi]]
            P.op("dve", lambda e: e.scalar_tensor_tensor(out=lnc[:, 2, :], in0=M[:, scr * KC:(scr + 1) * KC, 0], scalar=1.0,
                                                          in1=lng[:, l, k, :], op0=ALU.add, op1=ALU.mult),
                 reads=rd, writes=[B_lnc])
            P.op("dve", lambda e: e.scalar_tensor_tensor(out=lnc[:, 3, :], in0=M[:, scr * KC:(scr + 1) * KC, 0], scalar=1.0,
                                                          in1=lnb[:, l, k, :], op0=ALU.add, op1=ALU.mult),
                 reads=rd, writes=[B_lnc])
            P.op("dve", lambda e: e.tensor_tensor(out=lnc[:, 3, :], in0=lnc[:, 3, :], in1=M[:, shr * KC:(shr + 1) * KC, 0], op=ALU.add),
                 reads=rd + [B_lnc], writes=[B_lnc])
            gb_ = lng[:, l, k, :].unsqueeze(2).to_broadcast([128, KC, NS])
            bb_ = lnb[:, l, k, :].unsqueeze(2).to_broadcast([128, KC, NS])
            P.op("dve", lambda e: e.scalar_tensor_tensor(out=lncs[:, 0, :, :], in0=M[:, scr * KC:(scr + 1) * KC, 1:17], scalar=1.0,
                                                          in1=gb_, op0=ALU.add, op1=ALU.mult),
                 reads=rd, writes=[B_lnc])
            P.op("dve", lambda e: e.scalar_tensor_tensor(out=lncs[:, 1, :, :], in0=M[:, scr * KC:(scr + 1) * KC, 1:17], scalar=1.0,
                                                          in1=bb_, op0=ALU.add, op1=ALU.mult),
                 reads=rd, writes=[B_lnc])
            P.op("dve", lambda e: e.tensor_tensor(out=lncs[:, 1, :, :], in0=lncs[:, 1, :, :], in1=M[:, shr * KC:(shr + 1) * KC, 1:17], op=ALU.add),
                 reads=rd + [B_lnc], writes=[B_lnc])

        lnS = {}

        NRING = 3

        def carve_ln(cv):
            lnS.clear()
            lnS["n"] = 0
            lnS["pend"] = []
            lnS["vb"] = [cv.take([FB], BF16) for _ in range(NRING)]
            lnS["vq"] = [cv.take([FB], BF16) for _ in range(NRING)]
            lnS["Bvb"] = [abuf(f"vb{i}") for i in range(NRING)]
            lnS["Bvq"] = [abuf(f"vq{i}") for i in range(NRING)]
            lnS["rstd"] = [cv.take([FB], F32) for _ in range(2)]
            lnS["nmr"] = [cv.take([FB], F32) for _ in range(2)]
            lnS["Brs"] = [abuf("rstd0"), abuf("rstd1")]
            lnS["Bnm"] = [abuf("nmr0"), abuf("nmr1")]
            lnS["sscr"] = cv.take([NS], F32)
            lnS["Bsscr"] = abuf("sscr")
            return lnS["Bvb"] + lnS["Bvq"] + lnS["Brs"] + lnS["Bnm"] + [lnS["Bsscr"]]

        def ln_stats_feed(bi, c, on_block_done):
            c0, w = FBLK[bi]
            r = lnS["n"] % NRING
            lnS["n"] += 1
            vb, vq = lnS["vb"][r], lnS["vq"][r]
            Bvb, Bvq = lnS["Bvb"][r], lnS["Bvq"][r]
            P.op("act", lambda e: act_id(e, out=vb[:, :w], in_=X[:, c, c0:c0 + w]), reads=[B_X[c][bi]], writes=[Bvb])
            P.op("act", lambda e: e.activation(out=vq[:, :w], in_=X[:, c, c0:c0 + w], func=AF.Square), reads=[B_X[c][bi]], writes=[Bvq])

            def pe_part():
                def mm(e):
                    e.matmul(bank(6, 0, w), ones[:], vb[:, :w], start=(c == 0), stop=(c == KC - 1))
                    return e.matmul(bank(7, 0, w), ones[:], vq[:, :w], start=(c == 0), stop=(c == KC - 1))
                P.op("pe", mm, reads=[Bvb, Bvq, B_const], writes=[B_ps[6], B_ps[7]])
                if c == KC - 1:
                    on_block_done(bi)
            lnS["pend"].append(pe_part)
            if len(lnS["pend"]) > 2:
                lnS["pend"].pop(0)()

        def ln_flush():
            while lnS["pend"]:
                lnS["pend"].pop(0)()

        def ln_math(bi):
            c0, w = FBLK[bi]
            q = bi % 2
            rstd, nmr = lnS["rstd"][q], lnS["nmr"][q]
            Brs, Bnm = lnS["Brs"][q], lnS["Bnm"][q]
            inv = 1.0 / D
            P.op("act", lambda e: e.activation(out=rstd[:, :w], in_=bank(6, 0, w), func=AF.Square, scale=inv), reads=[B_ps[6]], writes=[Brs])
            P.op("dve", lambda e: e.scalar_tensor_tensor(out=rstd[:, :w], in0=bank(7, 0, w), scalar=inv, in1=rstd[:, :w],
                                                          op0=ALU.mult, op1=ALU.subtract),
                 reads=[B_ps[7], Brs], writes=[Brs])
            P.op("act", lambda e: e.activation(out=rstd[:, :w], in_=rstd[:, :w], func=AF.Ln, bias=epsT[:, 0:1]), reads=[Brs, B_const], writes=[Brs])
            P.op("act", lambda e: e.activation(out=rstd[:, :w], in_=rstd[:, :w], func=AF.Exp, scale=-0.5), reads=[Brs], writes=[Brs])
            P.op("dve", lambda e: e.scalar_tensor_tensor(out=nmr[:, :w], in0=bank(6, 0, w), scalar=-inv, in1=rstd[:, :w],
                                                          op0=ALU.mult, op1=ALU.mult),
                 reads=[B_ps[6], Brs], writes=[Bnm])

        def ln_apply(bi, with_hin, emit_out=False):
            c0, w = FBLK[bi]
            samp = (c0 >= NP)
            q = bi % 2
            rstd, nmr = lnS["rstd"][q], lnS["nmr"][q]
            Brs, Bnm = lnS["Brs"][q], lnS["Bnm"][q]
            tmp, Btmp = lnS["sscr"], lnS["Bsscr"]
            for c in range(KC):
                xs = X[:, c, c0:c0 + w]
                Bx = B_X[c][bi]
                P.op("dve", lambda e, xs=xs: e.tensor_tensor(out=xs, in0=xs, in1=rstd[:, :w], op=ALU.mult), reads=[Bx, Brs], writes=[Bx])
                P.op("dve", lambda e, xs=xs: e.tensor_tensor(out=xs, in0=xs, in1=nmr[:, :w], op=ALU.add), reads=[Bx, Bnm], writes=[Bx])
                if with_hin:
                    hs = HIN[:, c, c0:c0 + w]
                    hb = hin_bufs(c0, w, [c])
                    if not samp:
                        P.op("pool", lambda e, xs=xs, hs=hs, c=c: e.tensor_scalar(out=hs, in0=xs, scalar1=lnc[:, 2, c:c + 1],
                                                                                   scalar2=lnc[:, 3, c:c + 1], op0=ALU.mult, op1=ALU.add),
                             reads=[Bx, B_lnc], writes=hb)
                    else:
                        P.op("dve", lambda e, xs=xs, c=c: e.tensor_tensor(out=tmp[:, :w], in0=xs, in1=lncs[:, 0, c, :], op=ALU.mult),
                             reads=[Bx, B_lnc], writes=[Btmp])
                        P.op("dve", lambda e, hs=hs, c=c: e.tensor_tensor(out=hs, in0=tmp[:, :w], in1=lncs[:, 1, c, :], op=ALU.add),
                             reads=[Btmp, B_lnc], writes=hb)
                P.op("act", lambda e, xs=xs, c=c: act_id(e, xs, xs, lnc[:, 0, c:c + 1], lnc[:, 1, c:c + 1]),
                     reads=[Bx, B_lnc], writes=[Bx])
                if emit_out and c0 >= HALO:
                    P.dma("sp", lambda e, xs=xs, c=c: e.dma_start(out=o_x[:, c, c0 - HALO:c0 - HALO + w], in_=xs), f"ox{c}",
                          reads=[Bx, B_out])

        def resid_evac(bi, o, ps_ap, ps_bufs, gi):
            c0, w = FBLK[bi]
            xs = X[:, o, c0:c0 + w]
            if c0 < NP:
                P.op("dve", lambda e: e.scalar_tensor_tensor(out=xs, in0=ps_ap, scalar=GT[:, gi, o, 0:1], in1=xs,
                                                              op0=ALU.mult, op1=ALU.add),
                     reads=ps_bufs + [B_GT, B_X[o][bi]], writes=[B_X[o][bi]])
            else:
                t = lnS["sscr"]
                Bt = lnS["Bsscr"]
                P.op("dve", lambda e: e.tensor_tensor(out=t[:, :w], in0=ps_ap, in1=GT[:, gi, o, 1:17], op=ALU.mult),
                     reads=ps_bufs + [B_GT], writes=[Bt])
                P.op("dve", lambda e: e.tensor_tensor(out=xs, in0=t[:, :w], in1=xs, op=ALU.add),
                     reads=[Bt, B_X[o][bi]], writes=[B_X[o][bi]])

        prev_phase = ["init"]
        ffn_keep = {}

        def ffn_phase(l, which, ada_next, ln_k, nxt, cs, final=False, ada_list=None):
            set_start(cs)
            if prev_phase[0] == "ffn":
                HID, GU, DW, SG, B_hid, B_gu, B_dw, B_sg, ring, rbufs, lsave = ffn_keep["v"]
                ada["ring"], ada["bufs"] = ring, rbufs
                keepn, keepp = lnS["n"], lnS["pend"]
                lnS.clear()
                lnS.update(lsave)
                lnS["n"], lnS["pend"] = keepn, keepp
            else:
                cv = Carver()
                HID = cv.take([GMAX, NT], BF16)
                GU = [cv.take([2 * KC * 128], BF16) for _ in range(2)]
                DW = cv.take([GMAX, D], BF16)
                ada["ring"] = [cv.take([KC, 128], BF16) for _ in range(2)]
                SG = cv.take([FB], F32)
                lnb_ = carve_ln(cv)
                B_hid = [[abuf(f"hid{m}_{b}") for b in range(NB)] for m in range(GMAX)]
                B_gu = [abuf(f"gu{i}") for i in range(2)]
                B_dw = [abuf(f"dw{i}") for i in range(GMAX)]
                B_sg = abuf("sg")
                ada["bufs"] = [abuf("adar0"), abuf("adar1")]
                phase_switch([b for row in B_hid for b in row] + B_gu + B_dw + [B_sg] + ada["bufs"] + lnb_)
                ffn_keep["v"] = (HID, GU, DW, SG, B_hid, B_gu, B_dw, B_sg, ada["ring"], ada["bufs"], dict(lnS))
            prev_phase[0] = "ffn"

            gi = 0 if which == 0 else 2
            ada_js = (list(range(72)) if ada_list is None else list(ada_list)) if ada_next is not None else []
            ada_total = len(ada_js)
            if ada_next is not None and ada_list is None:
                ada_load_bias(ada_next)
            gates(l, gi)
            n_gu = 0
            n_ps = 0
            n_pd = 0
            m0 = 0
            ada_slot = [0]

            prevb = [None]

            def blk_done(bi):
                ln_math(bi)
                if prevb[0] is not None:
                    ln_apply(prevb[0], nxt is not None, final)
                prevb[0] = bi
            for g, gsz in enumerate(GROUPS):
                last = (g == len(GROUPS) - 1)
                for mi in range(gsz):
                    m = m0 + mi
                    slot = n_gu % 2
                    n_gu += 1
                    gut = GU[slot]
                    P.dma("pool", lambda e, gut=gut, m=m: e.dma_start(out=gut, in_=d_wgu[which][l, m]), f"gu{slot}", writes=[B_gu[slot]])
                    for bi, (c0, w) in enumerate(FBLK):
                        if w == 0:
                            continue
                        pg, pu = (0, 1) if n_ps % 2 == 0 else (2, 3)
                        n_ps += 1

                        def mm(e, gut=gut, c0=c0, w=w, pg=pg, pu=pu):
                            ins = None
                            for kc in range(KC):
                                ins = e.matmul(bank(pg, 0, w), gut[:, kc * 128:(kc + 1) * 128], HIN[:, kc, c0:c0 + w],
                                               start=(kc == 0), stop=(kc == KC - 1))
                            for kc in range(KC):
                                ins = e.matmul(bank(pu, 0, w), gut[:, (KC + kc) * 128:(KC + kc + 1) * 128], HIN[:, kc, c0:c0 + w],
                                               start=(kc == 0), stop=(kc == KC - 1))
                            return ins
                        P.op("pe", mm, reads=[B_gu[slot]] + hin_bufs(c0, w), writes=[B_ps[pg], B_ps[pu]])
                        P.op("act", lambda e, pg=pg, w=w: e.activation(out=SG[:, :w], in_=bank(pg, 0, w), func=AF.Silu),
                             reads=[B_ps[pg]], writes=[B_sg])
                        P.op("dve", lambda e, pu=pu, w=w, mi=mi, c0=c0: e.tensor_tensor(out=HID[:, mi, c0:c0 + w], in0=SG[:, :w],
                                                                                        in1=bank(pu, 0, w), op=ALU.mult),
                             reads=[B_sg, B_ps[pu]], writes=[B_hid[mi][bi]])
                        ada_slot[0] += 1
                        if ada_js and (ada_total - len(ada_js)) < (ada_slot[0] * ada_total) // 100:
                            ada_item(ada_next, ada_js.pop(0))
                if last:
                    while ada_js:
                        ada_item(ada_next, ada_js.pop(0))
                    ln_consts(l, ln_k, nxt, final)
                for mi in range(gsz):
                    m = m0 + mi
                    P.dma("pool", lambda e, mi=mi, m=m: e.dma_start(out=DW[:, mi, :], in_=d_wd[which][l, m]), f"dw{mi}", writes=[B_dw[mi]])
                for bi, (c0, w) in enumerate(FBLK):
                    if w == 0:
                        continue
                    for o in range(KC):
                        pd = 4 + (n_pd % 2)
                        n_pd += 1

                        def mm(e, c0=c0, w=w, pd=pd, o=o, gsz=gsz):
                            ins = None
                            for mi in range(gsz):
                                ins = e.matmul(bank(pd, 0, w), DW[:, mi, o * 128:(o + 1) * 128], HID[:, mi, c0:c0 + w],
                                               start=(mi == 0), stop=(mi == gsz - 1))
                            return ins
                        P.op("pe", mm, reads=B_dw[:gsz] + [B_hid[mi][bi] for mi in range(gsz)], writes=[B_ps[pd]])
                        resid_evac(bi, o, bank(pd, 0, w), [B_ps[pd]], gi)
                        if last:
                            ln_stats_feed(bi, o, blk_done)
                if last:
                    ln_flush()
                    ln_apply(NB - 1, nxt is not None, final)
                m0 += gsz
            while ada_js:
                ada_item(ada_next, ada_js.pop(0))

        def mixer_A(l, cs):
            prev_phase[0] = "A"
            cv = Carver()
            WIN_ = cv.take([KC, WEXT], BF16)
            off_after_win = cv.off
            CVx = cv.take([2, TB + 2], F32)
            GBf = cv.take([2 * TB], F32)
            GBs = GBf.rearrange("p (c t) -> p c t", c=2)
            TMP = [cv.take([TB], F32) for _ in range(2)]
            Ux = cv.take([2, TB + 15], F32)
            Sa = cv.take([TB + 15], F32)
            Sb = cv.take([TB + 15], F32)
            PL = cv.take([2, TB], BF16)
            Q = cv.take([4, TB], BF16)
            KZ = cv.take([4, TB + 128], BF16)
            Vd = [cv.take([256], BF16) for _ in range(3)]
            PT = cv.take([1024], BF16)
            RD = cv.take([4, 128], F32)
            Y = cv.take([KC, TB], F32)
            SQ = [cv.take([TB], BF16) for _ in range(2)]
            RSf = cv.take([3 * TB], F32)
            RS = RSf.rearrange("p (g t) -> p g t", g=3)
            B_win = [abuf(f"win{i}") for i in range(4)]
            B_cvx, B_gbs, B_ux, B_sa, B_sb, B_pl, B_q, B_kz = (abuf(n) for n in ("cvx", "gbs", "ux", "sa", "sb", "pl", "q", "kz"))
            B_tmp = [abuf("tmp0"), abuf("tmp1")]
            B_vd = [abuf(f"vd{i}") for i in range(3)]
            B_pt, B_rd, B_rs = abuf("pt"), abuf("rd"), abuf("rs")
            B_y = [abuf(f"y{c}") for c in range(KC)]
            B_sq = [abuf("sq0"), abuf("sq1")]
            newb = B_win + [B_cvx, B_gbs, B_ux, B_sa, B_sb, B_pl, B_q, B_kz] + B_tmp + B_vd + [B_pt, B_rd, B_rs] + B_y + B_sq
            phase_switch(newb)

            for i in range(4):
                P.dma("pool", lambda e, i=i: e.dma_start(out=WIN_[:, :, i * 512:(i + 1) * 512], in_=d_win[l, i]), f"win{i}", writes=[B_win[i]])
            P.op("dve", lambda e: e.memset(KZ[:], 0.0), writes=[B_kz])
            P.op("dve", lambda e: e.memset(CVx[:, :, 0:2], 0.0), writes=[B_cvx])
            P.op("dve", lambda e: e.memset(Ux[:, :, 0:15], 0.0), writes=[B_ux])

            st_ = {"hs": 0, "sq": 0}

            def inproj_chunk(c0, w, col0, evac):
                s = st_["hs"] % 4
                st_["hs"] += 1
                bk, hf = s // 2, s % 2
                wi = col0 // 512

                def mm(e):
                    ins = None
                    for kc in range(KC):
                        ins = e.matmul(bank(bk, hf * 256, hf * 256 + w), WIN_[:, kc, col0:col0 + 128], HIN[:, kc, c0:c0 + w],
                                       start=(kc == 0), stop=(kc == KC - 1))
                    return ins
                P.op("pe", mm, reads=[B_win[wi]] + hin_bufs(c0, w), writes=[B_hb[bk][hf]])
                evac(lambda p0=0, p1=128: bank(bk, hf * 256, hf * 256 + w, p0, p1), [B_hb[bk][hf]])

            def rms_group(grp, chunks, n, w, ysrc, ybufs, rs_ap, sqbank, sqlo):
                for i, c in enumerate(chunks):
                    r = st_["sq"] % 2
                    st_["sq"] += 1
                    sq = SQ[r]
                    P.op("act", lambda e, sq=sq, c=c: e.activation(out=sq[:, :w], in_=ysrc(c), func=AF.Square),
                         reads=[ybufs[c]], writes=[B_sq[r]])
                    P.op("pe", lambda e, sq=sq, i=i: e.matmul(bank(sqbank, sqlo, sqlo + w), ones[:], sq[:, :w], start=(i == 0),
                                                              stop=(i == len(chunks) - 1)),
                         reads=[B_sq[r], B_const], writes=[B_ps[sqbank]])
                P.op("act", lambda e: e.activation(out=rs_ap, in_=bank(sqbank, sqlo, sqlo + w), func=AF.Ln, scale=1.0 / n,
                                                   bias=epsT[:, 1:2]),
                     reads=[B_ps[sqbank], B_const], writes=[B_rs])
                P.op("act", lambda e: e.activation(out=rs_ap, in_=rs_ap, func=AF.Exp, scale=-0.5), reads=[B_rs], writes=[B_rs])

            JOWN = HALO // 128
            JS = cs // 128

            def tb_body(t, c0, w, first_own):
                if first_own:
                    P.op("pool", lambda e: e.tensor_scalar(out=CVx[:, :, 0:2], in0=CVx[:, :, 0:2], scalar1=valid[:, 0:1], scalar2=0.0,
                                                            op0=ALU.mult, op1=ALU.add), reads=[B_cvx, B_const], writes=[B_cvx])
                    P.op("pool", lambda e: e.tensor_scalar(out=Ux[:, :, 0:15], in0=Ux[:, :, 0:15], scalar1=valid[:, 0:1], scalar2=0.0,
                                                            op0=ALU.mult, op1=ALU.add), reads=[B_ux, B_const], writes=[B_ux])
                for c in range(4):
                    inproj_chunk(c0, w, 1024 + c * 128,
                                 lambda ps, pb, c=c: P.op("act", lambda e: act_id(e, out=Q[:, c, :w], in_=ps()), reads=pb, writes=[B_q]))
                for h in range(2):
                    def ev(ps, pb, h=h):
                        P.op("dve", lambda e: e.tensor_copy(out=KZ[0:64, 2 * h, 128:128 + w], in_=ps(0, 64)), reads=pb, writes=[B_kz])
                        P.op("dve", lambda e: e.tensor_copy(out=KZ[64:128, 2 * h + 1, 128:128 + w], in_=ps(64, 128)), reads=pb, writes=[B_kz])
                    inproj_chunk(c0, w, 1536 + h * 128, ev)
                thunks = []

                def th_u(cc):
                    inproj_chunk(c0, w, 768 + cc * 128,
                                 lambda ps, pb: P.op("act", lambda e: act_id(e, out=Ux[:, cc, 15:15 + w], in_=ps()), reads=pb, writes=[B_ux]))

                def th_gc(cc):
                    tm = TMP[cc]
                    inproj_chunk(c0, w, 256 + cc * 128,
                                 lambda ps, pb: P.op("act", lambda e: act_id(e, out=tm[:, :w], in_=ps()), reads=pb, writes=[B_tmp[cc]]))

                def th_xin(cc):
                    tm = TMP[cc]
                    inproj_chunk(c0, w, 512 + cc * 128,
                                 lambda ps, pb: P.op("dve", lambda e: e.tensor_tensor(out=CVx[:, cc, 2:2 + w], in0=tm[:, :w], in1=ps(), op=ALU.mult),
                                                     reads=pb + [B_tmp[cc]], writes=[B_cvx]))

                def th_gb(cc):
                    inproj_chunk(c0, w, cc * 128,
                                 lambda ps, pb: P.op("act", lambda e: act_id(e, out=GBs[:, cc, :w], in_=ps()), reads=pb, writes=[B_gbs]))
                for jj in range(w // 128):
                    j = c0 // 128 + jj
                    vs = j % 3
                    ca = c0 + jj * 128

                    def mm(e, ca=ca):
                        ins = None
                        for kc in range(KC):
                            ins = e.matmul(bank(2, 0, 256), HIN[:, kc, ca:ca + 128], WIN_[:, kc, 1792:2048], start=(kc == 0), stop=(kc == KC - 1))
                        return ins
                    P.op("pe", mm, reads=[B_win[3]] + hin_bufs(ca, 128), writes=[B_ps[2]])
                    P.op("act", lambda e, vs=vs: act_id(e, out=Vd[vs][:, :], in_=bank(2, 0, 256)), reads=[B_ps[2]], writes=[B_vd[vs]])
                def conv_chain(cc):
                    acc = TMP[cc]
                    t2 = TMP[1 - cc]
                    P.op("pool", lambda e, cc=cc, acc=acc: e.tensor_scalar(out=acc[:, :w], in0=CVx[:, cc, 0:w], scalar1=convw[:, l, 0, cc:cc + 1],
                                                                            scalar2=0.0, op0=ALU.mult, op1=ALU.add),
                         reads=[B_cvx, B_const], writes=[B_tmp[cc]])
                    for k in (1, 2):
                        P.op("pool", lambda e, cc=cc, t2=t2, k=k: e.tensor_scalar(out=t2[:, :w], in0=CVx[:, cc, k:k + w],
                                                                                   scalar1=convw[:, l, k, cc:cc + 1], scalar2=0.0,
                                                                                   op0=ALU.mult, op1=ALU.add),
                             reads=[B_cvx, B_const], writes=[B_tmp[1 - cc]])
                        P.op("pool", lambda e, acc=acc, t2=t2: e.tensor_tensor(out=acc[:, :w], in0=acc[:, :w], in1=t2[:, :w], op=ALU.add),
                             reads=[B_tmp[0], B_tmp[1]], writes=[B_tmp[cc]])
                    P.op("pool", lambda e, cc=cc, acc=acc: e.tensor_tensor(out=Y[:, cc, :w], in0=acc[:, :w], in1=GBs[:, cc, :w], op=ALU.mult),
                         reads=[B_tmp[cc], B_gbs], writes=[B_y[cc]])
                def conv_tail():
                    P.op("pool", lambda e: e.tensor_copy(out=CVx[:, :, 0:2], in_=CVx[:, :, w:w + 2]), reads=[B_cvx], writes=[B_cvx])
                WX = w + 15

                def pool_chain(cc):
                    ux = Ux[:, cc, :]
                    P.op("pool", lambda e, ux=ux: e.tensor_tensor(out=Sa[:, 1:WX], in0=ux[:, 1:WX], in1=ux[:, 0:WX - 1], op=ALU.add),
                         reads=[B_ux], writes=[B_sa])
                    if cc == 0:
                        P.op("pool", lambda e: e.tensor_tensor(out=Sb[64:128, 3:WX], in0=Sa[64:128, 3:WX], in1=Sa[64:128, 1:WX - 2], op=ALU.add),
                             reads=[B_sa], writes=[B_sb])
                    else:
                        P.op("pool", lambda e: e.tensor_tensor(out=Sb[:, 3:WX], in0=Sa[:, 3:WX], in1=Sa[:, 1:WX - 2], op=ALU.add),
                             reads=[B_sa], writes=[B_sb])
                        P.op("pool", lambda e: e.tensor_tensor(out=Sa[:, 7:WX], in0=Sb[:, 7:WX], in1=Sb[:, 3:WX - 4], op=ALU.add),
                             reads=[B_sb], writes=[B_sa])
                        P.op("pool", lambda e: e.tensor_tensor(out=Sb[64:128, 15:WX], in0=Sa[64:128, 15:WX], in1=Sa[64:128, 7:WX - 8], op=ALU.add),
                             reads=[B_sa], writes=[B_sb])
                    for (p0, p1, src, Bs) in ((0, 64, Sa, B_sa), (64, 128, Sb, B_sb)):
                        P.op("pool", lambda e, p0=p0, p1=p1, src=src, cc=cc: e.tensor_scalar(
                            out=src[p0:p1, 15:WX], in0=src[p0:p1, 15:WX], scalar1=invw[p0:p1, cc:cc + 1], scalar2=0.0, op0=ALU.mult, op1=ALU.add),
                            reads=[Bs, B_const], writes=[Bs])
                        if first_own:
                            P.op("pool", lambda e, p0=p0, p1=p1, src=src, cc=cc: e.tensor_tensor(
                                out=src[p0:p1, 15:31], in0=src[p0:p1, 15:31], in1=cntr[p0:p1, cc, :], op=ALU.mult),
                                reads=[Bs, B_const], writes=[Bs])
                        P.op("pool", lambda e, p0=p0, p1=p1, src=src, cc=cc: e.tensor_tensor(
                            out=PL[p0:p1, cc, :w], in0=src[p0:p1, 15:WX], in1=Ux[p0:p1, cc, 15:WX], op=ALU.subtract),
                            reads=[Bs, B_ux], writes=[B_pl])

                def pool_mm(cc):
                    P.op("pe", lambda e, cc=cc: e.matmul(bank(3, 0, w), poolw[:, l, cc, :], PL[:, cc, :w], start=True, stop=True),
                         reads=[B_pl, B_const], writes=[B_ps[3]])
                    P.op("act", lambda e, cc=cc: act_id(e, Y[:, 2 + cc, :w], bank(3, 0, w), pscale[:, l, cc:cc + 1]),
                         reads=[B_ps[3], B_const], writes=[B_y[2 + cc]])
                def pool_tail():
                    P.op("pool", lambda e: e.tensor_copy(out=Ux[:, :, 0:15], in_=Ux[:, :, w:w + 15]), reads=[B_ux], writes=[B_ux])
                thunks += [lambda: th_u(0), lambda: (th_u(1), pool_chain(0), pool_chain(1), pool_tail()),
                           lambda: th_gc(0), lambda: th_xin(0), lambda: (th_gb(0), conv_chain(0)),
                           lambda: th_gc(1), lambda: th_xin(1), lambda: (th_gb(1), conv_chain(1), conv_tail()),
                           lambda: pool_mm(0), lambda: pool_mm(1)]
                nper = 2 if w == TB else 4

                def pop_thunks(n):
                    for _ in range(n):
                        if thunks:
                            thunks.pop(0)()
                for jj in range(w // 128):
                    j = c0 // 128 + jj
                    qlo = jj * 128
                    kprev = qlo
                    kdiag = 128 + qlo
                    has_prev = j > JS
                    mprev = 2 if j == JOWN else 0
                    for h in range(2):
                        def mm(e, h=h, qlo=qlo, kprev=kprev, kdiag=kdiag, has_prev=has_prev, mprev=mprev):
                            ins = None
                            for g in range(4):
                                c = 2 * h + g // 2
                                half = g % 2
                                if has_prev:
                                    e.matmul(bank(4, g * 128, (g + 1) * 128), identb[:], masks[:, mprev, :], start=True, stop=False)
                                    ins = e.matmul(bank(4, g * 128, (g + 1) * 128), KZ[:, 2 * h + half, kprev:kprev + 128],
                                                   Q[:, c, qlo:qlo + 128], start=False, stop=True)
                                e.matmul(bank(5, g * 128, (g + 1) * 128), identb[:], masks[:, 1, :], start=True, stop=False)
                                ins = e.matmul(bank(5, g * 128, (g + 1) * 128), KZ[:, 2 * h + half, kdiag:kdiag + 128],
                                               Q[:, c, qlo:qlo + 128], start=False, stop=True)
                            return ins
                        P.op("pe", mm, reads=[B_kz, B_q, B_const], writes=[B_ps[4], B_ps[5]])
                        if has_prev:
                            P.op("act", lambda e: e.activation(out=PT[:, 0:512], in_=bank(4), func=AF.Exp, scale=0.125),
                                 reads=[B_ps[4]], writes=[B_pt])
                        P.op("act", lambda e: e.activation(out=PT[:, 512:1024], in_=bank(5), func=AF.Exp, scale=0.125),
                             reads=[B_ps[5]], writes=[B_pt])
                        pop_thunks(nper)
                        vprev, vcur = (j - 1) % 3, j % 3

                        def pv(e, h=h, vprev=vprev, vcur=vcur, has_prev=has_prev):
                            if has_prev:
                                e.matmul(bank(6), Vd[vprev][:, h * 128:(h + 1) * 128], PT[:, 0:512], start=True, stop=False)
                            e.matmul(bank(6), Vd[vcur][:, h * 128:(h + 1) * 128], PT[:, 512:1024], start=not has_prev, stop=True)
                            if has_prev:
                                e.matmul(bank(7), ones[:], PT[:, 0:512], start=True, stop=False)
                            return e.matmul(bank(7), ones[:], PT[:, 512:1024], start=not has_prev, stop=True)
                        P.op("pe", pv, reads=[B_pt, B_vd[vprev], B_vd[vcur], B_const], writes=[B_ps[6], B_ps[7]])
                        sk4 = sinke4[:, l, 4 * h:4 * h + 4].unsqueeze(2).to_broadcast([128, 4, 128])
                        P.op("dve", lambda e, sk4=sk4: e.tensor_tensor(out=RD[:, :, :], in0=bank(7).rearrange("p (g q) -> p g q", g=4), in1=sk4,
                                                                        op=ALU.add), reads=[B_ps[7], B_const], writes=[B_rd])
                        P.op("act", lambda e: e.activation(out=RD[:, :, :], in_=RD[:, :, :], func=AF.Ln), reads=[B_rd], writes=[B_rd])
                        P.op("act", lambda e: e.activation(out=RD[:, :, :], in_=RD[:, :, :], func=AF.Exp, scale=-1.0), reads=[B_rd], writes=[B_rd])
                        for half in range(2):
                            p0, p1 = half * 64, half * 64 + 64
                            oo = bank(6, 0, 512, p0, p1).rearrange("p (gg hf q) -> p gg hf q", gg=2, hf=2)[:, :, half, :]
                            rr = RD[p0:p1, :, :].rearrange("p (gg hf) q -> p gg hf q", hf=2)[:, :, half, :]
                            P.op("dve", lambda e, oo=oo, rr=rr, p0=p0, p1=p1, h=h, qlo=qlo: e.tensor_tensor(
                                out=Y[p0:p1, 4 + 2 * h:6 + 2 * h, qlo:qlo + 128], in0=oo, in1=rr, op=ALU.mult),
                                reads=[B_ps[6], B_rd], writes=[B_y[4 + 2 * h], B_y[5 + 2 * h]])
                pop_thunks(100)
                P.op("act", lambda e: act_id(e, out=KZ[:, :, 0:128], in_=KZ[:, :, w:w + 128]), reads=[B_kz], writes=[B_kz])
                if t == NTB - 1:
                    token_major_tail(l, NP - 128, 128, WIN_, B_win, GBf, B_gbs, TMP[0], B_tmp[0], False, RSf, B_rs)
                ysrc = lambda c, w=w: Y[:, c, :w]
                slots = [(2, 256), (3, 0), (3, 256)]
                for grp, (chunks, n) in enumerate([([0, 1], 256.0), ([2, 3], 256.0), ([4, 5, 6, 7], 512.0)]):
                    bk_, lo_ = slots[grp]
                    for i, c in enumerate(chunks):
                        r = st_["sq"] % 2
                        st_["sq"] += 1
                        sq = SQ[r]
                        P.op("act", lambda e, sq=sq, c=c, n=n: e.activation(out=sq[:, :w], in_=Y[:, c, :w], func=AF.Square, scale=float(n) ** -0.5),
                             reads=[B_y[c]], writes=[B_sq[r]])
                        P.op("pe", lambda e, sq=sq, i=i, bk_=bk_, lo_=lo_, chunks=chunks: e.matmul(bank(bk_, lo_, lo_ + w), ones[:], sq[:, :w],
                                                                                                 start=(i == 0), stop=(i == len(chunks) - 1)),
                             reads=[B_sq[r], B_const], writes=[B_ps[bk_]])
                P.op("act", lambda e: e.activation(out=RSf[:, 0:3 * TB], in_=psum[:, 2 * 512 + 256:4 * 512], func=AF.Ln, bias=epsT[:, 1:2]),
                     reads=[B_ps[2], B_ps[3], B_const], writes=[B_rs])
                P.op("act", lambda e: e.activation(out=RSf[:, 0:3 * TB], in_=RSf[:, 0:3 * TB], func=AF.Exp, scale=-0.5), reads=[B_rs], writes=[B_rs])
                for c in range(KC):
                    grp = 0 if c < 2 else (1 if c < 4 else 2)
                    P.op("dve", lambda e, c=c, grp=grp, c0=c0, w=w: e.scalar_tensor_tensor(out=HIN[:, c, c0:c0 + w], in0=Y[:, c, :w],
                                                                                scalar=mixg[:, l, c:c + 1], in1=RS[:, grp, :w],
                                                                                op0=ALU.mult, op1=ALU.mult),
                         reads=[B_y[c], B_rs, B_const], writes=hin_bufs(c0, w, [c]))


            for t in range(NTB):
                c0_ = max(t * TB, cs)
                w_ = (t + 1) * TB - c0_
                if w_ > 0:
                    tb_body(t, c0_, w_, c0_ == HALO)

            cs = Carver(off_after_win)
            cvS = cs.take([2, NS], F32)
            gbS = cs.take([2, NS], F32)
            tmS = cs.take([NS], F32)
            accS = cs.take([NS], F32)
            S1 = cs.take([NS], F32)
            SCV = cs.take([2, 2, NS], F32)
            U16 = cs.take([2, NS, 16], F32)
            PLS = cs.take([2, NS], BF16)
            STa = cs.take([512], F32)
            STb = cs.take([256], F32)
            STc = cs.take([512], F32)
            B_s = abuf("sampA")
            B_scv, B_u16 = abuf("scv"), abuf("u16")
            B_sta, B_stb, B_stc = abuf("sta"), abuf("stb"), abuf("stc")
            phase_switch([B_s, B_scv, B_u16, B_sta, B_stb, B_stc], keep=B_win)
            P.dma("sp", lambda e: e.dma_start(out=SCV[:], in_=d_sconvT[l]), "scv", writes=[B_scv])
            P.dma("sp", lambda e: e.dma_start(out=U16[:], in_=d_spoolT[l]), "u16", writes=[B_u16])
            c0, w = NP, NS
            for cc in range(2):
                inproj_chunk(c0, w, 256 + cc * 128,
                             lambda ps, pb: P.op("act", lambda e: act_id(e, out=tmS[:, :], in_=ps()), reads=pb, writes=[B_s]))
                inproj_chunk(c0, w, 512 + cc * 128,
                             lambda ps, pb, cc=cc: P.op("dve", lambda e: e.tensor_tensor(out=cvS[:, cc, :], in0=tmS[:, :], in1=ps(), op=ALU.mult),
                                                        reads=pb + [B_s], writes=[B_s]))
                inproj_chunk(c0, w, cc * 128,
                             lambda ps, pb, cc=cc: P.op("act", lambda e: act_id(e, out=gbS[:, cc, :], in_=ps()), reads=pb, writes=[B_s]))
                inproj_chunk(c0, w, 768 + cc * 128,
                             lambda ps, pb, cc=cc: P.op("act", lambda e: act_id(e, out=U16[:, cc, :, 15], in_=ps()), reads=pb, writes=[B_u16]))
            for c in range(4):
                inproj_chunk(c0, w, 1024 + c * 128,
                             lambda ps, pb, c=c: P.op("act", lambda e: act_id(e, out=QS[:, c, :], in_=ps()), reads=pb, writes=[B_samp]))
            for h in range(2):
                def ev(ps, pb, h=h):
                    P.op("dve", lambda e: e.tensor_copy(out=KZS[0:64, 2 * h, :], in_=ps(0, 64)), reads=pb, writes=[B_samp])
                    P.op("dve", lambda e: e.tensor_copy(out=KZS[64:128, 2 * h + 1, :], in_=ps(64, 128)), reads=pb, writes=[B_samp])
                inproj_chunk(c0, w, 1536 + h * 128, ev)
            for cc in range(2):
                P.op("dve", lambda e, cc=cc: e.tensor_scalar(out=accS[:, :], in0=SCV[:, cc, 0, :], scalar1=convw[:, l, 0, cc:cc + 1], scalar2=None,
                                                             op0=ALU.mult), reads=[B_scv, B_const], writes=[B_s])
                P.op("dve", lambda e, cc=cc: e.scalar_tensor_tensor(out=accS[:, :], in0=SCV[:, cc, 1, :], scalar=convw[:, l, 1, cc:cc + 1],
                                                                    in1=accS[:, :], op0=ALU.mult, op1=ALU.add),
                     reads=[B_scv, B_const, B_s], writes=[B_s])
                P.op("dve", lambda e, cc=cc: e.scalar_tensor_tensor(out=accS[:, :], in0=cvS[:, cc, :], scalar=convw[:, l, 2, cc:cc + 1],
                                                                    in1=accS[:, :], op0=ALU.mult, op1=ALU.add),
                     reads=[B_const, B_s], writes=[B_s])
                P.op("dve", lambda e, cc=cc: e.tensor_tensor(out=YS[:, cc, :], in0=accS[:, :], in1=gbS[:, cc, :], op=ALU.mult),
                     reads=[B_s], writes=[B_samp])
            for cc in range(2):
                for half in range(2):
                    p0, p1 = half * 64, half * 64 + 64
                    wd_ = 2 ** (2 * cc + half + 1)
                    P.op("dve", lambda e, p0=p0, p1=p1, cc=cc, wd_=wd_: e.tensor_reduce(out=S1[p0:p1, :], in_=U16[p0:p1, cc, :, 16 - wd_:16],
                                                                                         axis=mybir.AxisListType.X, op=ALU.add),
                         reads=[B_u16], writes=[B_s])
                    P.op("dve", lambda e, p0=p0, p1=p1, cc=cc: e.scalar_tensor_tensor(out=PLS[p0:p1, cc, :], in0=S1[p0:p1, :],
                                                                                      scalar=invw[p0:p1, cc:cc + 1], in1=U16[p0:p1, cc, :, 15],
                                                                                      op0=ALU.mult, op1=ALU.subtract),
                         reads=[B_s, B_u16, B_const], writes=[B_s])
                P.op("pe", lambda e, cc=cc: e.matmul(bank(3, 0, NS), poolw[:, l, cc, :], PLS[:, cc, :], start=True, stop=True),
                     reads=[B_s, B_const], writes=[B_ps[3]])
                P.op("act", lambda e, cc=cc: act_id(e, YS[:, 2 + cc, :], bank(3, 0, NS), pscale[:, l, cc:cc + 1]),
                     reads=[B_ps[3], B_const], writes=[B_samp])
            token_major_tail(l, NP, NS, WIN_, B_win, STa, B_sta, STb, B_stb, True, STc, B_stc)

        def token_major_tail(l, ca, M, WIN_, B_win, ST, B_st, ST2, B_st2, is_sample, ST3=None, B_st3=None):
            hb = hin_bufs(ca, M)
            if ST3 is None:
                raise ValueError

            def mm_pass(bk, col0, n):
                def mm(e):
                    ins = None
                    for kc in range(KC):
                        ins = e.matmul(bank(bk, 0, n, 0, M), HIN[:, kc, ca:ca + M], WIN_[:, kc, col0:col0 + n], start=(kc == 0), stop=(kc == KC - 1))
                    return ins
                wis = sorted(set([col0 // 512, (col0 + n - 1) // 512]))
                P.op("pe", mm, reads=[B_win[i] for i in wis] + hb, writes=[B_ps[bk]] + B_hb[bk])
            mm_pass(0, 256, 512)
            P.op("act", lambda e: act_id(e, out=ST2[0:M, 0:256], in_=bank(0, 0, 256, 0, M)), reads=[B_ps[0], B_hb[0][0], B_hb[0][1]], writes=[B_st2])
            P.op("dve", lambda e: e.tensor_tensor(out=ST[0:M, 0:256], in0=ST2[0:M, 0:256], in1=bank(0, 256, 512, 0, M), op=ALU.mult),
                 reads=[B_ps[0], B_hb[0][0], B_hb[0][1], B_st2], writes=[B_st])
            mm_pass(1, 768, 256)
            P.op("act", lambda e: act_id(e, out=ST[0:M, 256:512], in_=bank(1, 0, 256, 0, M)), reads=[B_ps[1], B_hb[1][0], B_hb[1][1]], writes=[B_st])
            mm_pass(0, 1536, 512)
            P.op("act", lambda e: act_id(e, out=ST3[0:M, 0:512], in_=bank(0, 0, 512, 0, M)), reads=[B_ps[0], B_hb[0][0], B_hb[0][1]], writes=[B_st3])
            kv = ST3[0:M, 0:512].rearrange("p (a h r) -> p a h r", a=2, h=2)[:, :, :, 0:64]
            tag = "s" if is_sample else "p"
            if not is_sample:
                P.dma("sp", lambda e: e.dma_start(out=o_pc[l], in_=ST[M - 2:M, 0:256]), "o_st" + tag, reads=[B_st, B_out])
                P.dma("sp", lambda e: e.dma_start(out=o_pp[l], in_=ST[M - 15:M, 256:512]), "o_st" + tag, reads=[B_st, B_out])
                P.dma("sp", lambda e: e.dma_start(out=o_pk[l].rearrange("t (h d) -> t h d", h=2), in_=kv[:, 0, :, :]), "o_st3" + tag,
                      reads=[B_st3, B_out])
                P.dma("sp", lambda e: e.dma_start(out=o_pv[l].rearrange("t (h d) -> t h d", h=2), in_=kv[:, 1, :, :]), "o_st3" + tag,
                      reads=[B_st3, B_out])
            else:
                P.dma("sp", lambda e: e.dma_start(out=o_sc[l, :, 1, :], in_=ST[0:M, 0:256]), "o_st" + tag, reads=[B_st, B_out])
                P.dma("sp", lambda e: e.dma_start(out=o_spl[l, :, 14, :], in_=ST[0:M, 256:512]), "o_st" + tag, reads=[B_st, B_out])
                P.dma("sp", lambda e: e.dma_start(out=o_sk[l, :, 127, :].rearrange("t (h d) -> t h d", h=2), in_=kv[:, 0, :, :]), "o_st3" + tag,
                      reads=[B_st3, B_out])
                P.dma("sp", lambda e: e.dma_start(out=o_sv[l, :, 127, :].rearrange("t (h d) -> t h d", h=2), in_=kv[:, 1, :, :]), "o_st3" + tag,
                      reads=[B_st3, B_out])
                P.op("act", lambda e: act_id(e, out=VtokS[0:M, :], in_=ST3[0:M, 256:512]), reads=[B_st3], writes=[B_samp])

        def mixer_B(l, cs):
            prev_phase[0] = "B"
            set_start(cs)
            cv = Carver()
            WOUT = cv.take([KC, D], BF16)
            lnb_ = carve_ln(cv)
            KS = [cv.take([4, 512], BF16) for _ in range(2)]
            VS = [cv.take([4, 256], BF16) for _ in range(2)]
            PTs = cv.take([128], BF16)
            T1 = cv.take([NS, 4], F32)
            SQs = [cv.take([NS], BF16) for _ in range(2)]
            RSs = cv.take([3, NS], F32)
            B_wout = [abuf("wout0"), abuf("wout1")]
            B_ks = [abuf("ks0"), abuf("ks1")]
            B_vs = [abuf("vs0"), abuf("vs1")]
            B_pts, B_t1, B_rss = abuf("pts"), abuf("t1"), abuf("rss")
            B_sqs = [abuf("sqs0"), abuf("sqs1")]
            phase_switch(B_wout + lnb_ + B_ks + B_vs + [B_pts, B_t1, B_rss] + B_sqs)
            for i in range(2):
                P.dma("pool", lambda e, i=i: e.dma_start(out=WOUT[:, :, i * 512:(i + 1) * 512], in_=d_wout[l, i]), f"wout{i}", writes=[B_wout[i]])
            ln_consts(l, 1, (l % 2, 6, 7))
            gates(l, 1)
            def samp_dma(gq):
                slot = gq % 2
                ks, vs = KS[slot], VS[slot]
                P.dma("pool", lambda e, ks=ks, gq=gq: e.dma_start(out=ks[:, :, :], in_=d_kTz[l, 4 * gq:4 * gq + 4].rearrange("b p n -> p b n")),
                      f"ks{slot}", writes=[B_ks[slot]])
                P.dma("pool", lambda e, vs=vs, gq=gq: e.dma_start(out=vs[:, :, :], in_=d_vd[l, 4 * gq:4 * gq + 4].rearrange("b p n -> p b n")),
                      f"vs{slot}", writes=[B_vs[slot]])
            samp_dma(0)
            samp_dma(1)

            def samp_group(gq):
                slot = gq % 2
                ks, vs = KS[slot], VS[slot]
                for bl in range(4):
                    b = 4 * gq + bl
                    P.op("act", lambda e, ks=ks, bl=bl, b=b: act_id(e, out=ks[:, bl, :].rearrange("p (x k) -> p x k", x=4)[:, :, 0], in_=KZS[:, :, b]),
                         reads=[B_samp, B_ks[slot]], writes=[B_ks[slot]])
                    P.op("pe", lambda e, b=b: e.matmul(bank(3, 0, 256, 0, 1), ident[0:16, b:b + 1], VtokS[0:16, :], start=True, stop=True),
                         reads=[B_samp, B_const], writes=[B_ps[3]])
                    P.op("act", lambda e, vs=vs, bl=bl: act_id(e, out=vs[0:1, bl, :], in_=bank(3, 0, 256, 0, 1)), reads=[B_ps[3], B_vs[slot]],
                         writes=[B_vs[slot]])

                    def sc(e, ks=ks, bl=bl, b=b):
                        ins = None
                        for h in range(2):
                            for half in range(2):
                                for cl in range(2):
                                    col = b * 8 + h * 4 + half * 2 + cl
                                    x = h * 2 + half
                                    ins = e.matmul(bank(0, col, col + 1), ks[:, bl, x * 128:(x + 1) * 128], QS[:, 2 * h + cl, b:b + 1],
                                                   start=True, stop=True)
                        return ins
                    P.op("pe", sc, reads=[B_ks[slot], B_samp], writes=[B_ps[0], B_hb[0][0], B_hb[0][1]])
                g0, g1 = gq * 32, gq * 32 + 32
                P.op("act", lambda e, g0=g0, g1=g1: e.activation(out=PTs[:, g0:g1], in_=bank(0, g0, g1), func=AF.Exp, scale=0.125),
                     reads=[B_ps[0], B_hb[0][0], B_hb[0][1]], writes=[B_pts])

                def pv(e, vs=vs, gq=gq, g0=g0, g1=g1):
                    ins = e.matmul(bank(1, g0, g1), ones[:], PTs[:, g0:g1], start=True, stop=True)
                    for bl in range(4):
                        b = 4 * gq + bl
                        for h in range(2):
                            col0 = b * 8 + h * 4
                            ins = e.matmul(bank(2, col0, col0 + 4), vs[:, bl, h * 128:(h + 1) * 128], PTs[:, col0:col0 + 4], start=True, stop=True)
                    return ins
                P.op("pe", pv, reads=[B_pts, B_vs[slot], B_const], writes=[B_ps[1], B_hb[1][0], B_hb[1][1], B_ps[2]])
                if gq + 2 < NS // 4:
                    samp_dma(gq + 2)
            def samp_finish():
                for half in range(2):
                    p0, p1 = half * 64, half * 64 + 64
                    den = bank(1, 0, 128, p0, p1).rearrange("p (b h f c) -> p b h f c", b=NS, h=2, f=2)[:, :, :, half, :]
                    oo = bank(2, 0, 128, p0, p1).rearrange("p (b h f c) -> p b h f c", b=NS, h=2, f=2)[:, :, :, half, :]
                    sk = sinke[p0:p1, l, :].rearrange("p (h c) -> p h c", h=2).unsqueeze(1).to_broadcast([64, NS, 2, 2])
                    t1 = T1[p0:p1, :, :].rearrange("p b (h c) -> p b h c", h=2)
                    P.op("dve", lambda e, den=den, sk=sk, t1=t1: e.tensor_tensor(out=t1, in0=den, in1=sk, op=ALU.add),
                         reads=[B_ps[1], B_hb[1][0], B_hb[1][1], B_const], writes=[B_t1])
                    P.op("dve", lambda e, t1=t1: e.reciprocal(out=t1, in_=t1), reads=[B_t1], writes=[B_t1])
                    ys = YS[p0:p1, 4:8, :].rearrange("p (h c) b -> p b h c", h=2)
                    P.op("dve", lambda e, oo=oo, t1=t1, ys=ys: e.tensor_tensor(out=ys, in0=oo, in1=t1, op=ALU.mult),
                         reads=[B_ps[2], B_t1], writes=[B_samp])
                nsq = [0]
                for grp, (chunks, n) in enumerate([([0, 1], 256.0), ([2, 3], 256.0), ([4, 5, 6, 7], 512.0)]):
                    for i, c in enumerate(chunks):
                        r = nsq[0] % 2
                        nsq[0] += 1
                        sq = SQs[r]
                        P.op("act", lambda e, sq=sq, c=c: e.activation(out=sq[:, :], in_=YS[:, c, :], func=AF.Square), reads=[B_samp], writes=[B_sqs[r]])
                        P.op("pe", lambda e, sq=sq, i=i, grp=grp, chunks=chunks: e.matmul(bank(1, 256 + grp * 16, 256 + grp * 16 + NS), ones[:], sq[:, :],
                                                                                          start=(i == 0), stop=(i == len(chunks) - 1)),
                             reads=[B_sqs[r], B_const], writes=[B_ps[1], B_hb[1][0], B_hb[1][1]])
                    P.op("act", lambda e, grp=grp, n=n: e.activation(out=RSs[:, grp, :], in_=bank(1, 256 + grp * 16, 256 + grp * 16 + NS), func=AF.Ln,
                                                                     scale=1.0 / n, bias=epsT[:, 1:2]),
                         reads=[B_ps[1], B_hb[1][0], B_hb[1][1], B_const], writes=[B_rss])
                    P.op("act", lambda e, grp=grp: e.activation(out=RSs[:, grp, :], in_=RSs[:, grp, :], func=AF.Exp, scale=-0.5), reads=[B_rss], writes=[B_rss])
                for c in range(KC):
                    grp = 0 if c < 2 else (1 if c < 4 else 2)
                    P.op("dve", lambda e, c=c, grp=grp: e.scalar_tensor_tensor(out=HIN[:, c, NP:NP + NS], in0=YS[:, c, :], scalar=mixg[:, l, c:c + 1],
                                                                                in1=RSs[:, grp, :], op0=ALU.mult, op1=ALU.mult),
                         reads=[B_samp, B_rss, B_const], writes=hin_bufs(NP, NS, [c]))

            prevb = [None]

            def blk_done(bi):
                ln_math(bi)
                if prevb[0] is not None:
                    ln_apply(prevb[0], True)
                prevb[0] = bi
            n_pd = 0
            sg_next = [0]

            def samp_some(n):
                for _ in range(n):
                    if sg_next[0] < NS // 4:
                        samp_group(sg_next[0])
                        sg_next[0] += 1
                        if sg_next[0] == NS // 4:
                            samp_finish()
            for bi, (c0, w) in enumerate(FBLK):
                if w == 0:
                    continue
                if c0 >= NP:
                    samp_some(NS // 4)
                for o in range(KC):
                    if o == 4 and c0 < NP:
                        samp_some(1)
                    pd = 4 + (n_pd % 2)
                    n_pd += 1

                    def mm(e, c0=c0, w=w, pd=pd, o=o):
                        ins = None
                        for kc in range(KC):
                            ins = e.matmul(bank(pd, 0, w), WOUT[:, kc, o * 128:(o + 1) * 128], HIN[:, kc, c0:c0 + w], start=(kc == 0),
                                           stop=(kc == KC - 1))
                        return ins
                    P.op("pe", mm, reads=[B_wout[o // 4]] + hin_bufs(c0, w), writes=[B_ps[pd]])
                    resid_evac(bi, o, bank(pd, 0, w), [B_ps[pd]], 1)
                    ln_stats_feed(bi, o, blk_done)
            ln_flush()
            ln_apply(NB - 1, True)

        for l in range(L):
            for (dst, src) in [(o_sc[l, :, 0, :], d_sconv[l, :, 1, :]), (o_spl[l, :, 0:14, :], d_spool[l, :, 1:15, :]),
                               (o_sk[l, :, 0:127, :], d_ck[l, :, 1:128, :]), (o_sv[l, :, 0:127, :], d_cv[l, :, 1:128, :])]:
                P.dma("sp", lambda e, dst=dst, src=src: e.dma_start(out=dst, in_=src), "dd", reads=[B_out])

        cv0 = Carver()
        NR0 = 8
        ada["ring"] = [cv0.take([KC, 128], BF16) for _ in range(NR0)]
        ada["bufs"] = [abuf(f"adar{i}") for i in range(NR0)]
        ada["nslot"] = NR0
        phase_switch(ada["bufs"])
        ada_load_bias(0)
        for j in range(24):
            ada_item(0, j)
        ada["nslot"] = 2
        ada["n"] = 0
        M = Mt[0]
        P.op("dve", lambda e: e.tensor_scalar(out=lnc[:, 2, :], in0=M[:, KC:2 * KC, 0], scalar1=1.0, scalar2=None, op0=ALU.add),
             reads=[B_M[0]], writes=[B_lnc])
        P.op("dve", lambda e: e.tensor_scalar(out=lncs[:, 0, :, :], in0=M[:, KC:2 * KC, 1:17], scalar1=1.0, scalar2=None, op0=ALU.add),
             reads=[B_M[0]], writes=[B_lnc])
        for bi, (c0, w) in enumerate(FBLK):
            for c in range(KC):
                xs = X[:, c, c0:c0 + w]
                hs = HIN[:, c, c0:c0 + w]
                hb = hin_bufs(c0, w, [c])
                if c0 < NP:
                    P.op("dve", lambda e, xs=xs, hs=hs, c=c: e.tensor_scalar(out=hs, in0=xs, scalar1=lnc[:, 2, c:c + 1], scalar2=M[:, c, 0:1],
                                                                              op0=ALU.mult, op1=ALU.add),
                         reads=[B_X[c][bi], B_lnc, B_M[0]], writes=hb)
                else:
                    P.op("dve", lambda e, xs=xs, c=c: e.tensor_tensor(out=YS[:, c, :], in0=xs, in1=lncs[:, 0, c, :], op=ALU.mult),
                         reads=[B_X[c][bi], B_lnc], writes=[B_samp])
                    P.op("dve", lambda e, hs=hs, c=c: e.tensor_tensor(out=hs, in0=YS[:, c, :], in1=M[:, c, 1:17], op=ALU.add),
                         reads=[B_samp, B_M[0]], writes=hb)
                P.op("act", lambda e, xs=xs: act_id(e, xs, xs, ALPHA), reads=[B_X[c][bi]] + hb, writes=[B_X[c][bi]])

        for l in range(L):
            last = (l == L - 1)
            cs0 = min(128 * l, HALO)
            cs1 = min(128 * (l + 1), HALO)
            ffn_phase(l, 0, 0 if l == 0 else None, 0, (l % 2, 3, 4), cs0, ada_list=list(range(24, 72)) if l == 0 else None)
            mixer_A(l, cs0)
            mixer_B(l, cs1)
            ffn_phase(l, 1, None if last else l + 1, 2, None if last else ((l + 1) % 2, 0, 1), cs1, final=last)

        P.wait_all("sp", [B_out])
        P.emit()
    return nc


def _shared_weights(inp, L):
    f = lambda a: np.ascontiguousarray(a, dtype=np.float32)
    w = {}
    w["wada"] = f(inp["w_ada"].reshape(L, KC, 128, 72, 128).transpose(0, 3, 2, 1, 4))
    w["bada"] = f(inp["b_ada"].reshape(L, 72, 128).transpose(0, 2, 1))
    for i, (g, u, d) in enumerate([("ffn1_gate", "ffn1_up", "ffn1_down"), ("ffn2_gate", "ffn2_up", "ffn2_down")]):
        gg = inp[g].reshape(L, KC, 128, MC, 128).transpose(0, 3, 2, 1, 4)
        uu = inp[u].reshape(L, KC, 128, MC, 128).transpose(0, 3, 2, 1, 4)
        w[f"wgu{i + 1}"] = f(np.stack([gg, uu], axis=3).reshape(L, MC, 128, 2 * KC * 128))
        w[f"wd{i + 1}"] = f(inp[d].reshape(L, MC, 128, D))
    W = inp["w_in"]
    K0, K1, V0, V1 = W[:, :, 1536:1600], W[:, :, 1600:1664], W[:, :, 1664:1728], W[:, :, 1728:1792]
    ext = np.concatenate([W[:, :, 0:1536], K0, K0, K1, K1, V0, V0, V1, V1], axis=2)
    w["win"] = f(ext.reshape(L, KC, 128, 4, 512).transpose(0, 3, 2, 1, 4))
    w["wout"] = f(inp["w_out"].reshape(L, KC, 128, 2, 512).transpose(0, 3, 2, 1, 4))
    w["lng"] = f(inp["ln_g"].reshape(L, 3, KC, 128).transpose(3, 0, 1, 2))
    w["lnb"] = f(inp["ln_b"].reshape(L, 3, KC, 128).transpose(3, 0, 1, 2))
    w["convw"] = f(inp["conv_w"].reshape(L, 3, 2, 128).transpose(3, 0, 1, 2))
    pw = np.zeros((L, 2, 128, 128), np.float32)
    for cc in range(2):
        pw[:, cc, 0:64, 0:64] = inp["pool_w"][:, 2 * cc]
        pw[:, cc, 64:128, 64:128] = inp["pool_w"][:, 2 * cc + 1]
    w["poolw"] = pw
    w["pscale"] = f(inp["pool_scale"].reshape(L, 2, 128).transpose(2, 0, 1))
    w["mixg"] = f(inp["mix_norm_g"].reshape(L, KC, 128).transpose(2, 0, 1))
    sk = np.zeros((128, L, 4), np.float32)
    for c in range(4):
        sk[0:64, :, c] = inp["attn_sinks"][:, 2 * c][None, :]
        sk[64:128, :, c] = inp["attn_sinks"][:, 2 * c + 1][None, :]
    w["sinks"] = sk
    iw = np.zeros((128, 2), np.float32)
    for cc in range(2):
        iw[0:64, cc] = 1.0 / (2 ** (2 * cc + 1))
        iw[64:128, cc] = 1.0 / (2 ** (2 * cc + 2))
    w["invw"] = iw
    w["ident"] = np.eye(16, dtype=np.float32)
    w["identb"] = np.eye(128, dtype=np.float32)
    w["sinks4"] = f(np.broadcast_to(inp["attn_sinks"][None, :, :], (128, L, 8)))
    return w


def _core_inputs(cfg, inp, core, shared):
    L, OWN, HALO, NP, NT = cfg.L, cfg.OWN, cfg.HALO, cfg.NP, cfg.NT
    f = lambda a: np.ascontiguousarray(a, dtype=np.float32)
    nseg = inp["x_prompt"].shape[1] // OWN
    b, seg = core // nseg, core % nseg
    s0 = seg * OWN
    idx = np.arange(s0 - HALO, s0 + OWN)
    ok = idx >= 0
    xall = np.zeros((NT, D), np.float32)
    xall[:NP][ok] = inp["x_prompt"][b, idx[ok]]
    sl = slice(NS * core, NS * core + NS)
    xall[NP:] = inp["x_sample"][sl, 0, :]
    m = dict(shared)
    m["xT"] = f(xall.T.reshape(KC, 128, NT).transpose(1, 0, 2))
    call = np.concatenate([inp["c_prompt"][b:b + 1], inp["c_sample"][sl]], axis=0)
    m["cT"] = f(call.T.reshape(KC, 128, 17).transpose(1, 0, 2))
    ic = np.zeros((128, 2, 16), np.float32)
    for cc in range(2):
        for half in range(2):
            wd_ = 2 ** (2 * cc + half + 1)
            pos = np.arange(16)
            cnt = np.minimum(pos + 1, wd_) if seg == 0 else np.full(16, wd_)
            ic[half * 64:half * 64 + 64, cc, :] = (1.0 / cnt)[None, :]
    m["invcnt"] = ic
    m["valid"] = np.full((128, 1), 0.0 if seg == 0 else 1.0, np.float32)
    i = np.arange(128)[:, None]
    t = np.arange(128)[None, :]
    mk = np.zeros((128, 3, 128), np.float32)
    NEG = -30000.0
    mk[:, 0, :] = np.where(i > t, 0.0, NEG)
    mk[:, 1, :] = np.where(i <= t, 0.0, NEG)
    mk[:, 2, :] = np.where(i > t, 0.0, NEG) if seg != 0 else NEG
    m["masks"] = mk
    cr = np.zeros((128, 2, 16), np.float32)
    for cc in range(2):
        for half in range(2):
            wd_ = 2 ** (2 * cc + half + 1)
            pos = np.arange(16)
            cnt = np.minimum(pos + 1, wd_) if seg == 0 else np.full(16, wd_)
            cr[half * 64:half * 64 + 64, cc, :] = (wd_ / cnt)[None, :]
    m["cntr"] = cr
    ck = inp["cache_k_win"][:, sl]
    cvv = inp["cache_v_win"][:, sl]
    kt = ck.transpose(0, 1, 3, 4, 2)
    kz = np.zeros((L, NS, 128, 2, 2, 128), np.float32)
    for h in range(2):
        kz[:, :, 0:64, h, 0, :] = kt[:, :, h]
        kz[:, :, 64:128, h, 1, :] = kt[:, :, h]
    m["kTz"] = kz.reshape(L, NS, 128, 512)
    m["vd"] = f(np.concatenate([cvv, cvv], axis=-1).reshape(L, NS, 128, 256))
    m["ck"] = f(ck.reshape(L, NS, 128, 128))
    m["cv"] = f(cvv.reshape(L, NS, 128, 128))
    sc = inp["state_conv"][:, sl]
    m["sconvT"] = f(sc.reshape(L, NS, 2, 2, 128).transpose(0, 4, 3, 2, 1))
    sp = inp["state_pool"][:, sl]
    spt = np.zeros((L, 128, 2, NS, 16), np.float32)
    spt[..., 0:15] = sp.reshape(L, NS, 15, 2, 128).transpose(0, 4, 3, 1, 2)
    m["spoolT"] = spt
    m["sconv"] = f(sc)
    m["spool"] = f(sp)
    return m


_PROG_CACHE = {}


def kernel(**inputs):
    inp = {k: np.asarray(v) for k, v in inputs.items()}
    L = inp["w_ada"].shape[0]
    B, SEQ = inp["x_prompt"].shape[:2]
    ncores = 8
    nseg = ncores // B
    OWN = SEQ // nseg
    HALO = max(TB, ((128 * L + TB - 1) // TB) * TB)
    cfg = Cfg(L, OWN, HALO)
    key = (L, OWN, HALO)
    if key not in _PROG_CACHE:
        _PROG_CACHE[key] = build_program(cfg)
    nc = _PROG_CACHE[key]
    shared = _shared_weights(inp, L)
    in_maps = [_core_inputs(cfg, inp, c, shared) for c in range(ncores)]
    res = run_bass_kernel_spmd(nc, in_maps, core_ids=list(range(ncores)))
    R = res.results
    NSB = inp["x_sample"].shape[0]
    y_prompt = np.zeros((B, SEQ, D), np.float32)
    y_sample = np.zeros((NSB, 1, D), np.float32)
    p_conv = np.zeros((L, B, 2, 256), np.float32)
    p_pool = np.zeros((L, B, 15, 256), np.float32)
    p_k = np.zeros((L, B, 128, 2, 64), np.float32)
    p_v = np.zeros((L, B, 128, 2, 64), np.float32)
    s_conv = np.zeros((L, NSB, 2, 256), np.float32)
    s_pool = np.zeros((L, NSB, 15, 256), np.float32)
    s_k = np.zeros((L, NSB, 128, 2, 64), np.float32)
    s_v = np.zeros((L, NSB, 128, 2, 64), np.float32)
    for c in range(ncores):
        b, seg = c // nseg, c % nseg
        r = R[c]
        yt = np.asarray(r["o_x"]).transpose(2, 1, 0).reshape(OWN + NS, D)
        y_prompt[b, seg * OWN:(seg + 1) * OWN] = yt[:OWN]
        sl = slice(NS * c, NS * c + NS)
        y_sample[sl, 0] = yt[OWN:]
        if seg == nseg - 1:
            p_conv[:, b] = r["o_pc"]
            p_pool[:, b] = r["o_pp"]
            p_k[:, b] = np.asarray(r["o_pk"]).reshape(L, 128, 2, 64)
            p_v[:, b] = np.asarray(r["o_pv"]).reshape(L, 128, 2, 64)
        s_conv[:, sl] = r["o_sc"]
        s_pool[:, sl] = r["o_spl"]
        s_k[:, sl] = np.asarray(r["o_sk"]).reshape(L, NS, 128, 2, 64)
        s_v[:, sl] = np.asarray(r["o_sv"]).reshape(L, NS, 128, 2, 64)
    return (y_prompt, y_sample, p_conv, p_pool, p_k, p_v, s_conv, s_pool, s_k, s_v)
```

```python
import contextlib
import numpy as np
import concourse.bass as bass
import concourse.mybir as mybir
from concourse.bass_utils import run_bass_kernel_spmd

F32 = mybir.dt.float32
BF16 = mybir.dt.bfloat16
AF = mybir.ActivationFunctionType
ALU = mybir.AluOpType

QUEUES = ("pe", "act", "dve", "pool", "sp")


class Buf:
    __slots__ = ("name", "last_w", "reads")

    def __init__(self, name):
        self.name = name
        self.last_w = None
        self.reads = []


class Op:
    __slots__ = ("fn", "waits", "event", "inc")

    def __init__(self, fn, waits, event, inc):
        self.fn, self.waits, self.event, self.inc = fn, waits, event, inc


class Prog:
    def __init__(self, nc):
        self.nc = nc
        self.ops = {q: [] for q in QUEUES}
        self.count = {}
        self.known = {q: {} for q in QUEUES}
        self.dma_sems = []

    def _deps(self, queue, reads, writes):
        need = {}

        def add(ev):
            if ev is None:
                return
            k, v = ev
            if queue == "pe" and k == "q_pe":
                return
            if need.get(k, 0) < v:
                need[k] = v

        for b in reads:
            add(b.last_w)
        for b in writes:
            add(b.last_w)
            for ev in b.reads:
                add(ev)
        waits = []
        kn = self.known[queue]
        for k, v in need.items():
            if kn.get(k, 0) >= v:
                continue
            kn[k] = v
            waits.append((k, v))
        return waits

    @staticmethod
    def _commit(ev, reads, writes):
        for b in reads:
            b.reads.append(ev)
        for b in writes:
            b.last_w = ev
            b.reads = []

    def op(self, queue, fn, reads=(), writes=()):
        waits = self._deps(queue, reads, writes)
        k = "q_" + queue
        v = self.count.get(k, 0) + 1
        self.count[k] = v
        ev = (k, v)
        self.ops[queue].append(Op(fn, waits, ev, 1))
        self._commit(ev, reads, writes)
        return ev

    def dma(self, queue, fn, semkey, reads=(), writes=()):
        waits = self._deps(queue, reads, writes)
        if semkey not in self.count:
            self.dma_sems.append(semkey)
        v = self.count.get(semkey, 0) + 16
        self.count[semkey] = v
        ev = (semkey, v)
        self.ops[queue].append(Op(fn, waits, ev, 16))
        self._commit(ev, reads, writes)
        return ev

    def wait_all(self, queue, bufs):
        waits = self._deps(queue, (), bufs)
        self.ops[queue].append(Op(None, waits, None, 0))

    def emit(self):
        nc = self.nc
        keys = ["q_" + q for q in QUEUES if ("q_" + q) in self.count] + self.dma_sems
        with contextlib.ExitStack() as st:
            sems = {k: st.enter_context(nc.semaphore("s_" + k)) for k in keys}
            block = st.enter_context(nc.Block())

            def run(queue):
                def body(eng):
                    for o in self.ops[queue]:
                        for (k, v) in o.waits:
                            eng.wait_ge(sems[k], v)
                        if o.fn is None:
                            continue
                        ins = o.fn(eng)
                        if o.event is not None:
                            ins.then_inc(sems[o.event[0]], o.inc)
                return body

            if self.ops["sp"]:
                block.sync(run("sp"))
            if self.ops["pe"]:
                block.tensor(run("pe"))
            if self.ops["act"]:
                block.scalar(run("act"))
            if self.ops["dve"]:
                block.vector(run("dve"))
            if self.ops["pool"]:
                block.gpsimd(run("pool"))


D = 1024
KC = 8
DFF = 2816
MC = 22
NS = 16
TB = 256
FB = 512
WIN = 128
NMOD = 9
LN_EPS = 1e-5
RMS_EPS = 1e-6
WEXT = 2048
GROUPS = (4, 4, 4, 5, 5)
GMAX = 5


class Cfg:
    def __init__(self, L, OWN, HALO, depth_full=4):
        self.L, self.OWN, self.HALO = L, OWN, HALO
        self.NP = HALO + OWN
        self.NT = self.NP + NS
        assert self.NP % FB == 0 and HALO % TB == 0
        self.NFB = self.NP // FB
        self.NTB = self.NP // TB
        self.ALPHA = float((2.0 * depth_full) ** 0.25)


ARENA = 68608
LN_OFF = 50336


def build_program(cfg):
    L, NP, NT, HALO, OWN = cfg.L, cfg.NP, cfg.NT, cfg.HALO, cfg.OWN
    NFB, NTB, ALPHA = cfg.NFB, cfg.NTB, cfg.ALPHA
    nc = bass.Bass("TRN2", target_bir_lowering=False)
    P = Prog(nc)

    def din(name, shape):
        return nc.dram_tensor(name, list(shape), F32, kind="ExternalInput").ap()

    def dout(name, shape):
        return nc.dram_tensor(name, list(shape), F32, kind="ExternalOutput").ap()

    d_xT = din("xT", [128, KC, NT])
    d_cT = din("cT", [128, KC, 17])
    d_wada = din("wada", [L, 72, 128, KC, 128])
    d_bada = din("bada", [L, 128, 72])
    d_wgu = [din("wgu1", [L, MC, 128, 2 * KC * 128]), din("wgu2", [L, MC, 128, 2 * KC * 128])]
    d_wd = [din("wd1", [L, MC, 128, D]), din("wd2", [L, MC, 128, D])]
    d_win = din("win", [L, 4, 128, KC, 512])
    d_wout = din("wout", [L, 2, 128, KC, 512])
    d_lng = din("lng", [128, L, 3, KC])
    d_lnb = din("lnb", [128, L, 3, KC])
    d_convw = din("convw", [128, L, 3, 2])
    d_poolw = din("poolw", [L, 2, 128, 128])
    d_pscale = din("pscale", [128, L, 2])
    d_mixg = din("mixg", [128, L, KC])
    d_sinks = din("sinks", [128, L, 4])
    d_invcnt = din("invcnt", [128, 2, 16])
    d_invw = din("invw", [128, 2])
    d_valid = din("valid", [128, 1])
    d_masks = din("masks", [128, 3, 128])
    d_ident = din("ident", [16, 16])
    d_identb = din("identb", [128, 128])
    d_sinks4 = din("sinks4", [128, L, 8])
    d_cntr = din("cntr", [128, 2, 16])
    d_kTz = din("kTz", [L, NS, 128, 512])
    d_vd = din("vd", [L, NS, 128, 256])
    d_ck = din("ck", [L, NS, 128, 128])
    d_cv = din("cv", [L, NS, 128, 128])
    d_sconvT = din("sconvT", [L, 128, 2, 2, NS])
    d_spoolT = din("spoolT", [L, 128, 2, NS, 16])
    d_sconv = din("sconv", [L, NS, 2, 256])
    d_spool = din("spool", [L, NS, 15, 256])

    o_x = dout("o_x", [128, KC, OWN + NS])
    o_pc = dout("o_pc", [L, 2, 256])
    o_pp = dout("o_pp", [L, 15, 256])
    o_pk = dout("o_pk", [L, 128, 128])
    o_pv = dout("o_pv", [L, 128, 128])
    o_sc = dout("o_sc", [L, NS, 2, 256])
    o_spl = dout("o_spl", [L, NS, 15, 256])
    o_sk = dout("o_sk", [L, NS, 128, 128])
    o_sv = dout("o_sv", [L, NS, 128, 128])
    B_out = Buf("outputs")
    n_out = [0]

    def out_dma(dst, src, reads):
        k = f"od{n_out[0] % 8}"
        n_out[0] += 1
        P.dma("sp", lambda e: e.dma_start(out=dst, in_=src), k, reads=reads + [B_out], writes=[])
        return k

    def act_id(e, out, in_, scale=1.0, bias=0.0):
        return e.activation(out=out, in_=in_, func=AF.Prelu, scale=scale, bias=bias, alpha=1.0)

    st = contextlib.ExitStack()
    with st:
        def sb(name, shape, dt=F32):
            return st.enter_context(nc.sbuf_tensor("s_" + name, list(shape), dt))

        X = sb("X", [128, KC, NT])
        HIN = sb("HIN", [128, KC, NT], BF16)
        Mt = [sb("M0", [128, 72, 17]), sb("M1", [128, 72, 17])]
        GT = sb("GT", [128, 3, KC, 17])
        SC_ = sb("siluc", [128, KC, 17], BF16)
        cTs = sb("cTs", [128, KC, 17])
        bada = sb("bada", [128, 72])
        lng = sb("lng", [128, L, 3, KC])
        lnb = sb("lnb", [128, L, 3, KC])
        convw = sb("convw", [128, L, 3, 2])
        pscale = sb("pscale", [128, L, 2])
        mixg = sb("mixg", [128, L, KC])
        sinke = sb("sinke", [128, L, 4])
        invw = sb("invw", [128, 2])
        valid = sb("valid", [128, 1])
        masks = sb("masks", [128, 3, 128], BF16)
        ident = sb("ident", [16, 16])
        ones = sb("ones", [128, 128], BF16)
        identb = sb("identb", [128, 128], BF16)
        sinke4 = sb("sinke4", [128, L, 8])
        cntr = sb("cntr", [128, 2, 16])
        poolw = sb("poolw", [128, L, 2, 128], BF16)
        lnc = sb("lnc", [128, 4, KC])
        lncs = sb("lncs", [128, 2, KC, NS])
        epsT = sb("epsT", [128, 2])
        VtokS = sb("VtokS", [16, 256])
        QS = sb("QS", [128, 4, NS], BF16)
        KZS = sb("KZS", [128, 4, NS], BF16)
        YS = sb("YS", [128, KC, NS])
        arena = sb("arena", [128, ARENA], mybir.dt.uint8)

        class Carver:
            def __init__(self, off=0):
                self.off = off

            def take(self, shape, dt):
                esz = 2 if dt == BF16 else 4
                n = int(np.prod(shape))
                self.off = (self.off + 31) // 32 * 32
                a = arena[:, self.off:self.off + n * esz].bitcast(dt)
                self.off += n * esz
                assert self.off <= ARENA, (self.off, ARENA)
                if len(shape) == 1:
                    return a
                names = " ".join(f"d{i}" for i in range(len(shape)))
                kw = {f"d{i}": int(s) for i, s in enumerate(shape)}
                return a.rearrange(f"p ({names}) -> p {names}", **kw)

        psum = st.enter_context(nc.psum_tensor("psum", [128, 8 * 512], F32))

        def bank(i, lo=0, hi=512, p0=0, p1=128):
            return psum[p0:p1, i * 512 + lo:i * 512 + hi]

        B_ps = [Buf(f"ps{i}") for i in range(8)]
        B_hb = [[B_ps[i], B_ps[i]] for i in range(2)]

        FBLK = [(i * FB, FB) for i in range(NFB)] + [(NP, NS)]
        NB = len(FBLK)

        def set_start(cs):
            for i in range(NFB):
                lo = max(i * FB, cs)
                FBLK[i] = (lo, max(0, (i + 1) * FB - lo))
        B_X = [[Buf(f"X{c}_{b}") for b in range(NB)] for c in range(KC)]
        B_H = [[Buf(f"H{c}_{t}") for t in range(NTB + 1)] for c in range(KC)]

        def hin_bufs(c0, w, chunks=range(KC)):
            if c0 >= NP:
                ts = [NTB]
            else:
                ts = list(range(c0 // TB, (c0 + w - 1) // TB + 1))
            return [B_H[c][t] for c in chunks for t in ts]

        B_M = [Buf("M0"), Buf("M1")]
        B_GT = Buf("GT")
        B_const = Buf("const")
        B_lnc = Buf("lnc")
        B_samp = Buf("samp_persist")
        arena_bufs = []

        def abuf(name):
            return Buf(name)

        arena_T = []

        def _inherit(old_list, new_bufs, keep=()):
            mx = {}
            for b in old_list:
                if b in keep:
                    continue
                evs = list(b.reads)
                if b.last_w is not None:
                    evs.append(b.last_w)
                for k, v in evs:
                    if mx.get(k, 0) < v:
                        mx[k] = v
            evl = list(mx.items())
            for b in new_bufs:
                b.last_w = None
                b.reads = list(evl)
            del old_list[:]
            old_list.extend(list(keep) + list(new_bufs))

        def phase_switch(new_bufs, keep=(), newT=None):
            _inherit(arena_bufs, new_bufs, keep)
            if newT is not None:
                _inherit(arena_T, newT)

        def ld(dst, src, key, queue="sp"):
            P.dma(queue, lambda e: e.dma_start(out=dst, in_=src), key, writes=[B_const])

        for (dst, src, key) in [(cTs[:], d_cT, "c0"), (lng[:], d_lng, "c2"),
                                (lnb[:], d_lnb, "c3"), (convw[:], d_convw, "c4"), (pscale[:], d_pscale, "c5"),
                                (mixg[:], d_mixg, "c6"), (sinke[:], d_sinks, "c7"),
                                (invw[:], d_invw, "c9"), (valid[:], d_valid, "c10"), (ident[:], d_ident, "c11")]:
            ld(dst, src, key)
        ld(masks[:], d_masks, "c12", "pool")
        ld(identb[:], d_identb, "c14", "pool")
        ld(sinke4[:], d_sinks4, "c15")
        ld(cntr[:], d_cntr, "c16")
        for l in range(L):
            ld(poolw[:, l, :, :], d_poolw[l].rearrange("c p n -> p c n"), f"c13_{l}", "pool")
        for c in range(KC):
            P.dma("sp", lambda e, c=c: e.dma_start(out=X[:, c, :], in_=d_xT[:, c, :]), f"xl{c}",
                  writes=[B_X[c][b] for b in range(NB)])
        P.op("dve", lambda e: e.memset(ones[:], 1.0), writes=[B_const])
        P.op("dve", lambda e: e.memset(epsT[:, 0:1], LN_EPS), writes=[B_const])
        P.op("dve", lambda e: e.memset(epsT[:, 1:2], RMS_EPS), writes=[B_const])
        P.op("dve", lambda e: e.memset(KZS[:], 0.0), writes=[B_samp])
        P.op("act", lambda e: e.activation(out=SC_[:], in_=cTs[:], func=AF.Silu), reads=[B_const], writes=[B_const])
        P.op("act", lambda e: e.activation(out=sinke[:], in_=sinke[:], func=AF.Exp), reads=[B_const], writes=[B_const])
        P.op("act", lambda e: e.activation(out=sinke4[:], in_=sinke4[:], func=AF.Exp), reads=[B_const], writes=[B_const])

        ada = {"ring": None, "bufs": None, "n": 0}
        B_bada = Buf("bada")

        def ada_load_bias(l):
            P.dma("sp", lambda e: e.dma_start(out=bada[:], in_=d_bada[l]), "c1", writes=[B_bada])

        def ada_item(l, j):
            slot = ada["n"] % ada.get("nslot", 2)
            ada["n"] += 1
            tile = ada["ring"][slot]
            bslot = ada["bufs"][slot]
            P.dma("pool", lambda e: e.dma_start(out=tile, in_=d_wada[l, j]), f"ada{slot}", writes=[bslot])

            def mm(e):
                ins = None
                for kc in range(KC):
                    ins = e.matmul(bank(7, 0, 17), tile[:, kc, :], SC_[:, kc, :], start=(kc == 0), stop=(kc == KC - 1))
                return ins
            P.op("pe", mm, reads=[bslot, B_const], writes=[B_ps[7]])
            P.op("dve", lambda e: e.tensor_scalar(out=Mt[l % 2][:, j, :], in0=bank(7, 0, 17), scalar1=bada[:, j:j + 1],
                                                   scalar2=None, op0=ALU.add),
                 reads=[B_ps[7], B_bada], writes=[B_M[l % 2]])

        def gates(l, i):
            M = Mt[l % 2]
            row, f = [(2, 0.5), (5, 1.0), (8, 0.5)][i]
            P.op("dve", lambda e: e.tensor_scalar(out=GT[:, i, :, :], in0=M[:, row * KC:(row + 1) * KC, :],
                                                   scalar1=f, scalar2=None, op0=ALU.mult),
                 reads=[B_M[l % 2]], writes=[B_GT])

        def ln_consts(l, k, nxt, final=False):
            a = 1.0 if final else ALPHA
            rd = [B_const]
            P.op("dve", lambda e: e.tensor_scalar(out=lnc[:, 0, :], in0=lng[:, l, k, :], scalar1=a, scalar2=None, op0=ALU.mult),
                 reads=rd, writes=[B_lnc])
            P.op("dve", lambda e: e.tensor_scalar(out=lnc[:, 1, :], in0=lnb[:, l, k, :], scalar1=a, scalar2=None, op0=ALU.mult),
                 reads=rd, writes=[B_lnc])
            if nxt is None:
                return
            mi, shr, scr = nxt
            M = Mt[mi]
            rd = [B_const, B_M[mi]]
            P.op("dve", lambda e: e.scalar_tensor_tensor(out=lnc[:, 2, :], in0=M[:, scr * KC:(scr + 1) * KC, 0], scalar=1.0,
                                                          in1=lng[:, l, k, :], op0=ALU.add, op1=ALU.mult),
                 reads=rd, writes=[B_lnc])
            P.op("dve", lambda e: e.scalar_tensor_tensor(out=lnc[:, 3, :], in0=M[:, scr * KC:(scr + 1) * KC, 0], scalar=1.0,
                                                          in1=lnb[:, l, k, :], op0=ALU.add, op1=ALU.mult),
                 reads=rd, writes=[B_lnc])
            P.op("dve", lambda e: e.tensor_tensor(out=lnc[:, 3, :], in0=lnc[:, 3, :], in1=M[:, shr * KC:(shr + 1) * KC, 0], op=ALU.add),
                 reads=rd + [B_lnc], writes=[B_lnc])
            gb_ = lng[:, l, k, :].unsqueeze(2).to_broadcast([128, KC, NS])
            bb_ = lnb[:, l, k, :].unsqueeze(2).to_broadcast([128, KC, NS])
            P.op("dve", lambda e: e.scalar_tensor_tensor(out=lncs[:, 0, :, :], in0=M[:, scr * KC:(scr + 1) * KC, 1:17], scalar=1.0,
                                                          in1=gb_, op0=ALU.add, op1=ALU.mult),
                 reads=rd, writes=[B_lnc])
            P.op("dve", lambda e: e.scalar_tensor_tensor(out=lncs[:, 1, :, :], in0=M[:, scr * KC:(scr + 1) * KC, 1:17], scalar=1.0,
                                                          in1=bb_, op0=ALU.add, op1=ALU.mult),
                 reads=rd, writes=[B_lnc])
            P.op("dve", lambda e: e.tensor_tensor(out=lncs[:, 1, :, :], in0=lncs[:, 1, :, :], in1=M[:, shr * KC:(shr + 1) * KC, 1:17], op=ALU.add),
                 reads=rd + [B_lnc], writes=[B_lnc])

        lnS = {}

        NRING = 3

        def carve_ln(cv):
            lnS.clear()
            lnS["n"] = 0
            lnS["pend"] = []
            lnS["vb"] = [cv.take([FB], BF16) for _ in range(NRING)]
            lnS["vq"] = [cv.take([FB], BF16) for _ in range(NRING)]
            lnS["Bvb"] = [abuf(f"vb{i}") for i in range(NRING)]
            lnS["Bvq"] = [abuf(f"vq{i}") for i in range(NRING)]
            lnS["rstd"] = [cv.take([FB], F32) for _ in range(2)]
            lnS["nmr"] = [cv.take([FB], F32) for _ in range(2)]
            lnS["Brs"] = [abuf("rstd0"), abuf("rstd1")]
            lnS["Bnm"] = [abuf("nmr0"), abuf("nmr1")]
            lnS["sscr"] = cv.take([NS], F32)
            lnS["Bsscr"] = abuf("sscr")
            return lnS["Bvb"] + lnS["Bvq"] + lnS["Brs"] + lnS["Bnm"] + [lnS["Bsscr"]]

        def ln_stats_feed(bi, c, on_block_done):
            c0, w = FBLK[bi]
            r = lnS["n"] % NRING
            lnS["n"] += 1
            vb, vq = lnS["vb"][r], lnS["vq"][r]
            Bvb, Bvq = lnS["Bvb"][r], lnS["Bvq"][r]
            P.op("act", lambda e: act_id(e, out=vb[:, :w], in_=X[:, c, c0:c0 + w]), reads=[B_X[c][bi]], writes=[Bvb])
            P.op("act", lambda e: e.activation(out=vq[:, :w], in_=X[:, c, c0:c0 + w], func=AF.Square), reads=[B_X[c][bi]], writes=[Bvq])

            def pe_part():
                def mm(e):
                    e.matmul(bank(6, 0, w), ones[:], vb[:, :w], start=(c == 0), stop=(c == KC - 1))
                    return e.matmul(bank(7, 0, w), ones[:], vq[:, :w], start=(c == 0), stop=(c == KC - 1))
                P.op("pe", mm, reads=[Bvb, Bvq, B_const], writes=[B_ps[6], B_ps[7]])
                if c == KC - 1:
                    on_block_done(bi)
            lnS["pend"].append(pe_part)
            if len(lnS["pend"]) > 2:
                lnS["pend"].pop(0)()

        def ln_flush():
            while lnS["pend"]:
                lnS["pend"].pop(0)()

        def ln_math(bi):
            c0, w = FBLK[bi]
            q = bi % 2
            rstd, nmr = lnS["rstd"][q], lnS["nmr"][q]
            Brs, Bnm = lnS["Brs"][q], lnS["Bnm"][q]
            inv = 1.0 / D
            P.op("act", lambda e: e.activation(out=rstd[:, :w], in_=bank(6, 0, w), func=AF.Square, scale=inv), reads=[B_ps[6]], writes=[Brs])
            P.op("dve", lambda e: e.scalar_tensor_tensor(out=rstd[:, :w], in0=bank(7, 0, w), scalar=inv, in1=rstd[:, :w],
                                                          op0=ALU.mult, op1=ALU.subtract),
                 reads=[B_ps[7], Brs], writes=[Brs])
            P.op("act", lambda e: e.activation(out=rstd[:, :w], in_=rstd[:, :w], func=AF.Ln, bias=epsT[:, 0:1]), reads=[Brs, B_const], writes=[Brs])
            P.op("act", lambda e: e.activation(out=rstd[:, :w], in_=rstd[:, :w], func=AF.Exp, scale=-0.5), reads=[Brs], writes=[Brs])
            P.op("dve", lambda e: e.scalar_tensor_tensor(out=nmr[:, :w], in0=bank(6, 0, w), scalar=-inv, in1=rstd[:, :w],
                                                          op0=ALU.mult, op1=ALU.mult),
                 reads=[B_ps[6], Brs], writes=[Bnm])

        def ln_apply(bi, with_hin, emit_out=False):
            c0, w = FBLK[bi]
            samp = (c0 >= NP)
            q = bi % 2
            rstd, nmr = lnS["rstd"][q], lnS["nmr"][q]
            Brs, Bnm = lnS["Brs"][q], lnS["Bnm"][q]
            tmp, Btmp = lnS["sscr"], lnS["Bsscr"]
            for c in range(KC):
                xs = X[:, c, c0:c0 + w]
                Bx = B_X[c][bi]
                P.op("dve", lambda e, xs=xs: e.tensor_tensor(out=xs, in0=xs, in1=rstd[:, :w], op=ALU.mult), reads=[Bx, Brs], writes=[Bx])
                P.op("dve", lambda e, xs=xs: e.tensor_tensor(out=xs, in0=xs, in1=nmr[:, :w], op=ALU.add), reads=[Bx, Bnm], writes=[Bx])
                if with_hin:
                    hs = HIN[:, c, c0:c0 + w]
                    hb = hin_bufs(c0, w, [c])
                    if not samp:
                        P.op("pool", lambda e, xs=xs, hs=hs, c=c: e.tensor_scalar(out=hs, in0=xs, scalar1=lnc[:, 2, c:c + 1],
                                                                                   scalar2=lnc[:, 3, c:c + 1], op0=ALU.mult, op1=ALU.add),
                             reads=[Bx, B_lnc], writes=hb)
                    else:
                        P.op("dve", lambda e, xs=xs, c=c: e.tensor_tensor(out=tmp[:, :w], in0=xs, in1=lncs[:, 0, c, :], op=ALU.mult),
                             reads=[Bx, B_lnc], writes=[Btmp])
                        P.op("dve", lambda e, hs=hs, c=c: e.tensor_tensor(out=hs, in0=tmp[:, :w], in1=lncs[:, 1, c, :], op=ALU.add),
                             reads=[Btmp, B_lnc], writes=hb)
                P.op("act", lambda e, xs=xs, c=c: act_id(e, xs, xs, lnc[:, 0, c:c + 1], lnc[:, 1, c:c + 1]),
                     reads=[Bx, B_lnc], writes=[Bx])
                if emit_out and c0 >= HALO:
                    P.dma("sp", lambda e, xs=xs, c=c: e.dma_start(out=o_x[:, c, c0 - HALO:c0 - HALO + w], in_=xs), f"ox{c}",
                          reads=[Bx, B_out])

        def resid_evac(bi, o, ps_ap, ps_bufs, gi):
            c0, w = FBLK[bi]
            xs = X[:, o, c0:c0 + w]
            if c0 < NP:
                P.op("dve", lambda e: e.scalar_tensor_tensor(out=xs, in0=ps_ap, scalar=GT[:, gi, o, 0:1], in1=xs,
                                                              op0=ALU.mult, op1=ALU.add),
                     reads=ps_bufs + [B_GT, B_X[o][bi]], writes=[B_X[o][bi]])
            else:
                t = lnS["sscr"]
                Bt = lnS["Bsscr"]
                P.op("dve", lambda e: e.tensor_tensor(out=t[:, :w], in0=ps_ap, in1=GT[:, gi, o, 1:17], op=ALU.mult),
                     reads=ps_bufs + [B_GT], writes=[Bt])
                P.op("dve", lambda e: e.tensor_tensor(out=xs, in0=t[:, :w], in1=xs, op=ALU.add),
                     reads=[Bt, B_X[o][bi]], writes=[B_X[o][bi]])

        prev_phase = ["init"]
        ffn_keep = {}

        def ffn_phase(l, which, ada_next, ln_k, nxt, cs, final=False, ada_list=None):
            set_start(cs)
            if prev_phase[0] == "ffn":
                HID, GU, DW, SG, B_hid, B_gu, B_dw, B_sg, ring, rbufs, lsave = ffn_keep["v"]
                ada["ring"], ada["bufs"] = ring, rbufs
                keepn, keepp = lnS["n"], lnS["pend"]
                lnS.clear()
                lnS.update(lsave)
                lnS["n"], lnS["pend"] = keepn, keepp
            else:
                cv = Carver()
                HID = cv.take([GMAX, NT], BF16)
                GU = [cv.take([2 * KC * 128], BF16) for _ in range(2)]
                DW = cv.take([GMAX, D], BF16)
                ada["ring"] = [cv.take([KC, 128], BF16) for _ in range(2)]
                SG = cv.take([FB], F32)
                assert cv.off <= LN_OFF, cv.off
                lnb_ = carve_ln(Carver(LN_OFF))
                B_hid = [[abuf(f"hid{m}_{b}") for b in range(NB)] for m in range(GMAX)]
                B_gu = [abuf(f"gu{i}") for i in range(2)]
                B_dw = [abuf(f"dw{i}") for i in range(GMAX)]
                B_sg = abuf("sg")
                ada["bufs"] = [abuf("adar0"), abuf("adar1")]
                phase_switch([b for row in B_hid for b in row] + B_gu + B_dw + [B_sg] + ada["bufs"], newT=lnb_)
                ffn_keep["v"] = (HID, GU, DW, SG, B_hid, B_gu, B_dw, B_sg, ada["ring"], ada["bufs"], dict(lnS))
            prev_phase[0] = "ffn"

            gi = 0 if which == 0 else 2
            ada_js = (list(range(72)) if ada_list is None else list(ada_list)) if ada_next is not None else []
            ada_total = len(ada_js)
            if ada_next is not None and ada_list is None:
                ada_load_bias(ada_next)
            gates(l, gi)
            n_gu = 0
            n_ps = 0
            n_pd = 0
            m0 = 0
            ada_slot = [0]

            prevb = [None]

            def blk_done(bi):
                ln_math(bi)
                if prevb[0] is not None:
                    ln_apply(prevb[0], nxt is not None, final)
                prevb[0] = bi
            for g, gsz in enumerate(GROUPS):
                last = (g == len(GROUPS) - 1)
                for mi in range(gsz):
                    m = m0 + mi
                    slot = n_gu % 2
                    n_gu += 1
                    gut = GU[slot]
                    P.dma("pool", lambda e, gut=gut, m=m: e.dma_start(out=gut, in_=d_wgu[which][l, m]), f"gu{slot}", writes=[B_gu[slot]])
                    for bi, (c0, w) in enumerate(FBLK):
                        if w == 0:
                            continue
                        pg, pu = (0, 1) if n_ps % 2 == 0 else (2, 3)
                        n_ps += 1

                        def mm(e, gut=gut, c0=c0, w=w, pg=pg, pu=pu):
                            ins = None
                            for kc in range(KC):
                                ins = e.matmul(bank(pg, 0, w), gut[:, kc * 128:(kc + 1) * 128], HIN[:, kc, c0:c0 + w],
                                               start=(kc == 0), stop=(kc == KC - 1))
                            for kc in range(KC):
                                ins = e.matmul(bank(pu, 0, w), gut[:, (KC + kc) * 128:(KC + kc + 1) * 128], HIN[:, kc, c0:c0 + w],
                                               start=(kc == 0), stop=(kc == KC - 1))
                            return ins
                        P.op("pe", mm, reads=[B_gu[slot]] + hin_bufs(c0, w), writes=[B_ps[pg], B_ps[pu]])
                        P.op("act", lambda e, pg=pg, w=w: e.activation(out=SG[:, :w], in_=bank(pg, 0, w), func=AF.Silu),
                             reads=[B_ps[pg]], writes=[B_sg])
                        P.op("dve", lambda e, pu=pu, w=w, mi=mi, c0=c0: e.tensor_tensor(out=HID[:, mi, c0:c0 + w], in0=SG[:, :w],
                                                                                        in1=bank(pu, 0, w), op=ALU.mult),
                             reads=[B_sg, B_ps[pu]], writes=[B_hid[mi][bi]])
                        ada_slot[0] += 1
                        if ada_js and (ada_total - len(ada_js)) < (ada_slot[0] * ada_total) // 100:
                            ada_item(ada_next, ada_js.pop(0))
                if last:
                    while ada_js:
                        ada_item(ada_next, ada_js.pop(0))
                    ln_consts(l, ln_k, nxt, final)
                for mi in range(gsz):
                    m = m0 + mi
                    P.dma("pool", lambda e, mi=mi, m=m: e.dma_start(out=DW[:, mi, :], in_=d_wd[which][l, m]), f"dw{mi}", writes=[B_dw[mi]])
                for bi, (c0, w) in enumerate(FBLK):
                    if w == 0:
                        continue
                    for o in range(KC):
                        pd = 4 + (n_pd % 2)
                        n_pd += 1

                        def mm(e, c0=c0, w=w, pd=pd, o=o, gsz=gsz):
                            ins = None
                            for mi in range(gsz):
                                ins = e.matmul(bank(pd, 0, w), DW[:, mi, o * 128:(o + 1) * 128], HID[:, mi, c0:c0 + w],
                                               start=(mi == 0), stop=(mi == gsz - 1))
                            return ins
                        P.op("pe", mm, reads=B_dw[:gsz] + [B_hid[mi][bi] for mi in range(gsz)], writes=[B_ps[pd]])
                        resid_evac(bi, o, bank(pd, 0, w), [B_ps[pd]], gi)
                        if last:
                            ln_stats_feed(bi, o, blk_done)
                if last:
                    ln_flush()
                    ln_apply(NB - 1, nxt is not None, final)
                m0 += gsz
            while ada_js:
                ada_item(ada_next, ada_js.pop(0))

        def mixer_A(l, cs):
            prev_phase[0] = "A"
            cv = Carver()
            WIN_ = cv.take([KC, WEXT], BF16)
            off_after_win = cv.off
            CVx = cv.take([2, TB + 2], F32)
            GBf = cv.take([2 * TB], F32)
            GBs = GBf.rearrange("p (c t) -> p c t", c=2)
            TMP = [cv.take([TB], F32) for _ in range(2)]
            Ux = cv.take([2, TB + 15], F32)
            Sa = cv.take([TB + 15], F32)
            Sb = cv.take([TB + 15], F32)
            PL = cv.take([2, TB], BF16)
            Q = cv.take([4, TB], BF16)
            KZ = cv.take([4, TB + 128], BF16)
            assert cv.off <= LN_OFF, cv.off
            cv = Carver(LN_OFF)
            Vd = [cv.take([256], BF16) for _ in range(3)]
            PT = cv.take([1024], BF16)
            RD = cv.take([4, 128], F32)
            Y = cv.take([KC, TB], F32)
            SQ = [cv.take([TB], BF16) for _ in range(2)]
            RSf = cv.take([3 * TB], F32)
            RS = RSf.rearrange("p (g t) -> p g t", g=3)
            B_win = [abuf(f"win{i}") for i in range(4)]
            B_cvx, B_gbs, B_ux, B_sa, B_sb, B_pl, B_q, B_kz = (abuf(n) for n in ("cvx", "gbs", "ux", "sa", "sb", "pl", "q", "kz"))
            B_tmp = [abuf("tmp0"), abuf("tmp1")]
            B_vd = [abuf(f"vd{i}") for i in range(3)]
            B_pt, B_rd, B_rs = abuf("pt"), abuf("rd"), abuf("rs")
            B_y = [abuf(f"y{c}") for c in range(KC)]
            B_sq = [abuf("sq0"), abuf("sq1")]
            phase_switch(B_win + [B_cvx, B_gbs, B_ux, B_sa, B_sb, B_pl, B_q, B_kz] + B_tmp,
                         newT=B_vd + [B_pt, B_rd, B_rs] + B_y + B_sq)

            for i in range(4):
                P.dma("pool", lambda e, i=i: e.dma_start(out=WIN_[:, :, i * 512:(i + 1) * 512], in_=d_win[l, i]), f"win{i}", writes=[B_win[i]])
            P.op("dve", lambda e: e.memset(KZ[:], 0.0), writes=[B_kz])
            P.op("dve", lambda e: e.memset(CVx[:, :, 0:2], 0.0), writes=[B_cvx])
            P.op("dve", lambda e: e.memset(Ux[:, :, 0:15], 0.0), writes=[B_ux])

            st_ = {"hs": 0, "sq": 0}

            def inproj_chunk(c0, w, col0, evac):
                s = st_["hs"] % 4
                st_["hs"] += 1
                bk, hf = s // 2, s % 2
                wi = col0 // 512

                def mm(e):
                    ins = None
                    for kc in range(KC):
                        ins = e.matmul(bank(bk, hf * 256, hf * 256 + w), WIN_[:, kc, col0:col0 + 128], HIN[:, kc, c0:c0 + w],
                                       start=(kc == 0), stop=(kc == KC - 1))
                    return ins
                P.op("pe", mm, reads=[B_win[wi]] + hin_bufs(c0, w), writes=[B_hb[bk][hf]])
                evac(lambda p0=0, p1=128: bank(bk, hf * 256, hf * 256 + w, p0, p1), [B_hb[bk][hf]])

            def rms_group(grp, chunks, n, w, ysrc, ybufs, rs_ap, sqbank, sqlo):
                for i, c in enumerate(chunks):
                    r = st_["sq"] % 2
                    st_["sq"] += 1
                    sq = SQ[r]
                    P.op("act", lambda e, sq=sq, c=c: e.activation(out=sq[:, :w], in_=ysrc(c), func=AF.Square),
                         reads=[ybufs[c]], writes=[B_sq[r]])
                    P.op("pe", lambda e, sq=sq, i=i: e.matmul(bank(sqbank, sqlo, sqlo + w), ones[:], sq[:, :w], start=(i == 0),
                                                              stop=(i == len(chunks) - 1)),
                         reads=[B_sq[r], B_const], writes=[B_ps[sqbank]])
                P.op("act", lambda e: e.activation(out=rs_ap, in_=bank(sqbank, sqlo, sqlo + w), func=AF.Ln, scale=1.0 / n,
                                                   bias=epsT[:, 1:2]),
                     reads=[B_ps[sqbank], B_const], writes=[B_rs])
                P.op("act", lambda e: e.activation(out=rs_ap, in_=rs_ap, func=AF.Exp, scale=-0.5), reads=[B_rs], writes=[B_rs])

            JOWN = HALO // 128
            JS = cs // 128

            def tb_body(t, c0, w, first_own):
                if first_own:
                    P.op("pool", lambda e: e.tensor_scalar(out=CVx[:, :, 0:2], in0=CVx[:, :, 0:2], scalar1=valid[:, 0:1], scalar2=0.0,
                                                            op0=ALU.mult, op1=ALU.add), reads=[B_cvx, B_const], writes=[B_cvx])
                    P.op("pool", lambda e: e.tensor_scalar(out=Ux[:, :, 0:15], in0=Ux[:, :, 0:15], scalar1=valid[:, 0:1], scalar2=0.0,
                                                            op0=ALU.mult, op1=ALU.add), reads=[B_ux, B_const], writes=[B_ux])
                for c in range(4):
                    inproj_chunk(c0, w, 1024 + c * 128,
                                 lambda ps, pb, c=c: P.op("act", lambda e: act_id(e, out=Q[:, c, :w], in_=ps()), reads=pb, writes=[B_q]))
                for h in range(2):
                    def ev(ps, pb, h=h):
                        P.op("dve", lambda e: e.tensor_copy(out=KZ[0:64, 2 * h, 128:128 + w], in_=ps(0, 64)), reads=pb, writes=[B_kz])
                        P.op("dve", lambda e: e.tensor_copy(out=KZ[64:128, 2 * h + 1, 128:128 + w], in_=ps(64, 128)), reads=pb, writes=[B_kz])
                    inproj_chunk(c0, w, 1536 + h * 128, ev)
                thunks = []

                def th_u(cc):
                    inproj_chunk(c0, w, 768 + cc * 128,
                                 lambda ps, pb: P.op("act", lambda e: act_id(e, out=Ux[:, cc, 15:15 + w], in_=ps()), reads=pb, writes=[B_ux]))

                def th_gc(cc):
                    tm = TMP[cc]
                    inproj_chunk(c0, w, 256 + cc * 128,
                                 lambda ps, pb: P.op("act", lambda e: act_id(e, out=tm[:, :w], in_=ps()), reads=pb, writes=[B_tmp[cc]]))

                def th_xin(cc):
                    tm = TMP[cc]
                    inproj_chunk(c0, w, 512 + cc * 128,
                                 lambda ps, pb: P.op("dve", lambda e: e.tensor_tensor(out=CVx[:, cc, 2:2 + w], in0=tm[:, :w], in1=ps(), op=ALU.mult),
                                                     reads=pb + [B_tmp[cc]], writes=[B_cvx]))

                def th_gb(cc):
                    inproj_chunk(c0, w, cc * 128,
                                 lambda ps, pb: P.op("act", lambda e: act_id(e, out=GBs[:, cc, :w], in_=ps()), reads=pb, writes=[B_gbs]))
                for jj in range(w // 128):
                    j = c0 // 128 + jj
                    vs = j % 3
                    ca = c0 + jj * 128

                    def mm(e, ca=ca):
                        ins = None
                        for kc in range(KC):
                            ins = e.matmul(bank(2, 0, 256), HIN[:, kc, ca:ca + 128], WIN_[:, kc, 1792:2048], start=(kc == 0), stop=(kc == KC - 1))
                        return ins
                    P.op("pe", mm, reads=[B_win[3]] + hin_bufs(ca, 128), writes=[B_ps[2]])
                    P.op("act", lambda e, vs=vs: act_id(e, out=Vd[vs][:, :], in_=bank(2, 0, 256)), reads=[B_ps[2]], writes=[B_vd[vs]])
                def conv_chain(cc):
                    acc = TMP[cc]
                    t2 = TMP[1 - cc]
                    P.op("pool", lambda e, cc=cc, acc=acc: e.tensor_scalar(out=acc[:, :w], in0=CVx[:, cc, 0:w], scalar1=convw[:, l, 0, cc:cc + 1],
                                                                            scalar2=0.0, op0=ALU.mult, op1=ALU.add),
                         reads=[B_cvx, B_const], writes=[B_tmp[cc]])
                    for k in (1, 2):
                        P.op("pool", lambda e, cc=cc, t2=t2, k=k: e.tensor_scalar(out=t2[:, :w], in0=CVx[:, cc, k:k + w],
                                                                                   scalar1=convw[:, l, k, cc:cc + 1], scalar2=0.0,
                                                                                   op0=ALU.mult, op1=ALU.add),
                             reads=[B_cvx, B_const], writes=[B_tmp[1 - cc]])
                        P.op("pool", lambda e, acc=acc, t2=t2: e.tensor_tensor(out=acc[:, :w], in0=acc[:, :w], in1=t2[:, :w], op=ALU.add),
                             reads=[B_tmp[0], B_tmp[1]], writes=[B_tmp[cc]])
                    P.op("pool", lambda e, cc=cc, acc=acc: e.tensor_tensor(out=Y[:, cc, :w], in0=acc[:, :w], in1=GBs[:, cc, :w], op=ALU.mult),
                         reads=[B_tmp[cc], B_gbs], writes=[B_y[cc]])
                def conv_tail():
                    P.op("pool", lambda e: e.tensor_copy(out=CVx[:, :, 0:2], in_=CVx[:, :, w:w + 2]), reads=[B_cvx], writes=[B_cvx])
                WX = w + 15

                def pool_chain(cc):
                    ux = Ux[:, cc, :]
                    P.op("pool", lambda e, ux=ux: e.tensor_tensor(out=Sa[:, 1:WX], in0=ux[:, 1:WX], in1=ux[:, 0:WX - 1], op=ALU.add),
                         reads=[B_ux], writes=[B_sa])
                    if cc == 0:
                        P.op("pool", lambda e: e.tensor_tensor(out=Sb[64:128, 3:WX], in0=Sa[64:128, 3:WX], in1=Sa[64:128, 1:WX - 2], op=ALU.add),
                             reads=[B_sa], writes=[B_sb])
                    else:
                        P.op("pool", lambda e: e.tensor_tensor(out=Sb[:, 3:WX], in0=Sa[:, 3:WX], in1=Sa[:, 1:WX - 2], op=ALU.add),
                             reads=[B_sa], writes=[B_sb])
                        P.op("pool", lambda e: e.tensor_tensor(out=Sa[:, 7:WX], in0=Sb[:, 7:WX], in1=Sb[:, 3:WX - 4], op=ALU.add),
                             reads=[B_sb], writes=[B_sa])
                        P.op("pool", lambda e: e.tensor_tensor(out=Sb[64:128, 15:WX], in0=Sa[64:128, 15:WX], in1=Sa[64:128, 7:WX - 8], op=ALU.add),
                             reads=[B_sa], writes=[B_sb])
                    for (p0, p1, src, Bs) in ((0, 64, Sa, B_sa), (64, 128, Sb, B_sb)):
                        P.op("pool", lambda e, p0=p0, p1=p1, src=src, cc=cc: e.tensor_scalar(
                            out=src[p0:p1, 15:WX], in0=src[p0:p1, 15:WX], scalar1=invw[p0:p1, cc:cc + 1], scalar2=0.0, op0=ALU.mult, op1=ALU.add),
                            reads=[Bs, B_const], writes=[Bs])
                        if first_own:
                            P.op("pool", lambda e, p0=p0, p1=p1, src=src, cc=cc: e.tensor_tensor(
                                out=src[p0:p1, 15:31], in0=src[p0:p1, 15:31], in1=cntr[p0:p1, cc, :], op=ALU.mult),
                                reads=[Bs, B_const], writes=[Bs])
                        P.op("pool", lambda e, p0=p0, p1=p1, src=src, cc=cc: e.tensor_tensor(
                            out=PL[p0:p1, cc, :w], in0=src[p0:p1, 15:WX], in1=Ux[p0:p1, cc, 15:WX], op=ALU.subtract),
                            reads=[Bs, B_ux], writes=[B_pl])

                def pool_mm(cc):
                    P.op("pe", lambda e, cc=cc: e.matmul(bank(3, 0, w), poolw[:, l, cc, :], PL[:, cc, :w], start=True, stop=True),
                         reads=[B_pl, B_const], writes=[B_ps[3]])
                    P.op("act", lambda e, cc=cc: act_id(e, Y[:, 2 + cc, :w], bank(3, 0, w), pscale[:, l, cc:cc + 1]),
                         reads=[B_ps[3], B_const], writes=[B_y[2 + cc]])
                def pool_tail():
                    P.op("pool", lambda e: e.tensor_copy(out=Ux[:, :, 0:15], in_=Ux[:, :, w:w + 15]), reads=[B_ux], writes=[B_ux])
                thunks += [lambda: th_u(0), lambda: (th_u(1), pool_chain(0), pool_chain(1), pool_tail()),
                           lambda: th_gc(0), lambda: th_xin(0), lambda: (th_gb(0), conv_chain(0)),
                           lambda: th_gc(1), lambda: th_xin(1), lambda: (th_gb(1), conv_chain(1), conv_tail()),
                           lambda: pool_mm(0), lambda: pool_mm(1)]
                nper = 2 if w == TB else 4

                def pop_thunks(n):
                    for _ in range(n):
                        if thunks:
                            thunks.pop(0)()
                for jj in range(w // 128):
                    j = c0 // 128 + jj
                    qlo = jj * 128
                    kprev = qlo
                    kdiag = 128 + qlo
                    has_prev = j > JS
                    mprev = 2 if j == JOWN else 0
                    for h in range(2):
                        def mm(e, h=h, qlo=qlo, kprev=kprev, kdiag=kdiag, has_prev=has_prev, mprev=mprev):
                            ins = None
                            for g in range(4):
                                c = 2 * h + g // 2
                                half = g % 2
                                if has_prev:
                                    e.matmul(bank(4, g * 128, (g + 1) * 128), identb[:], masks[:, mprev, :], start=True, stop=False)
                                    ins = e.matmul(bank(4, g * 128, (g + 1) * 128), KZ[:, 2 * h + half, kprev:kprev + 128],
                                                   Q[:, c, qlo:qlo + 128], start=False, stop=True)
                                e.matmul(bank(5, g * 128, (g + 1) * 128), identb[:], masks[:, 1, :], start=True, stop=False)
                                ins = e.matmul(bank(5, g * 128, (g + 1) * 128), KZ[:, 2 * h + half, kdiag:kdiag + 128],
                                               Q[:, c, qlo:qlo + 128], start=False, stop=True)
                            return ins
                        P.op("pe", mm, reads=[B_kz, B_q, B_const], writes=[B_ps[4], B_ps[5]])
                        if has_prev:
                            P.op("act", lambda e: e.activation(out=PT[:, 0:512], in_=bank(4), func=AF.Exp, scale=0.125),
                                 reads=[B_ps[4]], writes=[B_pt])
                        P.op("act", lambda e: e.activation(out=PT[:, 512:1024], in_=bank(5), func=AF.Exp, scale=0.125),
                             reads=[B_ps[5]], writes=[B_pt])
                        pop_thunks(nper)
                        vprev, vcur = (j - 1) % 3, j % 3

                        def pv(e, h=h, vprev=vprev, vcur=vcur, has_prev=has_prev):
                            if has_prev:
                                e.matmul(bank(6), Vd[vprev][:, h * 128:(h + 1) * 128], PT[:, 0:512], start=True, stop=False)
                            e.matmul(bank(6), Vd[vcur][:, h * 128:(h + 1) * 128], PT[:, 512:1024], start=not has_prev, stop=True)
                            if has_prev:
                                e.matmul(bank(7), ones[:], PT[:, 0:512], start=True, stop=False)
                            return e.matmul(bank(7), ones[:], PT[:, 512:1024], start=not has_prev, stop=True)
                        P.op("pe", pv, reads=[B_pt, B_vd[vprev], B_vd[vcur], B_const], writes=[B_ps[6], B_ps[7]])
                        sk4 = sinke4[:, l, 4 * h:4 * h + 4].unsqueeze(2).to_broadcast([128, 4, 128])
                        P.op("dve", lambda e, sk4=sk4: e.tensor_tensor(out=RD[:, :, :], in0=bank(7).rearrange("p (g q) -> p g q", g=4), in1=sk4,
                                                                        op=ALU.add), reads=[B_ps[7], B_const], writes=[B_rd])
                        P.op("act", lambda e: e.activation(out=RD[:, :, :], in_=RD[:, :, :], func=AF.Ln), reads=[B_rd], writes=[B_rd])
                        P.op("act", lambda e: e.activation(out=RD[:, :, :], in_=RD[:, :, :], func=AF.Exp, scale=-1.0), reads=[B_rd], writes=[B_rd])
                        for half in range(2):
                            p0, p1 = half * 64, half * 64 + 64
                            oo = bank(6, 0, 512, p0, p1).rearrange("p (gg hf q) -> p gg hf q", gg=2, hf=2)[:, :, half, :]
                            rr = RD[p0:p1, :, :].rearrange("p (gg hf) q -> p gg hf q", hf=2)[:, :, half, :]
                            P.op("dve", lambda e, oo=oo, rr=rr, p0=p0, p1=p1, h=h, qlo=qlo: e.tensor_tensor(
                                out=Y[p0:p1, 4 + 2 * h:6 + 2 * h, qlo:qlo + 128], in0=oo, in1=rr, op=ALU.mult),
                                reads=[B_ps[6], B_rd], writes=[B_y[4 + 2 * h], B_y[5 + 2 * h]])
                pop_thunks(100)
                P.op("act", lambda e: act_id(e, out=KZ[:, :, 0:128], in_=KZ[:, :, w:w + 128]), reads=[B_kz], writes=[B_kz])
                if t == NTB - 1:
                    token_major_tail(l, NP - 128, 128, WIN_, B_win, GBf, B_gbs, TMP[0], B_tmp[0], False, RSf, B_rs)
                ysrc = lambda c, w=w: Y[:, c, :w]
                slots = [(2, 256), (3, 0), (3, 256)]
                for grp, (chunks, n) in enumerate([([0, 1], 256.0), ([2, 3], 256.0), ([4, 5, 6, 7], 512.0)]):
                    bk_, lo_ = slots[grp]
                    for i, c in enumerate(chunks):
                        r = st_["sq"] % 2
                        st_["sq"] += 1
                        sq = SQ[r]
                        P.op("act", lambda e, sq=sq, c=c, n=n: e.activation(out=sq[:, :w], in_=Y[:, c, :w], func=AF.Square, scale=float(n) ** -0.5),
                             reads=[B_y[c]], writes=[B_sq[r]])
                        P.op("pe", lambda e, sq=sq, i=i, bk_=bk_, lo_=lo_, chunks=chunks: e.matmul(bank(bk_, lo_, lo_ + w), ones[:], sq[:, :w],
                                                                                                 start=(i == 0), stop=(i == len(chunks) - 1)),
                             reads=[B_sq[r], B_const], writes=[B_ps[bk_]])
                P.op("act", lambda e: e.activation(out=RSf[:, 0:3 * TB], in_=psum[:, 2 * 512 + 256:4 * 512], func=AF.Ln, bias=epsT[:, 1:2]),
                     reads=[B_ps[2], B_ps[3], B_const], writes=[B_rs])
                P.op("act", lambda e: e.activation(out=RSf[:, 0:3 * TB], in_=RSf[:, 0:3 * TB], func=AF.Exp, scale=-0.5), reads=[B_rs], writes=[B_rs])
                for c in range(KC):
                    grp = 0 if c < 2 else (1 if c < 4 else 2)
                    P.op("dve", lambda e, c=c, grp=grp, c0=c0, w=w: e.scalar_tensor_tensor(out=HIN[:, c, c0:c0 + w], in0=Y[:, c, :w],
                                                                                scalar=mixg[:, l, c:c + 1], in1=RS[:, grp, :w],
                                                                                op0=ALU.mult, op1=ALU.mult),
                         reads=[B_y[c], B_rs, B_const], writes=hin_bufs(c0, w, [c]))


            for t in range(NTB):
                c0_ = max(t * TB, cs)
                w_ = (t + 1) * TB - c0_
                if w_ > 0:
                    tb_body(t, c0_, w_, c0_ == HALO)

            cs = Carver(off_after_win)
            cvS = cs.take([2, NS], F32)
            gbS = cs.take([2, NS], F32)
            tmS = cs.take([NS], F32)
            accS = cs.take([NS], F32)
            S1 = cs.take([NS], F32)
            SCV = cs.take([2, 2, NS], F32)
            U16 = cs.take([2, NS, 16], F32)
            PLS = cs.take([2, NS], BF16)
            STa = cs.take([512], F32)
            STb = cs.take([256], F32)
            STc = cs.take([512], F32)
            B_s = abuf("sampA")
            B_scv, B_u16 = abuf("scv"), abuf("u16")
            B_sta, B_stb, B_stc = abuf("sta"), abuf("stb"), abuf("stc")
            phase_switch([B_s, B_scv, B_u16, B_sta, B_stb, B_stc], keep=B_win)
            P.dma("sp", lambda e: e.dma_start(out=SCV[:], in_=d_sconvT[l]), "scv", writes=[B_scv])
            P.dma("sp", lambda e: e.dma_start(out=U16[:], in_=d_spoolT[l]), "u16", writes=[B_u16])
            c0, w = NP, NS
            for cc in range(2):
                inproj_chunk(c0, w, 256 + cc * 128,
                             lambda ps, pb: P.op("act", lambda e: act_id(e, out=tmS[:, :], in_=ps()), reads=pb, writes=[B_s]))
                inproj_chunk(c0, w, 512 + cc * 128,
                             lambda ps, pb, cc=cc: P.op("dve", lambda e: e.tensor_tensor(out=cvS[:, cc, :], in0=tmS[:, :], in1=ps(), op=ALU.mult),
                                                        reads=pb + [B_s], writes=[B_s]))
                inproj_chunk(c0, w, cc * 128,
                             lambda ps, pb, cc=cc: P.op("act", lambda e: act_id(e, out=gbS[:, cc, :], in_=ps()), reads=pb, writes=[B_s]))
                inproj_chunk(c0, w, 768 + cc * 128,
                             lambda ps, pb, cc=cc: P.op("act", lambda e: act_id(e, out=U16[:, cc, :, 15], in_=ps()), reads=pb, writes=[B_u16]))
            for c in range(4):
                inproj_chunk(c0, w, 1024 + c * 128,
                             lambda ps, pb, c=c: P.op("act", lambda e: act_id(e, out=QS[:, c, :], in_=ps()), reads=pb, writes=[B_samp]))
            for h in range(2):
                def ev(ps, pb, h=h):
                    P.op("dve", lambda e: e.tensor_copy(out=KZS[0:64, 2 * h, :], in_=ps(0, 64)), reads=pb, writes=[B_samp])
                    P.op("dve", lambda e: e.tensor_copy(out=KZS[64:128, 2 * h + 1, :], in_=ps(64, 128)), reads=pb, writes=[B_samp])
                inproj_chunk(c0, w, 1536 + h * 128, ev)
            for cc in range(2):
                P.op("dve", lambda e, cc=cc: e.tensor_scalar(out=accS[:, :], in0=SCV[:, cc, 0, :], scalar1=convw[:, l, 0, cc:cc + 1], scalar2=None,
                                                             op0=ALU.mult), reads=[B_scv, B_const], writes=[B_s])
                P.op("dve", lambda e, cc=cc: e.scalar_tensor_tensor(out=accS[:, :], in0=SCV[:, cc, 1, :], scalar=convw[:, l, 1, cc:cc + 1],
                                                                    in1=accS[:, :], op0=ALU.mult, op1=ALU.add),
                     reads=[B_scv, B_const, B_s], writes=[B_s])
                P.op("dve", lambda e, cc=cc: e.scalar_tensor_tensor(out=accS[:, :], in0=cvS[:, cc, :], scalar=convw[:, l, 2, cc:cc + 1],
                                                                    in1=accS[:, :], op0=ALU.mult, op1=ALU.add),
                     reads=[B_const, B_s], writes=[B_s])
                P.op("dve", lambda e, cc=cc: e.tensor_tensor(out=YS[:, cc, :], in0=accS[:, :], in1=gbS[:, cc, :], op=ALU.mult),
                     reads=[B_s], writes=[B_samp])
            for cc in range(2):
                for half in range(2):
                    p0, p1 = half * 64, half * 64 + 64
                    wd_ = 2 ** (2 * cc + half + 1)
                    P.op("dve", lambda e, p0=p0, p1=p1, cc=cc, wd_=wd_: e.tensor_reduce(out=S1[p0:p1, :], in_=U16[p0:p1, cc, :, 16 - wd_:16],
                                                                                         axis=mybir.AxisListType.X, op=ALU.add),
                         reads=[B_u16], writes=[B_s])
                    P.op("dve", lambda e, p0=p0, p1=p1, cc=cc: e.scalar_tensor_tensor(out=PLS[p0:p1, cc, :], in0=S1[p0:p1, :],
                                                                                      scalar=invw[p0:p1, cc:cc + 1], in1=U16[p0:p1, cc, :, 15],
                                                                                      op0=ALU.mult, op1=ALU.subtract),
                         reads=[B_s, B_u16, B_const], writes=[B_s])
                P.op("pe", lambda e, cc=cc: e.matmul(bank(3, 0, NS), poolw[:, l, cc, :], PLS[:, cc, :], start=True, stop=True),
                     reads=[B_s, B_const], writes=[B_ps[3]])
                P.op("act", lambda e, cc=cc: act_id(e, YS[:, 2 + cc, :], bank(3, 0, NS), pscale[:, l, cc:cc + 1]),
                     reads=[B_ps[3], B_const], writes=[B_samp])
            token_major_tail(l, NP, NS, WIN_, B_win, STa, B_sta, STb, B_stb, True, STc, B_stc)

        def token_major_tail(l, ca, M, WIN_, B_win, ST, B_st, ST2, B_st2, is_sample, ST3=None, B_st3=None):
            hb = hin_bufs(ca, M)
            if ST3 is None:
                raise ValueError

            def mm_pass(bk, col0, n):
                def mm(e):
                    ins = None
                    for kc in range(KC):
                        ins = e.matmul(bank(bk, 0, n, 0, M), HIN[:, kc, ca:ca + M], WIN_[:, kc, col0:col0 + n], start=(kc == 0), stop=(kc == KC - 1))
                    return ins
                wis = sorted(set([col0 // 512, (col0 + n - 1) // 512]))
                P.op("pe", mm, reads=[B_win[i] for i in wis] + hb, writes=[B_ps[bk]] + B_hb[bk])
            mm_pass(0, 256, 512)
            P.op("act", lambda e: act_id(e, out=ST2[0:M, 0:256], in_=bank(0, 0, 256, 0, M)), reads=[B_ps[0], B_hb[0][0], B_hb[0][1]], writes=[B_st2])
            P.op("dve", lambda e: e.tensor_tensor(out=ST[0:M, 0:256], in0=ST2[0:M, 0:256], in1=bank(0, 256, 512, 0, M), op=ALU.mult),
                 reads=[B_ps[0], B_hb[0][0], B_hb[0][1], B_st2], writes=[B_st])
            mm_pass(1, 768, 256)
            P.op("act", lambda e: act_id(e, out=ST[0:M, 256:512], in_=bank(1, 0, 256, 0, M)), reads=[B_ps[1], B_hb[1][0], B_hb[1][1]], writes=[B_st])
            mm_pass(0, 1536, 512)
            P.op("act", lambda e: act_id(e, out=ST3[0:M, 0:512], in_=bank(0, 0, 512, 0, M)), reads=[B_ps[0], B_hb[0][0], B_hb[0][1]], writes=[B_st3])
            kv = ST3[0:M, 0:512].rearrange("p (a h r) -> p a h r", a=2, h=2)[:, :, :, 0:64]
            tag = "s" if is_sample else "p"
            if not is_sample:
                P.dma("sp", lambda e: e.dma_start(out=o_pc[l], in_=ST[M - 2:M, 0:256]), "o_st" + tag, reads=[B_st, B_out])
                P.dma("sp", lambda e: e.dma_start(out=o_pp[l], in_=ST[M - 15:M, 256:512]), "o_st" + tag, reads=[B_st, B_out])
                P.dma("sp", lambda e: e.dma_start(out=o_pk[l].rearrange("t (h d) -> t h d", h=2), in_=kv[:, 0, :, :]), "o_st3" + tag,
                      reads=[B_st3, B_out])
                P.dma("sp", lambda e: e.dma_start(out=o_pv[l].rearrange("t (h d) -> t h d", h=2), in_=kv[:, 1, :, :]), "o_st3" + tag,
                      reads=[B_st3, B_out])
            else:
                P.dma("sp", lambda e: e.dma_start(out=o_sc[l, :, 1, :], in_=ST[0:M, 0:256]), "o_st" + tag, reads=[B_st, B_out])
                P.dma("sp", lambda e: e.dma_start(out=o_spl[l, :, 14, :], in_=ST[0:M, 256:512]), "o_st" + tag, reads=[B_st, B_out])
                P.dma("sp", lambda e: e.dma_start(out=o_sk[l, :, 127, :].rearrange("t (h d) -> t h d", h=2), in_=kv[:, 0, :, :]), "o_st3" + tag,
                      reads=[B_st3, B_out])
                P.dma("sp", lambda e: e.dma_start(out=o_sv[l, :, 127, :].rearrange("t (h d) -> t h d", h=2), in_=kv[:, 1, :, :]), "o_st3" + tag,
                      reads=[B_st3, B_out])
                P.op("act", lambda e: act_id(e, out=VtokS[0:M, :], in_=ST3[0:M, 256:512]), reads=[B_st3], writes=[B_samp])

        def mixer_B(l, cs):
            prev_phase[0] = "B"
            set_start(cs)
            cv = Carver()
            WOUT = cv.take([KC, D], BF16)
            lnb_ = carve_ln(Carver(LN_OFF))
            KS = [cv.take([4, 512], BF16) for _ in range(2)]
            VS = [cv.take([4, 256], BF16) for _ in range(2)]
            PTs = cv.take([128], BF16)
            T1 = cv.take([NS, 4], F32)
            SQs = [cv.take([NS], BF16) for _ in range(2)]
            RSs = cv.take([3, NS], F32)
            B_wout = [abuf("wout0"), abuf("wout1")]
            B_ks = [abuf("ks0"), abuf("ks1")]
            B_vs = [abuf("vs0"), abuf("vs1")]
            B_pts, B_t1, B_rss = abuf("pts"), abuf("t1"), abuf("rss")
            B_sqs = [abuf("sqs0"), abuf("sqs1")]
            assert cv.off <= LN_OFF, cv.off
            phase_switch(B_wout + B_ks + B_vs + [B_pts, B_t1, B_rss] + B_sqs, newT=lnb_)
            for i in range(2):
                P.dma("pool", lambda e, i=i: e.dma_start(out=WOUT[:, :, i * 512:(i + 1) * 512], in_=d_wout[l, i]), f"wout{i}", writes=[B_wout[i]])
            ln_consts(l, 1, (l % 2, 6, 7))
            gates(l, 1)
            def samp_dma(gq):
                slot = gq % 2
                ks, vs = KS[slot], VS[slot]
                P.dma("pool", lambda e, ks=ks, gq=gq: e.dma_start(out=ks[:, :, :], in_=d_kTz[l, 4 * gq:4 * gq + 4].rearrange("b p n -> p b n")),
                      f"ks{slot}", writes=[B_ks[slot]])
                P.dma("pool", lambda e, vs=vs, gq=gq: e.dma_start(out=vs[:, :, :], in_=d_vd[l, 4 * gq:4 * gq + 4].rearrange("b p n -> p b n")),
                      f"vs{slot}", writes=[B_vs[slot]])
            samp_dma(0)
            samp_dma(1)

            def samp_group(gq):
                slot = gq % 2
                ks, vs = KS[slot], VS[slot]
                for bl in range(4):
                    b = 4 * gq + bl
                    P.op("act", lambda e, ks=ks, bl=bl, b=b: act_id(e, out=ks[:, bl, :].rearrange("p (x k) -> p x k", x=4)[:, :, 0], in_=KZS[:, :, b]),
                         reads=[B_samp, B_ks[slot]], writes=[B_ks[slot]])
                    P.op("pe", lambda e, b=b: e.matmul(bank(3, 0, 256, 0, 1), ident[0:16, b:b + 1], VtokS[0:16, :], start=True, stop=True),
                         reads=[B_samp, B_const], writes=[B_ps[3]])
                    P.op("act", lambda e, vs=vs, bl=bl: act_id(e, out=vs[0:1, bl, :], in_=bank(3, 0, 256, 0, 1)), reads=[B_ps[3], B_vs[slot]],
                         writes=[B_vs[slot]])

                    def sc(e, ks=ks, bl=bl, b=b):
                        ins = None
                        for h in range(2):
                            for half in range(2):
                                for cl in range(2):
                                    col = b * 8 + h * 4 + half * 2 + cl
                                    x = h * 2 + half
                                    ins = e.matmul(bank(0, col, col + 1), ks[:, bl, x * 128:(x + 1) * 128], QS[:, 2 * h + cl, b:b + 1],
                                                   start=True, stop=True)
                        return ins
                    P.op("pe", sc, reads=[B_ks[slot], B_samp], writes=[B_ps[0], B_hb[0][0], B_hb[0][1]])
                g0, g1 = gq * 32, gq * 32 + 32
                P.op("act", lambda e, g0=g0, g1=g1: e.activation(out=PTs[:, g0:g1], in_=bank(0, g0, g1), func=AF.Exp, scale=0.125),
                     reads=[B_ps[0], B_hb[0][0], B_hb[0][1]], writes=[B_pts])

                def pv(e, vs=vs, gq=gq, g0=g0, g1=g1):
                    ins = e.matmul(bank(1, g0, g1), ones[:], PTs[:, g0:g1], start=True, stop=True)
                    for bl in range(4):
                        b = 4 * gq + bl
                        for h in range(2):
                            col0 = b * 8 + h * 4
                            ins = e.matmul(bank(2, col0, col0 + 4), vs[:, bl, h * 128:(h + 1) * 128], PTs[:, col0:col0 + 4], start=True, stop=True)
                    return ins
                P.op("pe", pv, reads=[B_pts, B_vs[slot], B_const], writes=[B_ps[1], B_hb[1][0], B_hb[1][1], B_ps[2]])
                if gq + 2 < NS // 4:
                    samp_dma(gq + 2)
            def samp_finish():
                for half in range(2):
                    p0, p1 = half * 64, half * 64 + 64
                    den = bank(1, 0, 128, p0, p1).rearrange("p (b h f c) -> p b h f c", b=NS, h=2, f=2)[:, :, :, half, :]
                    oo = bank(2, 0, 128, p0, p1).rearrange("p (b h f c) -> p b h f c", b=NS, h=2, f=2)[:, :, :, half, :]
                    sk = sinke[p0:p1, l, :].rearrange("p (h c) -> p h c", h=2).unsqueeze(1).to_broadcast([64, NS, 2, 2])
                    t1 = T1[p0:p1, :, :].rearrange("p b (h c) -> p b h c", h=2)
                    P.op("dve", lambda e, den=den, sk=sk, t1=t1: e.tensor_tensor(out=t1, in0=den, in1=sk, op=ALU.add),
                         reads=[B_ps[1], B_hb[1][0], B_hb[1][1], B_const], writes=[B_t1])
                    P.op("dve", lambda e, t1=t1: e.reciprocal(out=t1, in_=t1), reads=[B_t1], writes=[B_t1])
                    ys = YS[p0:p1, 4:8, :].rearrange("p (h c) b -> p b h c", h=2)
                    P.op("dve", lambda e, oo=oo, t1=t1, ys=ys: e.tensor_tensor(out=ys, in0=oo, in1=t1, op=ALU.mult),
                         reads=[B_ps[2], B_t1], writes=[B_samp])
                nsq = [0]
                for grp, (chunks, n) in enumerate([([0, 1], 256.0), ([2, 3], 256.0), ([4, 5, 6, 7], 512.0)]):
                    for i, c in enumerate(chunks):
                        r = nsq[0] % 2
                        nsq[0] += 1
                        sq = SQs[r]
                        P.op("act", lambda e, sq=sq, c=c: e.activation(out=sq[:, :], in_=YS[:, c, :], func=AF.Square), reads=[B_samp], writes=[B_sqs[r]])
                        P.op("pe", lambda e, sq=sq, i=i, grp=grp, chunks=chunks: e.matmul(bank(1, 256 + grp * 16, 256 + grp * 16 + NS), ones[:], sq[:, :],
                                                                                          start=(i == 0), stop=(i == len(chunks) - 1)),
                             reads=[B_sqs[r], B_const], writes=[B_ps[1], B_hb[1][0], B_hb[1][1]])
                    P.op("act", lambda e, grp=grp, n=n: e.activation(out=RSs[:, grp, :], in_=bank(1, 256 + grp * 16, 256 + grp * 16 + NS), func=AF.Ln,
                                                                     scale=1.0 / n, bias=epsT[:, 1:2]),
                         reads=[B_ps[1], B_hb[1][0], B_hb[1][1], B_const], writes=[B_rss])
                    P.op("act", lambda e, grp=grp: e.activation(out=RSs[:, grp, :], in_=RSs[:, grp, :], func=AF.Exp, scale=-0.5), reads=[B_rss], writes=[B_rss])
                for c in range(KC):
                    grp = 0 if c < 2 else (1 if c < 4 else 2)
                    P.op("dve", lambda e, c=c, grp=grp: e.scalar_tensor_tensor(out=HIN[:, c, NP:NP + NS], in0=YS[:, c, :], scalar=mixg[:, l, c:c + 1],
                                                                                in1=RSs[:, grp, :], op0=ALU.mult, op1=ALU.mult),
                         reads=[B_samp, B_rss, B_const], writes=hin_bufs(NP, NS, [c]))

            prevb = [None]

            def blk_done(bi):
                ln_math(bi)
                if prevb[0] is not None:
                    ln_apply(prevb[0], True)
                prevb[0] = bi
            n_pd = 0
            sg_next = [0]

            def samp_some(n):
                for _ in range(n):
                    if sg_next[0] < NS // 4:
                        samp_group(sg_next[0])
                        sg_next[0] += 1
                        if sg_next[0] == NS // 4:
                            samp_finish()
            for bi, (c0, w) in enumerate(FBLK):
                if w == 0:
                    continue
                if c0 >= NP:
                    samp_some(NS // 4)
                for o in range(KC):
                    if o == 4 and c0 < NP:
                        samp_some(1)
                    pd = 4 + (n_pd % 2)
                    n_pd += 1

                    def mm(e, c0=c0, w=w, pd=pd, o=o):
                        ins = None
                        for kc in range(KC):
                            ins = e.matmul(bank(pd, 0, w), WOUT[:, kc, o * 128:(o + 1) * 128], HIN[:, kc, c0:c0 + w], start=(kc == 0),
                                           stop=(kc == KC - 1))
                        return ins
                    P.op("pe", mm, reads=[B_wout[o // 4]] + hin_bufs(c0, w), writes=[B_ps[pd]])
                    resid_evac(bi, o, bank(pd, 0, w), [B_ps[pd]], 1)
                    ln_stats_feed(bi, o, blk_done)
            ln_flush()
            ln_apply(NB - 1, True)

        for l in range(L):
            for (dst, src) in [(o_sc[l, :, 0, :], d_sconv[l, :, 1, :]), (o_spl[l, :, 0:14, :], d_spool[l, :, 1:15, :]),
                               (o_sk[l, :, 0:127, :], d_ck[l, :, 1:128, :]), (o_sv[l, :, 0:127, :], d_cv[l, :, 1:128, :])]:
                P.dma("sp", lambda e, dst=dst, src=src: e.dma_start(out=dst, in_=src), "dd", reads=[B_out])

        cv0 = Carver()
        NR0 = 8
        ada["ring"] = [cv0.take([KC, 128], BF16) for _ in range(NR0)]
        ada["bufs"] = [abuf(f"adar{i}") for i in range(NR0)]
        ada["nslot"] = NR0
        phase_switch(ada["bufs"])
        ada_load_bias(0)
        for j in range(24):
            ada_item(0, j)
        ada["nslot"] = 2
        ada["n"] = 0
        M = Mt[0]
        P.op("dve", lambda e: e.tensor_scalar(out=lnc[:, 2, :], in0=M[:, KC:2 * KC, 0], scalar1=1.0, scalar2=None, op0=ALU.add),
             reads=[B_M[0]], writes=[B_lnc])
        P.op("dve", lambda e: e.tensor_scalar(out=lncs[:, 0, :, :], in0=M[:, KC:2 * KC, 1:17], scalar1=1.0, scalar2=None, op0=ALU.add),
             reads=[B_M[0]], writes=[B_lnc])
        for bi, (c0, w) in enumerate(FBLK):
            for c in range(KC):
                xs = X[:, c, c0:c0 + w]
                hs = HIN[:, c, c0:c0 + w]
                hb = hin_bufs(c0, w, [c])
                if c0 < NP:
                    P.op("dve", lambda e, xs=xs, hs=hs, c=c: e.tensor_scalar(out=hs, in0=xs, scalar1=lnc[:, 2, c:c + 1], scalar2=M[:, c, 0:1],
                                                                              op0=ALU.mult, op1=ALU.add),
                         reads=[B_X[c][bi], B_lnc, B_M[0]], writes=hb)
                else:
                    P.op("dve", lambda e, xs=xs, c=c: e.tensor_tensor(out=YS[:, c, :], in0=xs, in1=lncs[:, 0, c, :], op=ALU.mult),
                         reads=[B_X[c][bi], B_lnc], writes=[B_samp])
                    P.op("dve", lambda e, hs=hs, c=c: e.tensor_tensor(out=hs, in0=YS[:, c, :], in1=M[:, c, 1:17], op=ALU.add),
                         reads=[B_samp, B_M[0]], writes=hb)
                P.op("act", lambda e, xs=xs: act_id(e, xs, xs, ALPHA), reads=[B_X[c][bi]] + hb, writes=[B_X[c][bi]])

        for l in range(L):
            last = (l == L - 1)
            cs0 = min(128 * l, HALO)
            cs1 = min(128 * (l + 1), HALO)
            ffn_phase(l, 0, 0 if l == 0 else None, 0, (l % 2, 3, 4), cs0, ada_list=list(range(24, 72)) if l == 0 else None)
            mixer_A(l, cs0)
            mixer_B(l, cs1)
            ffn_phase(l, 1, None if last else l + 1, 2, None if last else ((l + 1) % 2, 0, 1), cs1, final=last)

        P.wait_all("sp", [B_out])
        P.emit()
    return nc


def _shared_weights(inp, L):
    f = lambda a: np.ascontiguousarray(a, dtype=np.float32)
    w = {}
    w["wada"] = f(inp["w_ada"].reshape(L, KC, 128, 72, 128).transpose(0, 3, 2, 1, 4))
    w["bada"] = f(inp["b_ada"].reshape(L, 72, 128).transpose(0, 2, 1))
    for i, (g, u, d) in enumerate([("ffn1_gate", "ffn1_up", "ffn1_down"), ("ffn2_gate", "ffn2_up", "ffn2_down")]):
        gg = inp[g].reshape(L, KC, 128, MC, 128).transpose(0, 3, 2, 1, 4)
        uu = inp[u].reshape(L, KC, 128, MC, 128).transpose(0, 3, 2, 1, 4)
        w[f"wgu{i + 1}"] = f(np.stack([gg, uu], axis=3).reshape(L, MC, 128, 2 * KC * 128))
        w[f"wd{i + 1}"] = f(inp[d].reshape(L, MC, 128, D))
    W = inp["w_in"]
    K0, K1, V0, V1 = W[:, :, 1536:1600], W[:, :, 1600:1664], W[:, :, 1664:1728], W[:, :, 1728:1792]
    ext = np.concatenate([W[:, :, 0:1536], K0, K0, K1, K1, V0, V0, V1, V1], axis=2)
    w["win"] = f(ext.reshape(L, KC, 128, 4, 512).transpose(0, 3, 2, 1, 4))
    w["wout"] = f(inp["w_out"].reshape(L, KC, 128, 2, 512).transpose(0, 3, 2, 1, 4))
    w["lng"] = f(inp["ln_g"].reshape(L, 3, KC, 128).transpose(3, 0, 1, 2))
    w["lnb"] = f(inp["ln_b"].reshape(L, 3, KC, 128).transpose(3, 0, 1, 2))
    w["convw"] = f(inp["conv_w"].reshape(L, 3, 2, 128).transpose(3, 0, 1, 2))
    pw = np.zeros((L, 2, 128, 128), np.float32)
    for cc in range(2):
        pw[:, cc, 0:64, 0:64] = inp["pool_w"][:, 2 * cc]
        pw[:, cc, 64:128, 64:128] = inp["pool_w"][:, 2 * cc + 1]
    w["poolw"] = pw
    w["pscale"] = f(inp["pool_scale"].reshape(L, 2, 128).transpose(2, 0, 1))
    w["mixg"] = f(inp["mix_norm_g"].reshape(L, KC, 128).transpose(2, 0, 1))
    sk = np.zeros((128, L, 4), np.float32)
    for c in range(4):
        sk[0:64, :, c] = inp["attn_sinks"][:, 2 * c][None, :]
        sk[64:128, :, c] = inp["attn_sinks"][:, 2 * c + 1][None, :]
    w["sinks"] = sk
    iw = np.zeros((128, 2), np.float32)
    for cc in range(2):
        iw[0:64, cc] = 1.0 / (2 ** (2 * cc + 1))
        iw[64:128, cc] = 1.0 / (2 ** (2 * cc + 2))
    w["invw"] = iw
    w["ident"] = np.eye(16, dtype=np.float32)
    w["identb"] = np.eye(128, dtype=np.float32)
    w["sinks4"] = f(np.broadcast_to(inp["attn_sinks"][None, :, :], (128, L, 8)))
    return w


def _core_inputs(cfg, inp, core, shared):
    L, OWN, HALO, NP, NT = cfg.L, cfg.OWN, cfg.HALO, cfg.NP, cfg.NT
    f = lambda a: np.ascontiguousarray(a, dtype=np.float32)
    nseg = inp["x_prompt"].shape[1] // OWN
    b, seg = core // nseg, core % nseg
    s0 = seg * OWN
    idx = np.arange(s0 - HALO, s0 + OWN)
    ok = idx >= 0
    xall = np.zeros((NT, D), np.float32)
    xall[:NP][ok] = inp["x_prompt"][b, idx[ok]]
    sl = slice(NS * core, NS * core + NS)
    xall[NP:] = inp["x_sample"][sl, 0, :]
    m = dict(shared)
    m["xT"] = f(xall.T.reshape(KC, 128, NT).transpose(1, 0, 2))
    call = np.concatenate([inp["c_prompt"][b:b + 1], inp["c_sample"][sl]], axis=0)
    m["cT"] = f(call.T.reshape(KC, 128, 17).transpose(1, 0, 2))
    ic = np.zeros((128, 2, 16), np.float32)
    for cc in range(2):
        for half in range(2):
            wd_ = 2 ** (2 * cc + half + 1)
            pos = np.arange(16)
            cnt = np.minimum(pos + 1, wd_) if seg == 0 else np.full(16, wd_)
            ic[half * 64:half * 64 + 64, cc, :] = (1.0 / cnt)[None, :]
    m["invcnt"] = ic
    m["valid"] = np.full((128, 1), 0.0 if seg == 0 else 1.0, np.float32)
    i = np.arange(128)[:, None]
    t = np.arange(128)[None, :]
    mk = np.zeros((128, 3, 128), np.float32)
    NEG = -30000.0
    mk[:, 0, :] = np.where(i > t, 0.0, NEG)
    mk[:, 1, :] = np.where(i <= t, 0.0, NEG)
    mk[:, 2, :] = np.where(i > t, 0.0, NEG) if seg != 0 else NEG
    m["masks"] = mk
    cr = np.zeros((128, 2, 16), np.float32)
    for cc in range(2):
        for half in range(2):
            wd_ = 2 ** (2 * cc + half + 1)
            pos = np.arange(16)
            cnt = np.minimum(pos + 1, wd_) if seg == 0 else np.full(16, wd_)
            cr[half * 64:half * 64 + 64, cc, :] = (wd_ / cnt)[None, :]
    m["cntr"] = cr
    ck = inp["cache_k_win"][:, sl]
    cvv = inp["cache_v_win"][:, sl]
    kt = ck.transpose(0, 1, 3, 4, 2)
    kz = np.zeros((L, NS, 128, 2, 2, 128), np.float32)
    for h in range(2):
        kz[:, :, 0:64, h, 0, :] = kt[:, :, h]
        kz[:, :, 64:128, h, 1, :] = kt[:, :, h]
    m["kTz"] = kz.reshape(L, NS, 128, 512)
    m["vd"] = f(np.concatenate([cvv, cvv], axis=-1).reshape(L, NS, 128, 256))
    m["ck"] = f(ck.reshape(L, NS, 128, 128))
    m["cv"] = f(cvv.reshape(L, NS, 128, 128))
    sc = inp["state_conv"][:, sl]
    m["sconvT"] = f(sc.reshape(L, NS, 2, 2, 128).transpose(0, 4, 3, 2, 1))
    sp = inp["state_pool"][:, sl]
    spt = np.zeros((L, 128, 2, NS, 16), np.float32)
    spt[..., 0:15] = sp.reshape(L, NS, 15, 2, 128).transpose(0, 4, 3, 1, 2)
    m["spoolT"] = spt
    m["sconv"] = f(sc)
    m["spool"] = f(sp)
    return m


_PROG_CACHE = {}


def kernel(**inputs):
    inp = {k: np.asarray(v) for k, v in inputs.items()}
    L = inp["w_ada"].shape[0]
    B, SEQ = inp["x_prompt"].shape[:2]
    ncores = 8
    nseg = ncores // B
    OWN = SEQ // nseg
    HALO = max(TB, ((128 * L + TB - 1) // TB) * TB)
    cfg = Cfg(L, OWN, HALO)
    key = (L, OWN, HALO)
    if key not in _PROG_CACHE:
        _PROG_CACHE[key] = build_program(cfg)
    nc = _PROG_CACHE[key]
    shared = _shared_weights(inp, L)
    in_maps = [_core_inputs(cfg, inp, c, shared) for c in range(ncores)]
    res = run_bass_kernel_spmd(nc, in_maps, core_ids=list(range(ncores)))
    R = res.results
    NSB = inp["x_sample"].shape[0]
    y_prompt = np.zeros((B, SEQ, D), np.float32)
    y_sample = np.zeros((NSB, 1, D), np.float32)
    p_conv = np.zeros((L, B, 2, 256), np.float32)
    p_pool = np.zeros((L, B, 15, 256), np.float32)
    p_k = np.zeros((L, B, 128, 2, 64), np.float32)
    p_v = np.zeros((L, B, 128, 2, 64), np.float32)
    s_conv = np.zeros((L, NSB, 2, 256), np.float32)
    s_pool = np.zeros((L, NSB, 15, 256), np.float32)
    s_k = np.zeros((L, NSB, 128, 2, 64), np.float32)
    s_v = np.zeros((L, NSB, 128, 2, 64), np.float32)
    for c in range(ncores):
        b, seg = c // nseg, c % nseg
        r = R[c]
        yt = np.asarray(r["o_x"]).transpose(2, 1, 0).reshape(OWN + NS, D)
        y_prompt[b, seg * OWN:(seg + 1) * OWN] = yt[:OWN]
        sl = slice(NS * c, NS * c + NS)
        y_sample[sl, 0] = yt[OWN:]
        if seg == nseg - 1:
            p_conv[:, b] = r["o_pc"]
            p_pool[:, b] = r["o_pp"]
            p_k[:, b] = np.asarray(r["o_pk"]).reshape(L, 128, 2, 64)
            p_v[:, b] = np.asarray(r["o_pv"]).reshape(L, 128, 2, 64)
        s_conv[:, sl] = r["o_sc"]
        s_pool[:, sl] = r["o_spl"]
        s_k[:, sl] = np.asarray(r["o_sk"]).reshape(L, NS, 128, 2, 64)
        s_v[:, sl] = np.asarray(r["o_sv"]).reshape(L, NS, 128, 2, 64)
    return (y_prompt, y_sample, p_conv, p_pool, p_k, p_v, s_conv, s_pool, s_k, s_v)
```
